# Optimizing a Trainium2 kernel written in Bass

```python
import math
import jax, jax.numpy as jnp
from jax import lax
import numpy as np

D_MODEL = 1024
BATCH = 8
SEQ = 2048
DEPTH = 4
DEC_BATCH = 128
DEC_SEQ = 8
PAST_LEN = 16384
PAGE_SIZE = 128

MIX_WIDTH = D_MODEL
POOL_WIDTH = MIX_WIDTH // 2
SSM_WIDTH = MIX_WIDTH - POOL_WIDTH
POOL_WINDOWS = (2, 4, 8, 16)
N_POOL_GROUPS = len(POOL_WINDOWS)
POOL_GROUP_DIM = POOL_WIDTH // N_POOL_GROUPS
POOL_BUF = max(POOL_WINDOWS) - 1
SSM_GROUP_DIM = 16
N_SSM_GROUPS = SSM_WIDTH // SSM_GROUP_DIM
SSM_STATE = 64
D_FF = -(-8 * D_MODEL // (3 * 256)) * 256
PLE_DIM = 256
EPS = 1e-6
DT_MIN = 0.001
DT_MAX = 0.1

kernel_name = 'hymba_pool_s5_decoder_step'


def rmsnorm(x, g):
    xf = x.astype(jnp.float32)
    y = xf * lax.rsqrt(jnp.mean(xf * xf, axis=-1, keepdims=True) + EPS)
    return (y * g.astype(jnp.float32)).astype(x.dtype)


def pool_mix(u_ext, pos0, w_pool, pool_scale):
    bsz, n_ext, _ = u_ext.shape
    t_len = n_ext - POOL_BUF
    uf = u_ext.astype(jnp.float32).reshape(bsz, n_ext, N_POOL_GROUPS, POOL_GROUP_DIM)
    cs = jnp.concatenate([jnp.zeros_like(uf[:, :1]), jnp.cumsum(uf, axis=1)], axis=1)
    end = cs[:, POOL_BUF + 1:]
    pos = pos0 + jnp.arange(t_len, dtype=jnp.int32)
    means = []
    for g, w in enumerate(POOL_WINDOWS):
        start = cs[:, POOL_BUF + 1 - w: POOL_BUF + 1 - w + t_len, g]
        cnt = jnp.minimum(pos + 1, w).astype(jnp.float32)[None, :, None]
        means.append((end[:, :, g] - start) / cnt)
    diff = (jnp.stack(means, axis=2) - uf[:, POOL_BUF:]).astype(u_ext.dtype)
    y = jnp.einsum('btgc,gcd->btgd', diff, w_pool).reshape(bsz, t_len, POOL_WIDTH)
    return y * pool_scale


def _ssm_combine(e1, e2):
    a1r, a1i, b1r, b1i = e1
    a2r, a2i, b2r, b2i = e2
    return (a2r * a1r - a2i * a1i,
            a2r * a1i + a2i * a1r,
            a2r * b1r - a2i * b1i + b2r,
            a2r * b1i + a2i * b1r + b2i)


def ssm_mix(u, h0_re, h0_im, a_re, a_im, log_dt, b_re, b_im, c_re, c_im, d_skip, w_glu, b_glu):
    f32 = jnp.float32
    bsz, t_len, _ = u.shape
    ug = u.astype(f32).reshape(bsz, t_len, N_SSM_GROUPS, SSM_GROUP_DIM)
    ar, ai = a_re.astype(f32), a_im.astype(f32)
    dt = jnp.exp(log_dt.astype(f32))[:, None]
    mag = jnp.exp(dt * ar)
    lam_re, lam_im = mag * jnp.cos(dt * ai), mag * jnp.sin(dt * ai)
    den = ar * ar + ai * ai
    nr = lam_re - 1.0
    k_re = (nr * ar + lam_im * ai) / den
    k_im = (lam_im * ar - nr * ai) / den
    br, bi = b_re.astype(f32), b_im.astype(f32)
    bb_re = k_re[..., None] * br - k_im[..., None] * bi
    bb_im = k_re[..., None] * bi + k_im[..., None] * br
    bu_re = jnp.einsum('gpc,btgc->btgp', bb_re, ug)
    bu_im = jnp.einsum('gpc,btgc->btgp', bb_im, ug)
    hr0, hi0 = h0_re.astype(f32), h0_im.astype(f32)
    bu_re = bu_re.at[:, 0].add(lam_re * hr0 - lam_im * hi0)
    bu_im = bu_im.at[:, 0].add(lam_re * hi0 + lam_im * hr0)
    la_re = jnp.broadcast_to(lam_re, bu_re.shape)
    la_im = jnp.broadcast_to(lam_im, bu_im.shape)
    _, _, h_re, h_im = lax.associative_scan(_ssm_combine, (la_re, la_im, bu_re, bu_im), axis=1)
    y = (jnp.einsum('gcp,btgp->btgc', c_re.astype(f32), h_re)
         - jnp.einsum('gcp,btgp->btgc', c_im.astype(f32), h_im)
         + d_skip.astype(f32).reshape(N_SSM_GROUPS, SSM_GROUP_DIM) * ug)
    g = jax.nn.gelu(y.reshape(bsz, t_len, SSM_WIDTH).astype(u.dtype), approximate=False)
    out = g * jax.nn.sigmoid(g @ w_glu + b_glu)
    return out, h_re[:, -1].astype(h0_re.dtype), h_im[:, -1].astype(h0_im.dtype)


def layer(x, p, pool_buf, h_re, h_im, pos0, lw):
    (g_mix, w_in, w_pool, pool_scale, a_re, a_im, log_dt, b_re, b_im, c_re, c_im,
     d_skip, w_glu, b_glu, w_out, g_ffn, w_gate_up, w_down, g_ple, w_ple, w_ple_gate) = lw
    z = rmsnorm(x, g_mix) @ w_in
    u_pool, u_ssm = z[..., :POOL_WIDTH], z[..., POOL_WIDTH:]
    u_ext = jnp.concatenate([pool_buf.astype(z.dtype), u_pool], axis=1)
    y_pool = pool_mix(u_ext, pos0, w_pool, pool_scale)
    y_ssm, h_re_new, h_im_new = ssm_mix(u_ssm, h_re, h_im, a_re, a_im, log_dt, b_re, b_im,
                                        c_re, c_im, d_skip, w_glu, b_glu)
    x = x + jnp.concatenate([y_pool, y_ssm], axis=-1) @ w_out
    gu = rmsnorm(x, g_ffn) @ w_gate_up
    x = x + (jax.nn.silu(gu[..., :D_FF]) * gu[..., D_FF:]) @ w_down
    x = x + (p @ w_ple) * jax.nn.sigmoid(rmsnorm(x, g_ple) @ w_ple_gate)
    return x, u_ext[:, -POOL_BUF:], h_re_new, h_im_new


def trunk(x, p, pool_bufs, h_res, h_ims, pos0, weights):
    new_pool, new_re, new_im = [], [], []
    for i in range(DEPTH):
        lw = tuple(w[i] for w in weights)
        x, pb, hr, hi = layer(x, p[i], pool_bufs[i], h_res[i], h_ims[i], pos0, lw)
        new_pool.append(pb)
        new_re.append(hr)
        new_im.append(hi)
    return x, jnp.stack(new_pool), jnp.stack(new_re), jnp.stack(new_im)


def setup_inputs(seed: int = 0) -> dict:
    key = jax.random.key(seed)
    ks = jax.random.split(key, 32)
    nrm = jax.random.normal
    f32 = jnp.float32
    n_idx = jnp.arange(SSM_STATE, dtype=f32)
    a_re = -0.5 + 0.01 * nrm(ks[10], (DEPTH, N_SSM_GROUPS, SSM_STATE), f32)
    a_im = math.pi * n_idx[None, None, :] + 0.01 * nrm(ks[11], (DEPTH, N_SSM_GROUPS, SSM_STATE), f32)
    log_dt = jax.random.uniform(ks[12], (DEPTH, N_SSM_GROUPS), f32, math.log(DT_MIN), math.log(DT_MAX))
    return {
        'x_prompt': nrm(ks[0], (BATCH, SEQ, D_MODEL), f32),
        'x_sample': nrm(ks[1], (DEC_BATCH, DEC_SEQ, D_MODEL), f32),
        'state_pool': nrm(ks[2], (DEPTH, DEC_BATCH, POOL_BUF, POOL_WIDTH), f32),
        'state_ssm_re': 0.5 * nrm(ks[3], (DEPTH, DEC_BATCH, N_SSM_GROUPS, SSM_STATE), f32),
        'state_ssm_im': 0.5 * nrm(ks[4], (DEPTH, DEC_BATCH, N_SSM_GROUPS, SSM_STATE), f32),
        'p_prompt': nrm(ks[5], (DEPTH, BATCH, SEQ, PLE_DIM), f32),
        'p_sample': nrm(ks[6], (DEPTH, DEC_BATCH, DEC_SEQ, PLE_DIM), f32),
        'g_mix': 1.0 + 0.02 * nrm(ks[7], (DEPTH, D_MODEL), f32),
        'w_in': nrm(ks[8], (DEPTH, D_MODEL, MIX_WIDTH), f32) * D_MODEL ** -0.5,
        'w_pool': nrm(ks[9], (DEPTH, N_POOL_GROUPS, POOL_GROUP_DIM, POOL_GROUP_DIM), f32) * POOL_GROUP_DIM ** -0.5,
        'pool_scale': 0.5 + 0.05 * nrm(ks[13], (DEPTH, POOL_WIDTH), f32),
        'ssm_a_re': a_re,
        'ssm_a_im': a_im,
        'ssm_log_dt': log_dt,
        'ssm_b_re': nrm(ks[14], (DEPTH, N_SSM_GROUPS, SSM_STATE, SSM_GROUP_DIM), f32) * (2 * SSM_GROUP_DIM) ** -0.5,
        'ssm_b_im': nrm(ks[15], (DEPTH, N_SSM_GROUPS, SSM_STATE, SSM_GROUP_DIM), f32) * (2 * SSM_GROUP_DIM) ** -0.5,
        'ssm_c_re': nrm(ks[16], (DEPTH, N_SSM_GROUPS, SSM_GROUP_DIM, SSM_STATE), f32) * (2 * SSM_STATE) ** -0.5,
        'ssm_c_im': nrm(ks[17], (DEPTH, N_SSM_GROUPS, SSM_GROUP_DIM, SSM_STATE), f32) * (2 * SSM_STATE) ** -0.5,
        'ssm_d': 0.5 * nrm(ks[18], (DEPTH, SSM_WIDTH), f32),
        'w_glu': nrm(ks[19], (DEPTH, SSM_WIDTH, SSM_WIDTH), f32) * SSM_WIDTH ** -0.5,
        'b_glu': 0.01 * nrm(ks[20], (DEPTH, SSM_WIDTH), f32),
        'w_out': nrm(ks[21], (DEPTH, MIX_WIDTH, D_MODEL), f32) * MIX_WIDTH ** -0.5,
        'g_ffn': 1.0 + 0.02 * nrm(ks[22], (DEPTH, D_MODEL), f32),
        'w_gate_up': nrm(ks[23], (DEPTH, D_MODEL, 2 * D_FF), f32) * D_MODEL ** -0.5,
        'w_down': nrm(ks[24], (DEPTH, D_FF, D_MODEL), f32) * D_FF ** -0.5,
        'g_ple': 1.0 + 0.02 * nrm(ks[25], (DEPTH, D_MODEL), f32),
        'w_ple': nrm(ks[26], (DEPTH, PLE_DIM, D_MODEL), f32) * PLE_DIM ** -0.5,
        'w_ple_gate': nrm(ks[27], (DEPTH, D_MODEL, D_MODEL), f32) * D_MODEL ** -0.5,
        'g_final': 1.0 + 0.02 * nrm(ks[28], (D_MODEL,), f32),
    }


def reference(x_prompt, x_sample, state_pool, state_ssm_re, state_ssm_im, p_prompt, p_sample,
              g_mix, w_in, w_pool, pool_scale, ssm_a_re, ssm_a_im, ssm_log_dt, ssm_b_re, ssm_b_im,
              ssm_c_re, ssm_c_im, ssm_d, w_glu, b_glu, w_out, g_ffn, w_gate_up, w_down,
              g_ple, w_ple, w_ple_gate, g_final):
    weights = (g_mix, w_in, w_pool, pool_scale, ssm_a_re, ssm_a_im, ssm_log_dt, ssm_b_re, ssm_b_im,
               ssm_c_re, ssm_c_im, ssm_d, w_glu, b_glu, w_out, g_ffn, w_gate_up, w_down,
               g_ple, w_ple, w_ple_gate)
    bsz = x_prompt.shape[0]
    pool0 = jnp.zeros((DEPTH, bsz, POOL_BUF, POOL_WIDTH), x_prompt.dtype)
    h0 = jnp.zeros((DEPTH, bsz, N_SSM_GROUPS, SSM_STATE), state_ssm_re.dtype)
    xp, pool_p, re_p, im_p = trunk(x_prompt, p_prompt, pool0, h0, h0, 0, weights)
    xs, pool_s, re_s, im_s = trunk(x_sample, p_sample, state_pool, state_ssm_re, state_ssm_im,
                                   PAST_LEN, weights)
    y_prompt = rmsnorm(xp, g_final)
    y_sample = rmsnorm(xs, g_final)
    return (y_prompt, y_sample, pool_p, re_p, im_p, pool_s, re_s, im_s)
```

```python
import numpy as np
import concourse.bass as bass
import concourse.mybir as mybir
from contextlib import ExitStack

F32 = mybir.dt.float32
BF16 = mybir.dt.bfloat16
AF = mybir.ActivationFunctionType
ALU = mybir.AluOpType

ENGS = ("pe", "act", "dve", "pool", "sp")
EPOCH = 8000
NDMASEM = 20


class Instr:
    __slots__ = ("eng", "fn", "deps", "flag", "sem", "val", "is_dma", "dsem", "dval", "idx_")

    def __init__(self, eng, fn, deps):
        self.eng = eng; self.fn = fn; self.deps = deps
        self.flag = False; self.sem = None; self.val = 0
        self.is_dma = False; self.dsem = None; self.dval = 0


class DmaEv:
    __slots__ = ("sem", "val")

    def __init__(self, sem, val):
        self.sem = sem; self.val = val


class T:
    def __init__(self, prog, ap, name="", off=None, nbytes=None):
        self.prog = prog; self.ap = ap; self.name = name
        self.last_w = None
        self.readers = {}
        self.dreaders = []
        self.inherit = []
        self.off = off; self.nbytes = nbytes

    def events(self):
        ev = list(self.readers.values()) + list(self.dreaders) + list(self.inherit)
        if self.last_w is not None:
            ev.append(self.last_w)
        return ev


class Prog:
    def __init__(self, nc, arena_words):
        self.nc = nc
        self.es = ExitStack()
        self.streams = {e: [] for e in ENGS}
        self.arena = self.es.enter_context(nc.sbuf_tensor("arena", [128, arena_words], F32))
        self.arena_bytes = arena_words * 4
        self.free = [[0, self.arena_bytes, []]]
        self.psum = []
        for b in range(8):
            t = self.es.enter_context(nc.psum_tensor(f"psb{b}", [128, 512], F32))
            self.psum.append(T(self, t[:, :], f"ps{b}"))
        self.ps_free = list(range(8))
        self.dma_sems = {}
        self.dma_rr = {}
        self.dma_last = {}
        self.dma_cnt = {}
        self.out_events = []
        self.n_instr = 0

    def alloc(self, nbytes, name=""):
        nbytes = (nbytes + 63) // 64 * 64
        best = None
        for i, seg in enumerate(self.free):
            if seg[1] - seg[0] >= nbytes:
                st = seg[3] if len(seg) > 3 else -1
                if best is None or st < best[0]:
                    best = (st, i)
        if best is not None:
            i = best[1]; seg = self.free[i]; lo, hi, ev = seg[0], seg[1], seg[2]
            st = seg[3] if len(seg) > 3 else -1
            if hi - lo == nbytes:
                self.free.pop(i)
            else:
                self.free[i] = [lo + nbytes, hi, ev, st]
            t = T(self, self.arena[:, lo // 4:(lo + nbytes) // 4], name, lo, nbytes)
            t.inherit = list(ev)
            return t
        raise MemoryError(f"arena full allocating {nbytes} for {name}; free={[(q[0], q[1]) for q in self.free]}")

    def release(self, t):
        ev = self._dedupe(t.events())
        self.free.append([t.off, t.off + t.nbytes, ev, self.n_instr])
        self.free.sort(key=lambda s: s[0])
        merged = []
        for seg in self.free:
            if len(seg) < 4:
                seg.append(-1)
            if merged and merged[-1][1] == seg[0]:
                merged[-1][1] = seg[1]
                merged[-1][2] = self._dedupe(merged[-1][2] + seg[2])
                merged[-1][3] = max(merged[-1][3], seg[3])
            else:
                merged.append(seg)
        self.free = merged

    @staticmethod
    def _dedupe(evs):
        best = {}; out = []
        for e in evs:
            if isinstance(e, Instr) and not e.is_dma:
                k = e.eng
                if k not in best or best[k].idx_ < e.idx_:
                    best[k] = e
            else:
                out.append(e)
        seen = set(); res = []
        for e in out:
            if id(e) not in seen:
                seen.add(id(e)); res.append(e)
        return list(best.values()) + res

    def ps_alloc(self):
        assert self.ps_free, "out of PSUM banks"
        return self.psum[self.ps_free.pop(0)]

    def ps_release(self, t):
        self.ps_free.append(self.psum.index(t))

    def _deps(self, reads, writes):
        deps = []
        for t in reads:
            if t.last_w is not None:
                deps.append(t.last_w)
            deps.extend(t.inherit)
        for t in writes:
            deps.extend(t.events())
        return deps

    def op(self, eng, fn, reads=(), writes=()):
        ins = Instr(eng, fn, self._deps(reads, writes))
        ins.idx_ = self.n_instr; self.n_instr += 1
        self.streams[eng].append(ins)
        for t in reads:
            t.readers[eng] = ins
        for t in writes:
            t.last_w = ins; t.readers = {}; t.dreaders = []; t.inherit = []
        return ins

    def dma(self, queue, out_ap, in_ap, reads=(), writes=(), is_output=False, **kw):
        k = self.dma_rr.get(queue, 0)
        self.dma_rr[queue] = (k + 1) % NDMASEM
        key = (queue, k)
        if key not in self.dma_sems:
            self.dma_sems[key] = self.es.enter_context(self.nc.semaphore(f"d_{queue}{k}"))
            self.dma_cnt[key] = 0
        sem = self.dma_sems[key]
        deps = self._deps(reads, writes)
        if key in self.dma_last:
            deps.append(self.dma_last[key])
        self.dma_cnt[key] += 16
        ev = DmaEv(sem, self.dma_cnt[key])
        self.dma_last[key] = ev

        def fn(e, out_ap=out_ap, in_ap=in_ap, kw=kw):
            return e.dma_start(out=out_ap, in_=in_ap, **kw)
        ins = Instr(queue, fn, deps)
        ins.idx_ = self.n_instr; self.n_instr += 1
        ins.is_dma = True; ins.dsem = sem
        self.streams[queue].append(ins)
        for t in reads:
            t.dreaders.append(ev)
        for t in writes:
            t.last_w = ev; t.readers = {}; t.dreaders = []; t.inherit = []
        if is_output or not writes:
            self.out_events.append(ev)
        return ev

    def finalize(self):
        nc = self.nc
        for eng in ENGS:
            for ins in self.streams[eng]:
                for d in ins.deps:
                    if isinstance(d, Instr):
                        if d.eng == "pe" and ins.eng == "pe" and not ins.is_dma:
                            continue
                        d.flag = True
        sems = {}
        for eng in ENGS:
            c = 0
            for ins in self.streams[eng]:
                if ins.flag:
                    ep = c // EPOCH
                    if (eng, ep) not in sems:
                        sems[(eng, ep)] = self.es.enter_context(nc.semaphore(f"c_{eng}{ep}"))
                    ins.sem = sems[(eng, ep)]; ins.val = c % EPOCH + 1
                    c += 1
        out_events = self.out_events
        streams = self.streams
        engmap = {"pe": "tensor", "act": "scalar", "dve": "vector", "pool": "gpsimd", "sp": "sync"}
        blk = self.es.enter_context(nc.Block())
        stats = {}

        def make(eng):
            def body(e):
                waited = {}
                nw = 0
                for ins in streams[eng]:
                    need = {}
                    for d in ins.deps:
                        if isinstance(d, Instr):
                            if d.sem is None:
                                continue
                            s, v = d.sem, d.val
                        else:
                            s, v = d.sem, d.val
                        if need.get(id(s), (None, 0))[1] < v:
                            need[id(s)] = (s, v)
                    for sid, (s, v) in need.items():
                        if waited.get(sid, 0) < v:
                            e.wait_ge(s, v); waited[sid] = v; nw += 1
                    bi = ins.fn(e)
                    if ins.is_dma:
                        bi.then_inc(ins.dsem, 16)
                    elif ins.flag:
                        bi.then_inc(ins.sem, 1)
                if eng == "sp":
                    fin = {}
                    for ev in out_events:
                        if fin.get(id(ev.sem), (None, 0))[1] < ev.val:
                            fin[id(ev.sem)] = (ev.sem, ev.val)
                    for sid, (s, v) in fin.items():
                        if waited.get(sid, 0) < v:
                            e.wait_ge(s, v)
                stats[eng] = (len(streams[eng]), nw)
            return body
        for eng in ENGS:
            if streams[eng] or eng == "sp":
                getattr(blk, engmap[eng])(make(eng))
        self.stats = stats
        self.es.close()


NCORES = 8
D = 1024; DEPTH = 4; SEQ = 2048; NSS = 16; LS = 8
NTOK = SEQ + NSS * LS
DFF = 2816; PLE = 256
TT = 256
NT = 9
EPS = 1e-6
POOLW = (2, 4, 8, 16)


def tile_cols(t):
    return (256 * t, 256 * t + 256) if t < 8 else (2048, 2176)


class B:
    def __init__(self, P):
        self.P = P

    def mm(self, ps, lhsT, rhs, start, stop, reads, n):
        self.P.op("pe", lambda e: e.matmul(ps.ap[:, 0:n], lhsT=lhsT, rhs=rhs, start=start, stop=stop), reads, [ps])

    def act(self, out, in_, func, reads, writes, **kw):
        self.P.op("act", lambda e: e.activation(out=out, in_=in_, func=func, **kw), reads, writes)

    def tt(self, out, in0, in1, op, reads, writes, eng="dve"):
        self.P.op(eng, lambda e: e.tensor_tensor(out=out, in0=in0, in1=in1, op=op), reads, writes)

    def ts(self, out, in0, s1, s2, op0, op1, reads, writes, eng="dve"):
        if op1 is None:
            self.P.op(eng, lambda e: e.tensor_scalar(out=out, in0=in0, scalar1=s1, scalar2=None, op0=op0), reads, writes)
        else:
            self.P.op(eng, lambda e: e.tensor_scalar(out=out, in0=in0, scalar1=s1, scalar2=s2, op0=op0, op1=op1), reads, writes)

    def stt(self, out, in0, scalar, in1, op0, op1, reads, writes, eng="dve"):
        self.P.op(eng, lambda e: e.scalar_tensor_tensor(out=out, in0=in0, scalar=scalar, in1=in1, op0=op0, op1=op1), reads, writes)

    def scan(self, out, d0, d1, init, reads, writes):
        self.P.op("dve", lambda e: e.tensor_tensor_scan(out=out, data0=d0, data1=d1, initial=init, op0=ALU.mult, op1=ALU.add), reads, writes)

    def copy(self, out, in_, reads, writes, eng="dve"):
        self.P.op(eng, lambda e: e.tensor_copy(out=out, in_=in_), reads, writes)

    def memset(self, out, val, writes, eng="dve"):
        self.P.op(eng, lambda e: e.memset(out, val), [], writes)

    def recip(self, out, in_, reads, writes):
        self.P.op("dve", lambda e: e.reciprocal(out=out, in_=in_), reads, writes)


def build_program(depth_run=DEPTH):
    nc = bass.Bass("TRN2", target_bir_lowering=False)

    def din(name, shape):
        return nc.dram_tensor(name, list(shape), F32, kind="ExternalInput").ap()

    def dout(name, shape):
        return nc.dram_tensor(name, list(shape), F32, kind="ExternalOutput").ap()

    xT = din("xT", [128, 8, NTOK])
    pT = din("pT", [DEPTH, 128, 2, NTOK])
    poolbuf = din("poolbuf", [DEPTH, 128, 4, NSS, 15])
    spool_tm = din("spool_tm", [DEPTH, NSS, 15, 512])
    h0re = din("h0re", [DEPTH, 128, 16, NSS]); h0im = din("h0im", [DEPTH, 128, 16, NSS])
    gvec = din("gvec", [128, DEPTH * 3 * 8]); gfin = din("gfin", [128, 8])
    pscale = din("pscale", [128, DEPTH * 4]); bglu = din("bglu", [128, DEPTH * 4])
    invcnt = din("invcnt", [128, 64]); mask8 = din("mask8", [128, 128])
    afm2 = din("afm2", [128, 192])
    abc = din("abc", [DEPTH, 3, 2048])
    Bq = din("Bq", [DEPTH, 2, 128, 2048])
    Cq = din("Cq", [DEPTH, 2, 128, 2048])
    Dq = din("Dq", [DEPTH, 128, 512])
    w_in = din("w_in", [DEPTH, D, D]); w_out = din("w_out", [DEPTH, D, D])
    w_pool = din("w_pool", [DEPTH, 4, 128, 128]); w_glu = din("w_glu", [DEPTH, 512, 512])
    w_gu = din("w_gate_up", [DEPTH, D, 2 * DFF]); w_dn = din("w_down", [DEPTH, DFF, D])
    w_ple = din("w_ple", [DEPTH, PLE, D]); w_pg = din("w_ple_gate", [DEPTH, D, D])

    yT = dout("yT", [128, 8, NTOK])
    npool_p = dout("npool_p", [DEPTH, 15, 512])
    npool_s = dout("npool_s", [DEPTH, NSS, 15, 512])
    nre_p = dout("nre_p", [DEPTH, 128, 16]); nim_p = dout("nim_p", [DEPTH, 128, 16])
    nre_s = dout("nre_s", [DEPTH, 128, 16, NSS]); nim_s = dout("nim_s", [DEPTH, 128, 16, NSS])

    P = Prog(nc, 212000 // 4)
    b = B(P)

    def f32(t, n=None, off=0):
        a = t.ap
        return a[:, off:off + n] if n is not None else a

    def bf(t, n=None, off=0):
        a = t.ap.bitcast(BF16)
        return a[:, off:off + n] if n is not None else a

    Xblk = P.alloc(8 * NTOK * 4, "X")
    XU = [[T(P, Xblk.ap[:, k * NTOK + tile_cols(t)[0]: k * NTOK + tile_cols(t)[1]], f"x{k}_{t}") for t in range(NT)] for k in range(8)]
    cst = P.alloc(4096, "consts")
    ca = cst.ap
    G_ = ca[:, 0:96]; GF = ca[:, 96:104]; PSC = ca[:, 104:120]; BGL = ca[:, 120:136]
    INVC = ca[:, 136:200]; MASK = ca[:, 200:328]
    ONES = ca[:, 328:392].bitcast(BF16)
    onesf = P.alloc(512, "onesf")
    P.dma("sp", G_, gvec, writes=[cst]); P.dma("sp", GF, gfin, writes=[cst])
    P.dma("sp", PSC, pscale, writes=[cst]); P.dma("sp", BGL, bglu, writes=[cst])
    P.dma("sp", INVC, invcnt, writes=[cst]); P.dma("sp", MASK, mask8, writes=[cst])
    b.memset(onesf.ap[:, 0:128], 1.0, [onesf])
    b.copy(ONES, onesf.ap[:, 0:128], [onesf], [cst])
    for t in range(NT):
        c0, c1 = tile_cols(t)
        for k in range(8):
            P.dma("sp", XU[k][t].ap, xT[:, k, c0:c1], writes=[XU[k][t]])

    def load_w(dst_t, src_ap, nk, ncols, eltoff=0):
        dst = bf(dst_t)[:, eltoff:eltoff + nk * ncols].rearrange("p (k n) -> p k n", k=nk)
        src = src_ap.rearrange("(k p) n -> p k n", p=128)
        half = max(1, nk // 2)
        for k0 in range(0, nk, half):
            k1 = min(nk, k0 + half)
            P.dma("pool", dst[:, k0:k1, :], src[:, k0:k1, :], writes=[dst_t])

    def rmsnorm_tile(units, n, gcol0, out_fn, out_tiles, f32out=False):
        gsrc, g0 = gcol0
        xin = [P_ap_join(units[k]) for k in range(8)]
        ps = P.ps_alloc()
        for k in range(8):
            sq = P.alloc(n * 2, "sq")
            b.act(bf(sq, n), xin[k], AF.Square, units[k], [sq])
            b.mm(ps, ONES, bf(sq, n), k == 0, k == 7, [cst, sq], n)
            P.release(sq)
        rs = P.alloc(n * 4, "rstd")
        b.act(f32(rs, n), ps.ap[:, 0:n], AF.Sqrt, [ps], [rs], scale=1.0 / D, bias=EPS)
        P.ps_release(ps)
        b.recip(f32(rs, n), f32(rs, n), [rs], [rs])
        for k in range(8):
            b.stt(out_fn(k), xin[k], gsrc[:, g0 + k:g0 + k + 1], f32(rs, n), ALU.mult, ALU.mult,
                  units[k] + [rs, cst], [out_tiles[k]] if isinstance(out_tiles, list) else [out_tiles])
        P.release(rs)

    def P_ap_join(us):
        if len(us) == 1:
            return us[0].ap
        a0 = us[0].ap; n = sum(u.ap.shape[1] for u in us)
        return us[0].wide(n)

    def wide(self, n):
        return self.base[:, self.c0:self.c0 + n]
    T.wide = wide
    for k in range(8):
        for t in range(NT):
            XU[k][t].base = Xblk.ap; XU[k][t].c0 = k * NTOK + tile_cols(t)[0]

    def load_mixer_weights(l):
        Win = P.alloc(16384, "Win"); load_w(Win, w_in[l], 8, 1024)
        Wout = P.alloc(16384, "Wout"); load_w(Wout, w_out[l], 8, 1024)
        Wglu = P.alloc(4096, "Wglu"); load_w(Wglu, w_glu[l], 4, 512)
        Wpool = P.alloc(1024, "Wpool")
        P.dma("pool", bf(Wpool, 512).rearrange("p (g d) -> p g d", g=4), w_pool[l].rearrange("g c d -> c g d"), writes=[Wpool])
        DD = P.alloc(1024, "DD")
        P.dma("pool", bf(DD, 512), Dq[l], writes=[DD])
        BB = P.alloc(8192, "Braw")
        for ri in range(2):
            P.dma("pool", bf(BB)[:, ri * 2048:(ri + 1) * 2048], Bq[l, ri], writes=[BB])
        return (Win, Wout, Wglu, Wpool, None, DD, BB)

    for l in range(depth_run):
        if l == 0:
            MW = load_mixer_weights(0)
        Win, Wout, Wglu, Wpool, CTs, DD, BB = MW
        WinA = bf(Win).rearrange("p (k n) -> p k n", k=8); WoutA = bf(Wout).rearrange("p (k n) -> p k n", k=8)
        WgluA = bf(Wglu).rearrange("p (k n) -> p k n", k=4); WpoolA = bf(Wpool, 512).rearrange("p (g d) -> p g d", g=4)
        DDA = bf(DD, 512).rearrange("p (j n) -> p j n", j=4)
        BBA = bf(BB).rearrange("p (r j n) -> p j r n", r=2, j=4)

        def lam_math(are, aim, ldt, n, tiles_r, pfx):
            o = {}
            def new(nm):
                o[nm] = P.alloc(n * 4, pfx + nm); return o[nm]
            A = lambda t_: f32(t_, n)
            z = new("z"); b.ts(A(z), ldt, 0.125, None, ALU.mult, None, tiles_r, [z])
            dt = new("dt")
            b.ts(A(dt), A(z), 1.0 / 11, 1.0, ALU.mult, ALU.add, [z], [dt])
            for kk in range(10, 0, -1):
                b.tt(A(dt), A(dt), A(z), ALU.mult, [dt, z], [dt])
                b.ts(A(dt), A(dt), 1.0 / kk, 1.0, ALU.mult, ALU.add, [dt], [dt])
            for _ in range(3):
                b.tt(A(dt), A(dt), A(dt), ALU.mult, [dt], [dt])
            xx = new("xx"); b.tt(A(xx), A(dt), are, ALU.mult, [dt] + tiles_r, [xx])
            em1 = new("em1")
            b.ts(A(em1), A(xx), 1.0 / 6, 1.0, ALU.mult, ALU.add, [xx], [em1])
            for kk in (5, 4, 3, 2):
                b.tt(A(em1), A(em1), A(xx), ALU.mult, [em1, xx], [em1])
                b.ts(A(em1), A(em1), 1.0 / kk, 1.0, ALU.mult, ALU.add, [em1], [em1])
            b.tt(A(em1), A(em1), A(xx), ALU.mult, [em1, xx], [em1])
            r = new("r"); b.ts(A(r), A(em1), 1.0, None, ALU.add, None, [em1], [r])
            ang = new("ang"); b.tt(A(ang), A(dt), aim, ALU.mult, [dt] + tiles_r, [ang])
            s = new("s"); c = new("c"); tmp = new("tmp"); sh = new("sh")
            b.act(A(s), A(ang), AF.Sin, [ang], [s], scale=0.125)
            b.act(A(tmp), A(ang), AF.Sin, [ang], [tmp], scale=0.0625)
            b.tt(A(tmp), A(tmp), A(tmp), ALU.mult, [tmp], [tmp])
            b.ts(A(c), A(tmp), -2.0, 1.0, ALU.mult, ALU.add, [tmp], [c])
            for it in range(3):
                if it == 2:
                    b.copy(A(sh), A(s), [s], [sh])
                b.tt(A(tmp), A(s), A(s), ALU.mult, [s], [tmp])
                b.tt(A(s), A(s), A(c), ALU.mult, [s, c], [s])
                b.ts(A(s), A(s), 2.0, None, ALU.mult, None, [s], [s])
                b.tt(A(c), A(c), A(c), ALU.mult, [c], [c])
                b.tt(A(c), A(c), A(tmp), ALU.subtract, [c, tmp], [c])
            for nm in ("z", "xx", "ang"):
                P.release(o.pop(nm))
            o["tmp"] = tmp; o["sh"] = sh
            return o

        def k_math(o, are, aim, n, tiles_r):
            A = lambda t_: f32(t_, n)
            nr = P.alloc(n * 4, "nr"); li = P.alloc(n * 4, "li"); den = P.alloc(n * 4, "den")
            tmp = o["tmp"]
            b.tt(A(tmp), A(o["sh"]), A(o["sh"]), ALU.mult, [o["sh"]], [tmp])
            b.tt(A(nr), A(o["em1"]), A(o["c"]), ALU.mult, [o["em1"], o["c"]], [nr])
            b.stt(A(nr), A(tmp), -2.0, A(nr), ALU.mult, ALU.add, [tmp, nr], [nr])
            b.tt(A(li), A(o["r"]), A(o["s"]), ALU.mult, [o["r"], o["s"]], [li])
            b.tt(A(den), are, are, ALU.mult, tiles_r, [den])
            b.tt(A(tmp), aim, aim, ALU.mult, tiles_r, [tmp])
            b.tt(A(den), A(den), A(tmp), ALU.add, [den, tmp], [den])
            b.recip(A(den), A(den), [den], [den])
            kre = P.alloc(n * 4, "kre"); kim = P.alloc(n * 4, "kim")
            b.tt(A(kre), A(nr), are, ALU.mult, [nr] + tiles_r, [kre])
            b.tt(A(tmp), A(li), aim, ALU.mult, [li] + tiles_r, [tmp])
            b.tt(A(kre), A(kre), A(tmp), ALU.add, [kre, tmp], [kre])
            b.tt(A(kre), A(kre), A(den), ALU.mult, [kre, den], [kre])
            b.tt(A(kim), A(li), are, ALU.mult, [li] + tiles_r, [kim])
            b.tt(A(tmp), A(nr), aim, ALU.mult, [nr] + tiles_r, [tmp])
            b.tt(A(kim), A(kim), A(tmp), ALU.subtract, [kim, tmp], [kim])
            b.tt(A(kim), A(kim), A(den), ALU.mult, [kim, den], [kim])
            for t_ in (nr, li, den):
                P.release(t_)
            o["kre"] = kre; o["kim"] = kim

        if l == 0:
            pfA = P.alloc(768, "afmA"); P.dma("sp", f32(pfA, 192), afm2, writes=[pfA])
            foA = lam_math(f32(pfA, 64, 0), f32(pfA, 64, 64), f32(pfA, 64, 128), 64, [pfA], "fmA_")
            k_math(foA, f32(pfA, 64, 0), f32(pfA, 64, 64), 64, [pfA])
            lamrA = P.alloc(256, "lamrA"); lamiA = P.alloc(256, "lamiA")
            kt = P.alloc(256, "kt"); ikr = P.alloc(256, "ikr"); iki = P.alloc(256, "iki"); lr0 = P.alloc(256, "lr0"); li0 = P.alloc(256, "li0")
            F = lambda t_: f32(t_, 64)
            KrA, KiA, RtA, CtA, StA = foA["kre"], foA["kim"], foA["r"], foA["c"], foA["s"]
            b.tt(F(kt), F(KrA), F(KrA), ALU.mult, [KrA], [kt])
            b.tt(F(ikr), F(KiA), F(KiA), ALU.mult, [KiA], [ikr])
            b.tt(F(kt), F(kt), F(ikr), ALU.add, [kt, ikr], [kt])
            b.recip(F(kt), F(kt), [kt], [kt])
            b.tt(F(ikr), F(KrA), F(kt), ALU.mult, [KrA, kt], [ikr])
            b.stt(F(iki), F(KiA), -1.0, F(kt), ALU.mult, ALU.mult, [KiA, kt], [iki])
            b.tt(F(lr0), F(RtA), F(CtA), ALU.mult, [RtA, CtA], [lr0])
            b.tt(F(li0), F(RtA), F(StA), ALU.mult, [RtA, StA], [li0])
            b.tt(F(lamrA), F(lr0), F(ikr), ALU.mult, [lr0, ikr], [lamrA])
            b.tt(F(kt), F(li0), F(iki), ALU.mult, [li0, iki], [kt])
            b.tt(F(lamrA), F(lamrA), F(kt), ALU.subtract, [lamrA, kt], [lamrA])
            b.tt(F(lamiA), F(lr0), F(iki), ALU.mult, [lr0, iki], [lamiA])
            b.tt(F(kt), F(li0), F(ikr), ALU.mult, [li0, ikr], [kt])
            b.tt(F(lamiA), F(lamiA), F(kt), ALU.add, [lamiA, kt], [lamiA])
            for t_ in (kt, ikr, iki, lr0, li0, pfA, foA["dt"], foA["em1"], foA["tmp"], foA["sh"]):
                P.release(t_)
            GLB = dict(r=RtA, c=CtA, s=StA, kre=KrA, kim=KiA, lamr=lamrA, lami=lamiA)

        def lview(base):
            v = T(P, base.ap[:, 16 * l:16 * l + 16], base.name + f"_l{l}"); v.last_w = base.last_w
            return v
        Rt, Ct, St, Kr, Ki, lamr, lami = (lview(GLB[x_]) for x_ in ("r", "c", "s", "kre", "kim", "lamr", "lami"))
        CT = P.alloc(12288, "CT")
        CTA = bf(CT).rearrange("p (r b n) -> p r b n", r=3, b=16)
        for hb in range(2):
            cs_ = P.alloc(8192, "cstage"); c1 = P.alloc(4096, "c1"); c2 = P.alloc(4096, "c2")
            for ri in range(2):
                P.dma("sp", f32(cs_, 1024, 1024 * ri), Cq[l, ri, :, 1024 * hb:1024 * hb + 1024], writes=[cs_])
            cre = f32(cs_, 1024, 0).rearrange("p (b n) -> p b n", b=8); cim = f32(cs_, 1024, 1024).rearrange("p (b n) -> p b n", b=8)
            krb = f32(Kr, 16)[:, 8 * hb:8 * hb + 8].unsqueeze(2).to_broadcast([128, 8, 128])
            kib = f32(Ki, 16)[:, 8 * hb:8 * hb + 8].unsqueeze(2).to_broadcast([128, 8, 128])
            v1 = f32(c1, 1024).rearrange("p (b n) -> p b n", b=8); v2 = f32(c2, 1024).rearrange("p (b n) -> p b n", b=8)
            b.tt(v1, cre, krb, ALU.mult, [cs_, Kr], [c1]); b.tt(v2, cim, kib, ALU.mult, [cs_, Ki], [c2])
            b.tt(CTA[:, 0, 8 * hb:8 * hb + 8, :], v1, v2, ALU.subtract, [c1, c2], [CT])
            b.tt(CTA[:, 1, 8 * hb:8 * hb + 8, :], v2, v1, ALU.subtract, [c1, c2], [CT])
            b.tt(v1, cre, kib, ALU.mult, [cs_, Ki], [c1]); b.tt(v2, cim, krb, ALU.mult, [cs_, Kr], [c2])
            b.stt(CTA[:, 2, 8 * hb:8 * hb + 8, :], v1, -1.0, v2, ALU.mult, ALU.subtract, [c1, c2], [CT])
            for t_ in (cs_, c1, c2):
                P.release(t_)
        TC = P.alloc(16384, "TC"); TS = P.alloc(16384, "TS")
        TCA = f32(TC).rearrange("p (b n) -> p b n", b=16); TSA = f32(TS).rearrange("p (b n) -> p b n", b=16)
        b.memset(TCA[:, :, 0:1], 1.0, [TC]); b.memset(TSA[:, :, 0:1], 0.0, [TS])
        ec = P.alloc(64, "ec"); es_ = P.alloc(64, "es"); et = P.alloc(64, "et")
        b.copy(f32(ec, 16), f32(Ct, 16), [Ct], [ec]); b.copy(f32(es_, 16), f32(St, 16), [St], [es_])
        tq = P.alloc(16 * 128 * 4, "tq")
        kk = 1
        while kk < 256:
            ecb = f32(ec, 16).unsqueeze(2).to_broadcast([128, 16, kk]); esb = f32(es_, 16).unsqueeze(2).to_broadcast([128, 16, kk])
            tqa = f32(tq, 16 * kk).rearrange("p (b n) -> p b n", b=16)
            b.tt(TCA[:, :, kk:2 * kk], TCA[:, :, 0:kk], ecb, ALU.mult, [TC, ec], [TC])
            b.tt(tqa, TSA[:, :, 0:kk], esb, ALU.mult, [TS, es_], [tq])
            b.tt(TCA[:, :, kk:2 * kk], TCA[:, :, kk:2 * kk], tqa, ALU.subtract, [TC, tq], [TC])
            b.tt(TSA[:, :, kk:2 * kk], TSA[:, :, 0:kk], ecb, ALU.mult, [TS, ec], [TS])
            b.tt(tqa, TCA[:, :, 0:kk], esb, ALU.mult, [TC, es_], [tq])
            b.tt(TSA[:, :, kk:2 * kk], TSA[:, :, kk:2 * kk], tqa, ALU.add, [TS, tq], [TS])
            b.tt(f32(et, 16), f32(es_, 16), f32(es_, 16), ALU.mult, [es_], [et])
            b.tt(f32(es_, 16), f32(es_, 16), f32(ec, 16), ALU.mult, [es_, ec], [es_])
            b.ts(f32(es_, 16), f32(es_, 16), 2.0, None, ALU.mult, None, [es_], [es_])
            b.tt(f32(ec, 16), f32(ec, 16), f32(ec, 16), ALU.mult, [ec], [ec])
            b.tt(f32(ec, 16), f32(ec, 16), f32(et, 16), ALU.subtract, [ec, et], [ec])
            kk *= 2
        P.release(tq); P.release(et)
        nes = P.alloc(128, "nes")
        b.ts(f32(nes, 16), f32(es_, 16), -1.0, None, ALU.mult, None, [es_], [nes])
        b.ts(f32(nes, 32)[:, 16:32], TSA[:, :, 255], -1.0, None, ALU.mult, None, [TS], [nes])
        hr = P.alloc(1024, "h0r"); hi = P.alloc(1024, "h0i")
        P.dma("sp", f32(hr, 256), h0re[l].rearrange("p b s -> p (b s)"), writes=[hr])
        P.dma("sp", f32(hi, 256), h0im[l].rearrange("p b s -> p (b s)"), writes=[hi])
        injr = P.alloc(1024, "injr"); inji = P.alloc(1024, "inji"); itmp = P.alloc(1024, "itmp")
        h3 = lambda t_: f32(t_, 256).rearrange("p (b s) -> p b s", b=16)
        lrb = f32(lamr, 16).unsqueeze(2).to_broadcast([128, 16, 16]); lib = f32(lami, 16).unsqueeze(2).to_broadcast([128, 16, 16])
        b.tt(h3(injr), h3(hr), lrb, ALU.mult, [hr, lamr], [injr])
        b.tt(h3(itmp), h3(hi), lib, ALU.mult, [hi, lami], [itmp])
        b.tt(h3(injr), h3(injr), h3(itmp), ALU.subtract, [injr, itmp], [injr])
        b.tt(h3(inji), h3(hi), lrb, ALU.mult, [hi, lamr], [inji])
        b.tt(h3(itmp), h3(hr), lib, ALU.mult, [hr, lami], [itmp])
        b.tt(h3(inji), h3(inji), h3(itmp), ALU.add, [inji, itmp], [inji])
        for t_ in (hr, hi, itmp):
            P.release(t_)
        initr = P.alloc(64, "initr"); initi = P.alloc(64, "initi")
        b.memset(f32(initr, 16), 0.0, [initr]); b.memset(f32(initi, 16), 0.0, [initi])
        hfr = P.alloc(64, "hfr"); hfi = P.alloc(64, "hfi")
        hsr = P.alloc(1024, "hsr"); hsi = P.alloc(1024, "hsi")

        UE = [P.alloc(max(15 + TT, NSS * 23) * 4, f"uext{g}") for g in range(4)]
        for g in range(4):
            b.memset(f32(UE[g], 15), 0.0, [UE[g]])

        def front_gen(t, c):
            c0, c1 = tile_cols(t); n = c1 - c0
            sample = (t == NT - 1)
            nseq, L = (NSS, LS) if sample else (1, TT)
            c.update(t=t, n=n, sample=sample, nseq=nseq, L=L)
            ps = P.ps_alloc()
            for k in range(8):
                sq = P.alloc(n * 2, "sq")
                b.act(bf(sq, n), XU[k][t].ap, AF.Square, [XU[k][t]], [sq])
                b.mm(ps, ONES, bf(sq, n), k == 0, k == 7, [cst, sq], n)
                P.release(sq)
            rs = P.alloc(n * 4, "rstd")
            b.act(f32(rs, n), ps.ap[:, 0:n], AF.Sqrt, [ps], [rs], scale=1.0 / D, bias=EPS)
            P.ps_release(ps)
            yield
            xn = [P.alloc(n * 2, f"xn{k}") for k in range(8)]
            b.recip(f32(rs, n), f32(rs, n), [rs], [rs])
            g0_ = (l * 3 + 0) * 8
            for k in range(8):
                b.stt(bf(xn[k], n), XU[k][t].ap, G_[:, g0_ + k:g0_ + k + 1], f32(rs, n), ALU.mult, ALU.mult, [XU[k][t], rs, cst], [xn[k]])
            P.release(rs)
            yield
            if sample:
                for g in range(4):
                    P.dma("sp", f32(UE[g], NSS * 23).rearrange("p (s j) -> p s j", s=NSS)[:, :, 0:15], poolbuf[l, :, g], writes=[UE[g]])
            us = [P.alloc(n * 2, f"us{j}") for j in range(4)]
            for m in range(8):
                ps = P.ps_alloc()
                for k in range(8):
                    b.mm(ps, WinA[:, k, 128 * m:128 * m + 128], bf(xn[k], n), k == 0, k == 7, [Win, xn[k]], n)
                if m < 4:
                    dst = f32(UE[m], nseq * (15 + L)).rearrange("p (s j) -> p s j", s=nseq)[:, :, 15:15 + L]
                    b.act(dst, ps.ap[:, 0:n].rearrange("p (s j) -> p s j", s=nseq), AF.Copy, [ps], [UE[m]])
                else:
                    b.act(bf(us[m - 4], n), ps.ap[:, 0:n], AF.Copy, [ps], [us[m - 4]])
                P.ps_release(ps)
                yield
            if t == 7 or sample:
                ps = P.ps_alloc()
                m0 = n - 15 if t == 7 else 0
                mrows = n - m0
                for k in range(8):
                    o_ap = ps.ap[0:mrows, 0:512]; l_ap = bf(xn[k], n)[:, m0:n]; r_ap = WinA[:, k, 0:512]
                    P.op("pe", lambda e, o_ap=o_ap, l_ap=l_ap, r_ap=r_ap, st=(k == 0), sp_=(k == 7): e.matmul(o_ap, lhsT=l_ap, rhs=r_ap, start=st, stop=sp_), [Win, xn[k]], [ps])
                zt = P.alloc(2048, "ztm")
                b.act(f32(zt, 512)[0:mrows, :], ps.ap[0:mrows, 0:512], AF.Copy, [ps], [zt])
                P.ps_release(ps)
                if t == 7:
                    P.dma("sp", npool_p[l], f32(zt, 512)[0:15, :], reads=[zt], is_output=True)
                else:
                    for s_ in range(NSS):
                        P.dma("sp", npool_s[l, s_, 7:15, :], f32(zt, 512)[8 * s_:8 * s_ + 8, :], reads=[zt], is_output=True)
                    P.dma("sp", npool_s[l, :, 0:7, :], spool_tm[l, :, 8:15, :], is_output=True)
                P.release(zt)
            for k in range(8):
                P.release(xn[k])
            yield
            ycat = [P.alloc(n * 2, f"yc{k}") for k in range(4)] + [None] * 4
            c.update(us=us, ycat=ycat)
            W_ = 15 + L
            for g in range(4):
                E3 = f32(UE[g], nseq * W_).rearrange("p (s j) -> p s j", s=nseq)
                sa = P.alloc(nseq * W_ * 4, "sa"); sb = P.alloc(nseq * W_ * 4, "sb")
                A3 = f32(sa, nseq * W_).rearrange("p (s j) -> p s j", s=nseq); B3 = f32(sb, nseq * W_).rearrange("p (s j) -> p s j", s=nseq)
                b.tt(A3[:, :, 1:W_], E3[:, :, 1:W_], E3[:, :, 0:W_ - 1], ALU.add, [UE[g]], [sa])
                cur, curT, oth, othT, lo, sh_ = A3, sa, B3, sb, 1, 2
                for _ in range(g):
                    b.tt(oth[:, :, lo + sh_:W_], cur[:, :, lo + sh_:W_], cur[:, :, lo:W_ - sh_], ALU.add, [curT], [othT])
                    cur, curT, oth, othT = oth, othT, cur, curT
                    lo += sh_; sh_ *= 2
                df = P.alloc(n * 2, "diff")
                D3 = bf(df, n).rearrange("p (s j) -> p s j", s=nseq)
                b.stt(D3, cur[:, :, 15:W_], 1.0 / POOLW[g], E3[:, :, 15:W_], ALU.mult, ALU.subtract, [curT, UE[g]], [df])
                if t == 0:
                    fx = P.alloc(64, "fx")
                    b.tt(f32(fx, 16), f32(curT, W_)[:, 15:31], INVC[:, 16 * g:16 * g + 16], ALU.mult, [curT, cst], [fx])
                    b.tt(bf(df, n)[:, 0:16], f32(fx, 16), f32(UE[g], W_)[:, 15:31], ALU.subtract, [fx, UE[g]], [df])
                    P.release(fx)
                P.release(sa); P.release(sb)
                ps = P.ps_alloc()
                b.mm(ps, WpoolA[:, g, :], bf(df, n), True, True, [Wpool, df], n)
                b.act(bf(ycat[g], n), ps.ap[:, 0:n], AF.Copy, [ps, cst], [ycat[g]], scale=PSC[:, l * 4 + g:l * 4 + g + 1])
                P.ps_release(ps); P.release(df)
                if not sample and t < 7:
                    hc = P.alloc(64, "hc")
                    b.copy(f32(hc, 15), f32(UE[g], W_)[:, L:L + 15], [UE[g]], [hc])
                    b.copy(f32(UE[g], 15), f32(hc, 15), [hc], [UE[g]])
                    P.release(hc)
                yield
            return

        def ssm(c, gen=None, wgen=None):
            t = c["t"]; n = c["n"]; sample = c["sample"]; nseq = c["nseq"]; L = c["L"]; us = c["us"]
            gel = [P.alloc(n * 2, f"gel{j}") for j in range(4)]
            ystate = {"yps": None}

            G = 2 if sample else 1
            NG = 16 // G
            gn = G * n

            def gview(ap):
                if sample:
                    return ap.rearrange("p (g s j) -> p g s j", g=G, s=NSS)
                return ap.rearrange("p (g m) -> p g m", g=G)

            def ph0(gi):
                psb = P.ps_alloc()
                for g in range(G):
                    blk = gi * G + g; j, i = blk // 4, blk % 4
                    for ri in (0, 1):
                        o_ap = psb.ap[:, 256 * ri + g * n:256 * ri + (g + 1) * n]; l_ap = BBA[:, j, ri, 128 * i:128 * i + 128]; r_ap = bf(us[j], n)
                        P.op("pe", lambda e, o_ap=o_ap, l_ap=l_ap, r_ap=r_ap: e.matmul(o_ap, lhsT=l_ap, rhs=r_ap, start=True, stop=True), [BB, us[j]], [psb])
                return psb

            def ph1(gi, psb):
                b0 = gi * G
                if sample:
                    Cb = TCA[:, b0:b0 + G, 0:LS].unsqueeze(2).to_broadcast([128, G, NSS, LS])
                    Sb = TSA[:, b0:b0 + G, 0:LS].unsqueeze(2).to_broadcast([128, G, NSS, LS])
                else:
                    Cb = TCA[:, b0:b0 + G, 0:n]; Sb = TSA[:, b0:b0 + G, 0:n]
                V = gview
                pr = P.alloc(gn * 4, "pr"); pi_ = P.alloc(gn * 4, "pi")
                qr = P.alloc(gn * 4, "qr"); qi = P.alloc(gn * 4, "qi")
                tw, tw2 = qr, qi
                re_ap = psb.ap[:, 0:gn]; im_ap = psb.ap[:, 256:256 + gn]
                b.tt(V(f32(pr, gn)), V(re_ap), Cb, ALU.mult, [psb, TC], [pr])
                b.tt(V(f32(tw, gn)), V(im_ap), Sb, ALU.mult, [psb, TS], [tw])
                b.tt(V(f32(pi_, gn)), V(im_ap), Cb, ALU.mult, [psb, TC], [pi_])
                b.tt(V(f32(tw2, gn)), V(re_ap), Sb, ALU.mult, [psb, TS], [tw2])
                b.tt(f32(pr, gn), f32(pr, gn), f32(tw, gn), ALU.add, [pr, tw], [pr], eng="pool")
                b.tt(f32(pi_, gn), f32(pi_, gn), f32(tw2, gn), ALU.subtract, [pi_, tw2], [pi_], eng="pool")
                P.ps_release(psb)
                return dict(gi=gi, pr=pr, pi_=pi_, qr=qr, qi=qi, Cb=Cb, Sb=Sb)

            def ph2(c):
                gi, pr, pi_, Cb, Sb = c["gi"], c["pr"], c["pi_"], c["Cb"], c["Sb"]
                V = gview
                b0 = gi * G
                qr, qi = c["qr"], c["qi"]
                if sample:
                    p0 = V(f32(pr, gn))[:, :, :, 0]; p1 = V(f32(pi_, gn))[:, :, :, 0]
                    b.tt(p0, p0, h3(injr)[:, b0:b0 + G, :], ALU.add, [pr, injr], [pr])
                    b.tt(p1, p1, h3(inji)[:, b0:b0 + G, :], ALU.add, [pi_, inji], [pi_])
                    rm = P.alloc(gn * 4, "rm")
                    b.tt(f32(rm, gn).rearrange("p (g m) -> p g m", g=G), MASK.unsqueeze(1).to_broadcast([128, G, 128]),
                         f32(Rt, 16)[:, b0:b0 + G].unsqueeze(2).to_broadcast([128, G, 128]), ALU.mult, [cst, Rt], [rm])
                    b.scan(f32(qr, gn), f32(rm, gn), f32(pr, gn), 0.0, [rm, pr], [qr])
                    b.scan(f32(qi, gn), f32(rm, gn), f32(pi_, gn), 0.0, [rm, pi_], [qi])
                    P.release(rm)
                else:
                    for g in range(G):
                        blk = b0 + g; sl = slice(g * n, (g + 1) * n)
                        rb = f32(Rt, 16)[:, blk:blk + 1].to_broadcast([128, n])
                        b.scan(f32(qr, gn)[:, sl], rb, f32(pr, gn)[:, sl], f32(initr, 16)[:, blk:blk + 1], [Rt, pr, initr], [qr])
                        b.scan(f32(qi, gn)[:, sl], rb, f32(pi_, gn)[:, sl], f32(initi, 16)[:, blk:blk + 1], [Rt, pi_, initi], [qi])
                P.release(pr); P.release(pi_)
                a1 = P.alloc(gn * 2, "a1"); a2 = P.alloc(gn * 2, "a2"); a3 = P.alloc(gn * 2, "a3"); a4 = P.alloc(gn * 2, "a4")
                b.tt(V(bf(a4, gn)), V(f32(qi, gn)), Cb, ALU.mult, [qi, TC], [a4], eng="pool")
                b.tt(V(bf(a1, gn)), V(f32(qr, gn)), Cb, ALU.mult, [qr, TC], [a1])
                b.tt(V(bf(a2, gn)), V(f32(qi, gn)), Sb, ALU.mult, [qi, TS], [a2], eng="pool")
                b.tt(V(bf(a3, gn)), V(f32(qr, gn)), Sb, ALU.mult, [qr, TS], [a3], eng="pool")
                if not sample:
                    for g in range(G):
                        blk = b0 + g
                        ql_r = f32(qr, gn)[:, (g + 1) * n - 1:(g + 1) * n]; ql_i = f32(qi, gn)[:, (g + 1) * n - 1:(g + 1) * n]
                        sm = P.alloc(64, "sm")
                        if t < 7:
                            cL = f32(ec, 16)[:, blk:blk + 1]; sL = f32(es_, 16)[:, blk:blk + 1]
                            dr, di = f32(initr, 16)[:, blk:blk + 1], f32(initi, 16)[:, blk:blk + 1]; dT = (initr, initi)
                        else:
                            cL = TCA[:, blk, n - 1:n]; sL = TSA[:, blk, n - 1:n]
                            dr, di = f32(hfr, 16)[:, blk:blk + 1], f32(hfi, 16)[:, blk:blk + 1]; dT = (hfr, hfi)
                        rdT = [ec, es_, nes] if t < 7 else [TC, TS, nes]
                        nsL = f32(nes, 32)[:, blk:blk + 1] if t < 7 else f32(nes, 32)[:, 16 + blk:16 + blk + 1]
                        sm2 = P.alloc(64, "sm2")
                        b.act(f32(sm, 1), ql_i, AF.Identity, [qi] + rdT, [sm], scale=nsL)
                        b.act(dr, ql_r, AF.Identity, [qr, sm] + rdT, [dT[0]], scale=cL, bias=f32(sm, 1))
                        b.act(f32(sm2, 1), ql_r, AF.Identity, [qr] + rdT, [sm2], scale=sL)
                        b.act(di, ql_i, AF.Identity, [qi, sm2] + rdT, [dT[1]], scale=cL, bias=f32(sm2, 1))
                        P.release(sm); P.release(sm2)
                else:
                    q7r = V(f32(qr, gn))[:, :, :, LS - 1]; q7i = V(f32(qi, gn))[:, :, :, LS - 1]
                    c7 = TCA[:, b0:b0 + G, LS - 1:LS].to_broadcast([128, G, NSS]); s7 = TSA[:, b0:b0 + G, LS - 1:LS].to_broadcast([128, G, NSS])
                    sm = P.alloc(G * NSS * 4, "sm"); smv = f32(sm, G * NSS).rearrange("p (g s) -> p g s", g=G)
                    dr_ = h3(hsr)[:, b0:b0 + G, :]; di_ = h3(hsi)[:, b0:b0 + G, :]
                    b.tt(smv, q7i, s7, ALU.mult, [qi, TS], [sm])
                    b.tt(dr_, q7r, c7, ALU.mult, [qr, TC], [hsr])
                    b.tt(dr_, dr_, smv, ALU.subtract, [hsr, sm], [hsr])
                    b.tt(smv, q7r, s7, ALU.mult, [qr, TS], [sm])
                    b.tt(di_, q7i, c7, ALU.mult, [qi, TC], [hsi])
                    b.tt(di_, di_, smv, ALU.add, [hsi, sm], [hsi])
                    P.release(sm)
                P.release(qr); P.release(qi)
                return dict(gi=gi, a=(a1, a2, a3, a4))

            def ph3(c):
                gi = c["gi"]
                a1, a2, a3, a4 = c["a"]
                for g in range(G):
                    blk = gi * G + g; j, i = blk // 4, blk % 4
                    sl = slice(g * n, (g + 1) * n)
                    if i == 0:
                        ystate["yps"] = P.ps_alloc()
                    yps = ystate["yps"]
                    b.mm(yps, CTA[:, 0, blk, :], bf(a1, gn)[:, sl], i == 0, False, [CT, a1], n)
                    b.mm(yps, CTA[:, 1, blk, :], bf(a2, gn)[:, sl], False, False, [CT, a2], n)
                    b.mm(yps, CTA[:, 2, blk, :], bf(a3, gn)[:, sl], False, False, [CT, a3], n)
                    b.mm(yps, CTA[:, 2, blk, :], bf(a4, gn)[:, sl], False, False, [CT, a4], n)
                    if i == 3:
                        b.mm(yps, DDA[:, j, :], bf(us[j], n), False, True, [DD, us[j]], n)
                        b.act(bf(gel[j], n), yps.ap[:, 0:n], AF.Gelu, [yps], [gel[j]])
                        P.ps_release(yps)
                for t_ in (a1, a2, a3, a4):
                    P.release(t_)
            c0s, c1s, c2s = {0: ph0(0), 1: ph0(1)}, {}, {}
            for step in range(NG + 2):
                if step + 2 < NG:
                    c0s[step + 2] = ph0(step + 2)
                if step < NG:
                    c1s[step] = ph1(step, c0s.pop(step))
                if 1 <= step <= NG:
                    c2s[step - 1] = ph2(c1s.pop(step - 1))
                if step >= 2:
                    ph3(c2s.pop(step - 2))
                if wgen is not None:
                    next(wgen, None)
                if gen is not None and step >= 2:
                    next(gen, None)
            for g_ in (wgen, gen):
                if g_ is not None:
                    for _ in g_:
                        pass
            for j in range(4):
                P.release(us[j])
            c["gel"] = gel

        def glu(c):
            t = c["t"]; n = c["n"]; gel = c["gel"]; ycat = c["ycat"]
            for m in range(4):
                ps = P.ps_alloc()
                for k in range(4):
                    b.mm(ps, WgluA[:, k, 128 * m:128 * m + 128], bf(gel[k], n), k == 0, k == 3, [Wglu, gel[k]], n)
                sg = P.alloc(n * 2, "sg")
                b.act(bf(sg, n), ps.ap[:, 0:n], AF.Sigmoid, [ps, cst], [sg], bias=BGL[:, l * 4 + m:l * 4 + m + 1])
                P.ps_release(ps)
                ycat[4 + m] = P.alloc(n * 2, f"yc{4 + m}")
                b.tt(bf(ycat[4 + m], n), bf(gel[m], n), bf(sg, n), ALU.mult, [gel[m], sg], [ycat[4 + m]], eng="pool")
                P.release(sg)
            for j in range(4):
                P.release(gel[j])

        def wout_gen(c):
            t = c["t"]; n = c["n"]; ycat = c["ycat"]
            prev = None
            for m in range(9):
                cur = None
                if m < 8:
                    ps = P.ps_alloc()
                    for k in range(8):
                        b.mm(ps, WoutA[:, k, 128 * m:128 * m + 128], bf(ycat[k], n), k == 0, k == 7, [Wout, ycat[k]], n)
                    cur = (m, ps)
                if prev is not None:
                    pm, pps = prev
                    b.tt(XU[pm][t].ap, pps.ap[:, 0:n], XU[pm][t].ap, ALU.add, [pps, XU[pm][t]], [XU[pm][t]])
                    P.ps_release(pps)
                prev = cur
                yield
            for k in range(8):
                P.release(ycat[k])

        def wout(c):
            for _ in wout_gen(c):
                pass

        ctxs = {0: {}}
        for _ in front_gen(0, ctxs[0]):
            pass
        for t in range(NT):
            gen = None
            if t + 1 < NT:
                ctxs[t + 1] = {}
                gen = front_gen(t + 1, ctxs[t + 1])
                next(gen)
            wgen = wout_gen(ctxs.pop(t - 1)) if t >= 1 else None
            ssm(ctxs[t], gen, wgen); glu(ctxs[t])
        wout(ctxs.pop(NT - 1))
        def cmul_k(xr, xi, n, view, kr_ap, ki_ap):
            t1 = P.alloc(n * 4, "ck1"); t2 = P.alloc(n * 4, "ck2")
            b.tt(view(t1), view(xr), kr_ap, ALU.mult, [xr, Kr], [t1])
            b.tt(view(t2), view(xi), ki_ap, ALU.mult, [xi, Ki], [t2])
            b.tt(view(t1), view(t1), view(t2), ALU.subtract, [t1, t2], [t1])
            b.tt(view(t2), view(xr), ki_ap, ALU.mult, [xr, Ki], [t2])
            b.tt(view(xr), view(xi), kr_ap, ALU.mult, [xi, Kr], [xr])
            b.tt(view(xi), view(xr), view(t2), ALU.add, [xr, t2], [xi])
            b.copy(view(xr), view(t1), [t1], [xr])
            P.release(t1); P.release(t2)
        cmul_k(hfr, hfi, 16, lambda t_: f32(t_, 16), f32(Kr, 16), f32(Ki, 16))
        cmul_k(hsr, hsi, 256, h3, f32(Kr, 16).unsqueeze(2).to_broadcast([128, 16, 16]), f32(Ki, 16).unsqueeze(2).to_broadcast([128, 16, 16]))
        P.dma("sp", nre_p[l], f32(hfr, 16), reads=[hfr], is_output=True)
        P.dma("sp", nim_p[l], f32(hfi, 16), reads=[hfi], is_output=True)
        P.dma("sp", nre_s[l].rearrange("p b s -> p (b s)"), f32(hsr, 256), reads=[hsr], is_output=True)
        P.dma("sp", nim_s[l].rearrange("p b s -> p (b s)"), f32(hsi, 256), reads=[hsi], is_output=True)
        for t_ in (Win, Wout, Wglu, Wpool, CT, DD, BB, TC, TS, ec, es_, nes, injr, inji,
                   initr, initi, hfr, hfi, hsr, hsi) + tuple(UE):
            P.release(t_)

        XNblk = P.alloc(8 * NTOK * 2, "XN")
        XNA = bf(XNblk).rearrange("p (k n) -> p k n", k=8)
        FT = [(0, 512, [0, 1]), (512, 1024, [2, 3]), (1024, 1536, [4, 5]), (1536, 2048, [6, 7]), (2048, 2176, [8])]
        XNU = {}
        for (c0_, c1_, uu_) in FT:
            XNU[c0_] = T(P, XNblk.ap, f"xn_{c0_}"); XNU[c0_].inherit = list(XNblk.inherit)

        def norm_all(gi):
            for (c0, c1, uu) in FT:
                n = c1 - c0
                rmsnorm_tile([[XU[k][u] for u in uu] for k in range(8)], n, (G_, (l * 3 + gi) * 8),
                             lambda k, c0=c0, c1=c1: XNA[:, k, c0:c1], XNU[c0])
        groups = [(q * 4, 4) for q in range(5)] + [(20, 2)]
        def load_group(gi):
            ch0, nck = groups[gi]
            Wg = P.alloc(8 * 128 * nck * 2, "Wg"); Wu = P.alloc(8 * 128 * nck * 2, "Wu"); Wd = P.alloc(nck * 1024 * 2, "Wd")
            load_w(Wg, w_gu[l][:, 128 * ch0:128 * (ch0 + nck)], 8, 128 * nck)
            load_w(Wu, w_gu[l][:, DFF + 128 * ch0:DFF + 128 * (ch0 + nck)], 8, 128 * nck)
            load_w(Wd, w_dn[l][128 * ch0:128 * (ch0 + nck), :], nck, 1024)
            return (Wg, Wu, Wd)

        def load_ple():
            Wpg = P.alloc(16384, "Wpg"); load_w(Wpg, w_pg[l], 8, 1024)
            Wpl = P.alloc(4096, "Wpl"); load_w(Wpl, w_ple[l], 2, 1024)
            return (Wpg, Wpl)
        GW0 = load_group(0)
        norm_all(1)
        nxt = GW0
        PW = None
        for gi, (ch0, nck) in enumerate(groups):
            Wg, Wu, Wd = nxt
            if gi + 1 < len(groups):
                nxt = load_group(gi + 1)
            else:
                PW = load_ple()
            WgA = bf(Wg).rearrange("p (k n) -> p k n", k=8); WuA = bf(Wu).rearrange("p (k n) -> p k n", k=8)
            WdA = bf(Wd).rearrange("p (k n) -> p k n", k=nck)
            def gateup(c0, c1, uu):
                n = c1 - c0
                hh = [P.alloc(n * 2, f"h{c}") for c in range(nck)]
                for c in range(nck):
                    pg = P.ps_alloc(); pu = P.ps_alloc()
                    for k in range(8):
                        b.mm(pg, WgA[:, k, 128 * c:128 * c + 128], XNA[:, k, c0:c1], k == 0, k == 7, [Wg, XNU[c0]], n)
                    for k in range(8):
                        b.mm(pu, WuA[:, k, 128 * c:128 * c + 128], XNA[:, k, c0:c1], k == 0, k == 7, [Wu, XNU[c0]], n)
                    sl = P.alloc(n * 4, "silu")
                    b.act(f32(sl, n), pg.ap[:, 0:n], AF.Silu, [pg], [sl])
                    b.tt(bf(hh[c], n), pu.ap[:, 0:n], f32(sl, n), ALU.mult, [pu, sl], [hh[c]])
                    P.ps_release(pg); P.ps_release(pu); P.release(sl)
                return (n, uu, hh)

            def down(ctx):
                n, uu, hh = ctx
                for m in range(8):
                    ps = P.ps_alloc()
                    for c in range(nck):
                        b.mm(ps, WdA[:, c, 128 * m:128 * m + 128], bf(hh[c], n), c == 0, c == nck - 1, [Wd, hh[c]], n)
                    xa = P_ap_join([XU[m][u] for u in uu])
                    b.tt(xa, ps.ap[:, 0:n], xa, ALU.add, [ps] + [XU[m][u] for u in uu], [XU[m][u] for u in uu])
                    P.ps_release(ps)
                for c in range(nck):
                    P.release(hh[c])
            prev = None
            for (c0, c1, uu) in FT:
                cur = gateup(c0, c1, uu)
                if prev is not None:
                    down(prev)
                prev = cur
            down(prev)
            P.release(Wg); P.release(Wu); P.release(Wd)

        Wpg, Wpl = PW
        if l + 1 < depth_run:
            MW = load_mixer_weights(l + 1)
        WpgA = bf(Wpg).rearrange("p (k n) -> p k n", k=8); WplA = bf(Wpl).rearrange("p (k n) -> p k n", k=2)
        def norm_gen(fi, gi):
            c0, c1, uu = FT[fi]; n = c1 - c0
            xin = [P_ap_join([XU[k][u] for u in uu]) for k in range(8)]
            ps = P.ps_alloc()
            for k in range(8):
                sq = P.alloc(n * 2, "sq")
                b.act(bf(sq, n), xin[k], AF.Square, [XU[k][u] for u in uu], [sq])
                b.mm(ps, ONES, bf(sq, n), k == 0, k == 7, [cst, sq], n)
                P.release(sq)
            rs = P.alloc(n * 4, "rstd")
            b.act(f32(rs, n), ps.ap[:, 0:n], AF.Sqrt, [ps], [rs], scale=1.0 / D, bias=EPS)
            P.ps_release(ps)
            yield
            b.recip(f32(rs, n), f32(rs, n), [rs], [rs])
            g0_ = (l * 3 + gi) * 8
            for k in range(8):
                b.stt(XNA[:, k, c0:c1], xin[k], G_[:, g0_ + k:g0_ + k + 1], f32(rs, n), ALU.mult, ALU.mult,
                      [XU[k][u] for u in uu] + [rs, cst], [XNU[c0]])
            P.release(rs)
            yield
        for _ in norm_gen(0, 2):
            pass
        for fi, (c0, c1, uu) in enumerate(FT):
            ngen = norm_gen(fi + 1, 2) if fi + 1 < len(FT) else None
            n = c1 - c0
            pb = P.alloc(2 * n * 2, "pTb")
            pbA = bf(pb, 2 * n).rearrange("p (k n) -> p k n", k=2)
            P.dma("pool", pbA, pT[l, :, :, c0:c1], writes=[pb])
            for m in range(8):
                ps = P.ps_alloc(); ps2 = P.ps_alloc()
                for k in range(8):
                    b.mm(ps, WpgA[:, k, 128 * m:128 * m + 128], XNA[:, k, c0:c1], k == 0, k == 7, [Wpg, XNU[c0]], n)
                for k in range(2):
                    b.mm(ps2, WplA[:, k, 128 * m:128 * m + 128], pbA[:, k, :], k == 0, k == 1, [Wpl, pb], n)
                sg = P.alloc(n * 4, "psg")
                b.act(f32(sg, n), ps.ap[:, 0:n], AF.Sigmoid, [ps], [sg])
                b.tt(f32(sg, n), ps2.ap[:, 0:n], f32(sg, n), ALU.mult, [ps2, sg], [sg])
                xa = P_ap_join([XU[m][u] for u in uu])
                b.tt(xa, xa, f32(sg, n), ALU.add, [sg] + [XU[m][u] for u in uu], [XU[m][u] for u in uu])
                P.ps_release(ps); P.ps_release(ps2); P.release(sg)
                if ngen is not None and m in (2, 6):
                    next(ngen, None)
            if ngen is not None:
                for _ in ngen:
                    pass
            P.release(pb)
        for u_ in XNU.values():
            XNblk.inherit = XNblk.inherit + u_.events()
        P.release(Wpg); P.release(Wpl); P.release(XNblk)

    for (c0, c1, uu) in [(0, 512, [0, 1]), (512, 1024, [2, 3]), (1024, 1536, [4, 5]), (1536, 2048, [6, 7]), (2048, 2176, [8])]:
        n = c1 - c0
        yo = [P.alloc(n * 4, f"yo{k}") for k in range(8)]
        rmsnorm_tile([[XU[k][u] for u in uu] for k in range(8)], n, (GF, 0), lambda k: f32(yo[k], n), yo)
        for k in range(8):
            P.dma("sp", yT[:, k, c0:c1], f32(yo[k], n), reads=[yo[k]], is_output=True)
            P.release(yo[k])
    P.finalize()
    return nc, P


_CACHE = {}


def _prep_shared(inp):
    f = np.float32
    sh = {}
    for nm in ("w_in", "w_out", "w_pool", "w_glu", "w_gate_up", "w_down", "w_ple", "w_ple_gate"):
        sh[nm] = np.ascontiguousarray(inp[nm], dtype=f)
    g3 = np.stack([inp["g_mix"], inp["g_ffn"], inp["g_ple"]], axis=1)
    sh["gvec"] = np.ascontiguousarray(g3.reshape(DEPTH, 3, 8, 128).transpose(3, 0, 1, 2).reshape(128, DEPTH * 3 * 8), dtype=f)
    sh["gfin"] = np.ascontiguousarray(np.asarray(inp["g_final"]).reshape(8, 128).T, dtype=f)
    sh["pscale"] = np.ascontiguousarray(np.asarray(inp["pool_scale"]).reshape(DEPTH, 4, 128).transpose(2, 0, 1).reshape(128, DEPTH * 4), dtype=f)
    sh["bglu"] = np.ascontiguousarray(np.asarray(inp["b_glu"]).reshape(DEPTH, 4, 128).transpose(2, 0, 1).reshape(128, DEPTH * 4), dtype=f)
    ic = np.zeros((128, 4, 16), f)
    for g, w in enumerate(POOLW):
        ic[:, g, :] = 1.0 / np.minimum(np.arange(16) + 1, w)
    sh["invcnt"] = ic.reshape(128, 64)
    mk = np.ones((128, 128), f); mk[:, 0::8] = 0.0
    sh["mask8"] = mk
    are = np.asarray(inp["ssm_a_re"], f).reshape(DEPTH, 2048); aim = np.asarray(inp["ssm_a_im"], f).reshape(DEPTH, 2048)
    ldt = np.repeat(np.asarray(inp["ssm_log_dt"], f), 64, axis=1)
    sh["abc"] = np.ascontiguousarray(np.stack([are, aim, ldt], axis=1))
    fm = lambda a: a.reshape(DEPTH, 16, 128).transpose(0, 2, 1)
    cat = lambda a: np.concatenate([fm(a)[l_] for l_ in range(DEPTH)], axis=1)
    sh["afm2"] = np.ascontiguousarray(np.concatenate([cat(are), cat(aim), cat(ldt)], axis=1))
    Bq = np.zeros((DEPTH, 2, 128, 2048), f)
    for ri, nm in enumerate(("ssm_b_re", "ssm_b_im")):
        Bm = np.asarray(inp[nm], f)
        for g in range(32):
            j, gg = g // 8, g % 8
            Bq[:, ri, 16 * gg:16 * gg + 16, 512 * j + 64 * gg:512 * j + 64 * gg + 64] = Bm[:, g].transpose(0, 2, 1)
    sh["Bq"] = Bq
    Cq = np.zeros((DEPTH, 2, 128, 16, 128), f)
    for ri, nm in enumerate(("ssm_c_re", "ssm_c_im")):
        Cm = np.asarray(inp[nm], f)
        for g in range(32):
            blk, gg = g // 2, g % 2
            c0 = 32 * (blk % 4) + 16 * gg
            Cq[:, ri, 64 * gg:64 * gg + 64, blk, c0:c0 + 16] = Cm[:, g].transpose(0, 2, 1)
    sh["Cq"] = Cq.reshape(DEPTH, 2, 128, 2048)
    Dq = np.zeros((DEPTH, 128, 4, 128), f)
    dd = np.asarray(inp["ssm_d"], f).reshape(DEPTH, 4, 128)
    for j in range(4):
        Dq[:, np.arange(128), j, np.arange(128)] = dd[:, j, :]
    sh["Dq"] = Dq.reshape(DEPTH, 128, 512)
    return sh


def _prep_core(inp, c):
    f = np.float32
    m = {}
    xs = np.asarray(inp["x_sample"], f)[NSS * c:NSS * c + NSS].reshape(NSS * LS, D)
    xa = np.concatenate([np.asarray(inp["x_prompt"], f)[c], xs], axis=0)
    m["xT"] = np.ascontiguousarray(xa.T.reshape(8, 128, NTOK).transpose(1, 0, 2))
    ps = np.asarray(inp["p_sample"], f)[:, NSS * c:NSS * c + NSS].reshape(DEPTH, NSS * LS, PLE)
    pa = np.concatenate([np.asarray(inp["p_prompt"], f)[:, c], ps], axis=1)
    m["pT"] = np.ascontiguousarray(pa.transpose(0, 2, 1).reshape(DEPTH, 2, 128, NTOK).transpose(0, 2, 1, 3))
    sp = np.asarray(inp["state_pool"], f)[:, NSS * c:NSS * c + NSS]
    m["spool_tm"] = np.ascontiguousarray(sp)
    m["poolbuf"] = np.ascontiguousarray(sp.reshape(DEPTH, NSS, 15, 4, 128).transpose(0, 4, 3, 1, 2))
    for nm, key in (("h0re", "state_ssm_re"), ("h0im", "state_ssm_im")):
        h = np.asarray(inp[key], f)[:, NSS * c:NSS * c + NSS].reshape(DEPTH, NSS, 16, 128)
        m[nm] = np.ascontiguousarray(h.transpose(0, 3, 2, 1))
    return m


def kernel(**inputs):
    from concourse.bass_utils import run_bass_kernel_spmd
    import os
    dr = int(os.environ.get("KDEPTH", DEPTH))
    if ("prog", dr) not in _CACHE:
        _CACHE[("prog", dr)] = build_program(dr)
    nc, _ = _CACHE[("prog", dr)]
    sh = _prep_shared(inputs)
    in_maps = []
    for c in range(NCORES):
        m = dict(sh); m.update(_prep_core(inputs, c)); in_maps.append(m)
    res = run_bass_kernel_spmd(nc, in_maps, core_ids=list(range(NCORES)))
    R = res.results
    f = np.float32
    y_p = np.zeros((NCORES, SEQ, D), f); y_s = np.zeros((NCORES * NSS, LS, D), f)
    pool_p = np.zeros((DEPTH, NCORES, 15, 512), f); pool_s = np.zeros((DEPTH, NCORES * NSS, 15, 512), f)
    re_p = np.zeros((DEPTH, NCORES, 32, 64), f); im_p = np.zeros_like(re_p)
    re_s = np.zeros((DEPTH, NCORES * NSS, 32, 64), f); im_s = np.zeros_like(re_s)
    for c in range(NCORES):
        r = R[c]
        ya = np.asarray(r["yT"]).transpose(1, 0, 2).reshape(D, NTOK).T
        y_p[c] = ya[:SEQ]; y_s[NSS * c:NSS * c + NSS] = ya[SEQ:].reshape(NSS, LS, D)
        pool_p[:, c] = np.asarray(r["npool_p"]); pool_s[:, NSS * c:NSS * c + NSS] = np.asarray(r["npool_s"])
        re_p[:, c] = np.asarray(r["nre_p"]).transpose(0, 2, 1).reshape(DEPTH, 32, 64)
        im_p[:, c] = np.asarray(r["nim_p"]).transpose(0, 2, 1).reshape(DEPTH, 32, 64)
        re_s[:, NSS * c:NSS * c + NSS] = np.asarray(r["nre_s"]).transpose(0, 3, 2, 1).reshape(DEPTH, NSS, 32, 64)
        im_s[:, NSS * c:NSS * c + NSS] = np.asarray(r["nim_s"]).transpose(0, 3, 2, 1).reshape(DEPTH, NSS, 32, 64)
    return (y_p, y_s, pool_p, re_p, im_p, pool_s, re_s, im_s)
```

```python
import numpy as np
import concourse.bass as bass
import concourse.mybir as mybir
from contextlib import ExitStack

F32 = mybir.dt.float32
BF16 = mybir.dt.bfloat16
AF = mybir.ActivationFunctionType
ALU = mybir.AluOpType

ENGS = ("pe", "act", "dve", "pool", "sp")
EPOCH = 8000
NDMASEM = 20


class Instr:
    __slots__ = ("eng", "fn", "deps", "flag", "sem", "val", "is_dma", "dsem", "dval", "idx_")

    def __init__(self, eng, fn, deps):
        self.eng = eng; self.fn = fn; self.deps = deps
        self.flag = False; self.sem = None; self.val = 0
        self.is_dma = False; self.dsem = None; self.dval = 0


class DmaEv:
    __slots__ = ("sem", "val")

    def __init__(self, sem, val):
        self.sem = sem; self.val = val


class T:
    def __init__(self, prog, ap, name="", off=None, nbytes=None):
        self.prog = prog; self.ap = ap; self.name = name
        self.last_w = None
        self.readers = {}
        self.dreaders = []
        self.inherit = []
        self.off = off; self.nbytes = nbytes

    def events(self):
        ev = list(self.readers.values()) + list(self.dreaders) + list(self.inherit)
        if self.last_w is not None:
            ev.append(self.last_w)
        return ev


class Prog:
    def __init__(self, nc, arena_words):
        self.nc = nc
        self.es = ExitStack()
        self.streams = {e: [] for e in ENGS}
        self.arena = self.es.enter_context(nc.sbuf_tensor("arena", [128, arena_words], F32))
        self.arena_bytes = arena_words * 4
        self.free = [[0, self.arena_bytes, []]]
        self.psum = []
        for b in range(8):
            t = self.es.enter_context(nc.psum_tensor(f"psb{b}", [128, 512], F32))
            self.psum.append(T(self, t[:, :], f"ps{b}"))
        self.ps_free = list(range(8))
        self.dma_sems = {}
        self.dma_rr = {}
        self.dma_last = {}
        self.dma_cnt = {}
        self.out_events = []
        self.n_instr = 0

    def alloc(self, nbytes, name=""):
        nbytes = (nbytes + 63) // 64 * 64
        best = None
        for i, seg in enumerate(self.free):
            if seg[1] - seg[0] >= nbytes:
                st = seg[3] if len(seg) > 3 else -1
                if best is None or st < best[0]:
                    best = (st, i)
        if best is not None:
            i = best[1]; seg = self.free[i]; lo, hi, ev = seg[0], seg[1], seg[2]
            st = seg[3] if len(seg) > 3 else -1
            if hi - lo == nbytes:
                self.free.pop(i)
            else:
                self.free[i] = [lo + nbytes, hi, ev, st]
            t = T(self, self.arena[:, lo // 4:(lo + nbytes) // 4], name, lo, nbytes)
            t.inherit = list(ev)
            return t
        raise MemoryError(f"arena full allocating {nbytes} for {name}; free={[(q[0], q[1]) for q in self.free]}")

    def release(self, t):
        ev = self._dedupe(t.events())
        self.free.append([t.off, t.off + t.nbytes, ev, self.n_instr])
        self.free.sort(key=lambda s: s[0])
        merged = []
        for seg in self.free:
            if len(seg) < 4:
                seg.append(-1)
            if merged and merged[-1][1] == seg[0]:
                merged[-1][1] = seg[1]
                merged[-1][2] = self._dedupe(merged[-1][2] + seg[2])
                merged[-1][3] = max(merged[-1][3], seg[3])
            else:
                merged.append(seg)
        self.free = merged

    @staticmethod
    def _dedupe(evs):
        best = {}; out = []
        for e in evs:
            if isinstance(e, Instr) and not e.is_dma:
                k = e.eng
                if k not in best or best[k].idx_ < e.idx_:
                    best[k] = e
            else:
                out.append(e)
        seen = set(); res = []
        for e in out:
            if id(e) not in seen:
                seen.add(id(e)); res.append(e)
        return list(best.values()) + res

    def ps_alloc(self):
        assert self.ps_free, "out of PSUM banks"
        return self.psum[self.ps_free.pop(0)]

    def ps_release(self, t):
        self.ps_free.append(self.psum.index(t))

    def _deps(self, reads, writes):
        deps = []
        for t in reads:
            if t.last_w is not None:
                deps.append(t.last_w)
            deps.extend(t.inherit)
        for t in writes:
            deps.extend(t.events())
        return deps

    def op(self, eng, fn, reads=(), writes=()):
        ins = Instr(eng, fn, self._deps(reads, writes))
        ins.idx_ = self.n_instr; self.n_instr += 1
        self.streams[eng].append(ins)
        for t in reads:
            t.readers[eng] = ins
        for t in writes:
            t.last_w = ins; t.readers = {}; t.dreaders = []; t.inherit = []
        return ins

    def dma(self, queue, out_ap, in_ap, reads=(), writes=(), is_output=False, **kw):
        k = self.dma_rr.get(queue, 0)
        self.dma_rr[queue] = (k + 1) % NDMASEM
        key = (queue, k)
        if key not in self.dma_sems:
            self.dma_sems[key] = self.es.enter_context(self.nc.semaphore(f"d_{queue}{k}"))
            self.dma_cnt[key] = 0
        sem = self.dma_sems[key]
        deps = self._deps(reads, writes)
        if key in self.dma_last:
            deps.append(self.dma_last[key])
        self.dma_cnt[key] += 16
        ev = DmaEv(sem, self.dma_cnt[key])
        self.dma_last[key] = ev

        def fn(e, out_ap=out_ap, in_ap=in_ap, kw=kw):
            return e.dma_start(out=out_ap, in_=in_ap, **kw)
        ins = Instr(queue, fn, deps)
        ins.idx_ = self.n_instr; self.n_instr += 1
        ins.is_dma = True; ins.dsem = sem
        self.streams[queue].append(ins)
        for t in reads:
            t.dreaders.append(ev)
        for t in writes:
            t.last_w = ev; t.readers = {}; t.dreaders = []; t.inherit = []
        if is_output or not writes:
            self.out_events.append(ev)
        return ev

    def finalize(self):
        nc = self.nc
        for eng in ENGS:
            for ins in self.streams[eng]:
                for d in ins.deps:
                    if isinstance(d, Instr):
                        if d.eng == "pe" and ins.eng == "pe" and not ins.is_dma:
                            continue
                        d.flag = True
        sems = {}
        for eng in ENGS:
            c = 0
            for ins in self.streams[eng]:
                if ins.flag:
                    ep = c // EPOCH
                    if (eng, ep) not in sems:
                        sems[(eng, ep)] = self.es.enter_context(nc.semaphore(f"c_{eng}{ep}"))
                    ins.sem = sems[(eng, ep)]; ins.val = c % EPOCH + 1
                    c += 1
        out_events = self.out_events
        streams = self.streams
        engmap = {"pe": "tensor", "act": "scalar", "dve": "vector", "pool": "gpsimd", "sp": "sync"}
        blk = self.es.enter_context(nc.Block())
        stats = {}

        def make(eng):
            def body(e):
                waited = {}
                nw = 0
                for ins in streams[eng]:
                    need = {}
                    for d in ins.deps:
                        if isinstance(d, Instr):
                            if d.sem is None:
                                continue
                            s, v = d.sem, d.val
                        else:
                            s, v = d.sem, d.val
                        if need.get(id(s), (None, 0))[1] < v:
                            need[id(s)] = (s, v)
                    for sid, (s, v) in need.items():
                        if waited.get(sid, 0) < v:
                            e.wait_ge(s, v); waited[sid] = v; nw += 1
                    bi = ins.fn(e)
                    if ins.is_dma:
                        bi.then_inc(ins.dsem, 16)
                    elif ins.flag:
                        bi.then_inc(ins.sem, 1)
                if eng == "sp":
                    fin = {}
                    for ev in out_events:
                        if fin.get(id(ev.sem), (None, 0))[1] < ev.val:
                            fin[id(ev.sem)] = (ev.sem, ev.val)
                    for sid, (s, v) in fin.items():
                        if waited.get(sid, 0) < v:
                            e.wait_ge(s, v)
                stats[eng] = (len(streams[eng]), nw)
            return body
        for eng in ENGS:
            if streams[eng] or eng == "sp":
                getattr(blk, engmap[eng])(make(eng))
        self.stats = stats
        self.es.close()


NCORES = 8
D = 1024; DEPTH = 4; SEQ = 2048; NSS = 16; LS = 8
NTOK = SEQ + NSS * LS
DFF = 2816; PLE = 256
TT = 256
NT = 9
EPS = 1e-6
POOLW = (2, 4, 8, 16)


def tile_cols(t):
    return (256 * t, 256 * t + 256) if t < 8 else (2048, 2176)


class B:
    def __init__(self, P):
        self.P = P

    def mm(self, ps, lhsT, rhs, start, stop, reads, n):
        self.P.op("pe", lambda e: e.matmul(ps.ap[:, 0:n], lhsT=lhsT, rhs=rhs, start=start, stop=stop), reads, [ps])

    def act(self, out, in_, func, reads, writes, **kw):
        self.P.op("act", lambda e: e.activation(out=out, in_=in_, func=func, **kw), reads, writes)

    def tt(self, out, in0, in1, op, reads, writes, eng="dve"):
        self.P.op(eng, lambda e: e.tensor_tensor(out=out, in0=in0, in1=in1, op=op), reads, writes)

    def ts(self, out, in0, s1, s2, op0, op1, reads, writes, eng="dve"):
        if op1 is None:
            self.P.op(eng, lambda e: e.tensor_scalar(out=out, in0=in0, scalar1=s1, scalar2=None, op0=op0), reads, writes)
        else:
            self.P.op(eng, lambda e: e.tensor_scalar(out=out, in0=in0, scalar1=s1, scalar2=s2, op0=op0, op1=op1), reads, writes)

    def stt(self, out, in0, scalar, in1, op0, op1, reads, writes, eng="dve"):
        self.P.op(eng, lambda e: e.scalar_tensor_tensor(out=out, in0=in0, scalar=scalar, in1=in1, op0=op0, op1=op1), reads, writes)

    def scan(self, out, d0, d1, init, reads, writes):
        self.P.op("dve", lambda e: e.tensor_tensor_scan(out=out, data0=d0, data1=d1, initial=init, op0=ALU.mult, op1=ALU.add), reads, writes)

    def copy(self, out, in_, reads, writes, eng="dve"):
        self.P.op(eng, lambda e: e.tensor_copy(out=out, in_=in_), reads, writes)

    def memset(self, out, val, writes, eng="dve"):
        self.P.op(eng, lambda e: e.memset(out, val), [], writes)

    def recip(self, out, in_, reads, writes):
        self.P.op("dve", lambda e: e.reciprocal(out=out, in_=in_), reads, writes)


def build_program(depth_run=DEPTH):
    nc = bass.Bass("TRN2", target_bir_lowering=False)

    def din(name, shape):
        return nc.dram_tensor(name, list(shape), F32, kind="ExternalInput").ap()

    def dout(name, shape):
        return nc.dram_tensor(name, list(shape), F32, kind="ExternalOutput").ap()

    xT = din("xT", [128, 8, NTOK])
    pT = din("pT", [DEPTH, 128, 2, NTOK])
    poolbuf = din("poolbuf", [DEPTH, 128, 4, NSS, 15])
    spool_tm = din("spool_tm", [DEPTH, NSS, 15, 512])
    h0re = din("h0re", [DEPTH, 128, 16, NSS]); h0im = din("h0im", [DEPTH, 128, 16, NSS])
    gvec = din("gvec", [128, DEPTH * 3 * 8]); gfin = din("gfin", [128, 8])
    pscale = din("pscale", [128, DEPTH * 4]); bglu = din("bglu", [128, DEPTH * 4])
    invcnt = din("invcnt", [128, 64]); mask8 = din("mask8", [128, 128])
    afm2 = din("afm2", [128, 192])
    abc = din("abc", [DEPTH, 3, 2048])
    Bq = din("Bq", [DEPTH, 2, 128, 2048])
    Cq = din("Cq", [DEPTH, 2, 128, 2048])
    Dq = din("Dq", [DEPTH, 128, 512])
    w_in = din("w_in", [DEPTH, D, D]); w_out = din("w_out", [DEPTH, D, D])
    w_pool = din("w_pool", [DEPTH, 4, 128, 128]); w_glu = din("w_glu", [DEPTH, 512, 512])
    w_gu = din("w_gate_up", [DEPTH, D, 2 * DFF]); w_dn = din("w_down", [DEPTH, DFF, D])
    w_ple = din("w_ple", [DEPTH, PLE, D]); w_pg = din("w_ple_gate", [DEPTH, D, D])

    yT = dout("yT", [128, 8, NTOK])
    npool_p = dout("npool_p", [DEPTH, 15, 512])
    npool_s = dout("npool_s", [DEPTH, NSS, 15, 512])
    nre_p = dout("nre_p", [DEPTH, 128, 16]); nim_p = dout("nim_p", [DEPTH, 128, 16])
    nre_s = dout("nre_s", [DEPTH, 128, 16, NSS]); nim_s = dout("nim_s", [DEPTH, 128, 16, NSS])

    P = Prog(nc, 212000 // 4)
    b = B(P)

    def f32(t, n=None, off=0):
        a = t.ap
        return a[:, off:off + n] if n is not None else a

    def bf(t, n=None, off=0):
        a = t.ap.bitcast(BF16)
        return a[:, off:off + n] if n is not None else a

    Xblk = P.alloc(8 * NTOK * 4, "X")
    XU = [[T(P, Xblk.ap[:, k * NTOK + tile_cols(t)[0]: k * NTOK + tile_cols(t)[1]], f"x{k}_{t}") for t in range(NT)] for k in range(8)]
    cst = P.alloc(4096, "consts")
    ca = cst.ap
    G_ = ca[:, 0:96]; GF = ca[:, 96:104]; PSC = ca[:, 104:120]; BGL = ca[:, 120:136]
    INVC = ca[:, 136:200]; MASK = ca[:, 200:328]
    ONES = ca[:, 328:392].bitcast(BF16)
    onesf = P.alloc(512, "onesf")
    P.dma("sp", G_, gvec, writes=[cst]); P.dma("sp", GF, gfin, writes=[cst])
    P.dma("sp", PSC, pscale, writes=[cst]); P.dma("sp", BGL, bglu, writes=[cst])
    P.dma("sp", INVC, invcnt, writes=[cst]); P.dma("sp", MASK, mask8, writes=[cst])
    b.memset(onesf.ap[:, 0:128], 1.0, [onesf])
    b.copy(ONES, onesf.ap[:, 0:128], [onesf], [cst])
    for t in range(NT):
        c0, c1 = tile_cols(t)
        for k in range(8):
            P.dma("sp", XU[k][t].ap, xT[:, k, c0:c1], writes=[XU[k][t]])

    def load_w(dst_t, src_ap, nk, ncols, eltoff=0):
        dst = bf(dst_t)[:, eltoff:eltoff + nk * ncols].rearrange("p (k n) -> p k n", k=nk)
        src = src_ap.rearrange("(k p) n -> p k n", p=128)
        half = max(1, nk // 2)
        for k0 in range(0, nk, half):
            k1 = min(nk, k0 + half)
            P.dma("pool", dst[:, k0:k1, :], src[:, k0:k1, :], writes=[dst_t])

    def rmsnorm_tile(units, n, gcol0, out_fn, out_tiles, f32out=False):
        gsrc, g0 = gcol0
        xin = [P_ap_join(units[k]) for k in range(8)]
        ps = P.ps_alloc()
        for k in range(8):
            sq = P.alloc(n * 2, "sq")
            b.act(bf(sq, n), xin[k], AF.Square, units[k], [sq])
            b.mm(ps, ONES, bf(sq, n), k == 0, k == 7, [cst, sq], n)
            P.release(sq)
        rs = P.alloc(n * 4, "rstd")
        b.act(f32(rs, n), ps.ap[:, 0:n], AF.Sqrt, [ps], [rs], scale=1.0 / D, bias=EPS)
        P.ps_release(ps)
        b.recip(f32(rs, n), f32(rs, n), [rs], [rs])
        for k in range(8):
            b.stt(out_fn(k), xin[k], gsrc[:, g0 + k:g0 + k + 1], f32(rs, n), ALU.mult, ALU.mult,
                  units[k] + [rs, cst], [out_tiles[k]] if isinstance(out_tiles, list) else [out_tiles])
        P.release(rs)

    def P_ap_join(us):
        if len(us) == 1:
            return us[0].ap
        a0 = us[0].ap; n = sum(u.ap.shape[1] for u in us)
        return us[0].wide(n)

    def wide(self, n):
        return self.base[:, self.c0:self.c0 + n]
    T.wide = wide
    for k in range(8):
        for t in range(NT):
            XU[k][t].base = Xblk.ap; XU[k][t].c0 = k * NTOK + tile_cols(t)[0]

    def load_mixer_weights(l):
        Win = P.alloc(16384, "Win"); load_w(Win, w_in[l], 8, 1024)
        Wout = P.alloc(16384, "Wout"); load_w(Wout, w_out[l], 8, 1024)
        Wglu = P.alloc(4096, "Wglu"); load_w(Wglu, w_glu[l], 4, 512)
        Wpool = P.alloc(1024, "Wpool")
        P.dma("pool", bf(Wpool, 512).rearrange("p (g d) -> p g d", g=4), w_pool[l].rearrange("g c d -> c g d"), writes=[Wpool])
        DD = P.alloc(1024, "DD")
        P.dma("pool", bf(DD, 512), Dq[l], writes=[DD])
        BB = P.alloc(8192, "Braw")
        for ri in range(2):
            P.dma("pool", bf(BB)[:, ri * 2048:(ri + 1) * 2048], Bq[l, ri], writes=[BB])
        return (Win, Wout, Wglu, Wpool, None, DD, BB)

    for l in range(depth_run):
        if l == 0:
            MW = load_mixer_weights(0)
        Win, Wout, Wglu, Wpool, CTs, DD, BB = MW
        WinA = bf(Win).rearrange("p (k n) -> p k n", k=8); WoutA = bf(Wout).rearrange("p (k n) -> p k n", k=8)
        WgluA = bf(Wglu).rearrange("p (k n) -> p k n", k=4); WpoolA = bf(Wpool, 512).rearrange("p (g d) -> p g d", g=4)
        DDA = bf(DD, 512).rearrange("p (j n) -> p j n", j=4)
        BBA = bf(BB).rearrange("p (r j n) -> p j r n", r=2, j=4)

        def lam_math(are, aim, ldt, n, tiles_r, pfx):
            o = {}
            def new(nm):
                o[nm] = P.alloc(n * 4, pfx + nm); return o[nm]
            A = lambda t_: f32(t_, n)
            z = new("z"); b.ts(A(z), ldt, 0.125, None, ALU.mult, None, tiles_r, [z])
            dt = new("dt")
            b.ts(A(dt), A(z), 1.0 / 11, 1.0, ALU.mult, ALU.add, [z], [dt])
            for kk in range(10, 0, -1):
                b.tt(A(dt), A(dt), A(z), ALU.mult, [dt, z], [dt])
                b.ts(A(dt), A(dt), 1.0 / kk, 1.0, ALU.mult, ALU.add, [dt], [dt])
            for _ in range(3):
                b.tt(A(dt), A(dt), A(dt), ALU.mult, [dt], [dt])
            xx = new("xx"); b.tt(A(xx), A(dt), are, ALU.mult, [dt] + tiles_r, [xx])
            em1 = new("em1")
            b.ts(A(em1), A(xx), 1.0 / 6, 1.0, ALU.mult, ALU.add, [xx], [em1])
            for kk in (5, 4, 3, 2):
                b.tt(A(em1), A(em1), A(xx), ALU.mult, [em1, xx], [em1])
                b.ts(A(em1), A(em1), 1.0 / kk, 1.0, ALU.mult, ALU.add, [em1], [em1])
            b.tt(A(em1), A(em1), A(xx), ALU.mult, [em1, xx], [em1])
            r = new("r"); b.ts(A(r), A(em1), 1.0, None, ALU.add, None, [em1], [r])
            ang = new("ang"); b.tt(A(ang), A(dt), aim, ALU.mult, [dt] + tiles_r, [ang])
            s = new("s"); c = new("c"); tmp = new("tmp"); sh = new("sh")
            b.act(A(s), A(ang), AF.Sin, [ang], [s], scale=0.125)
            b.act(A(tmp), A(ang), AF.Sin, [ang], [tmp], scale=0.0625)
            b.tt(A(tmp), A(tmp), A(tmp), ALU.mult, [tmp], [tmp])
            b.ts(A(c), A(tmp), -2.0, 1.0, ALU.mult, ALU.add, [tmp], [c])
            for it in range(3):
                if it == 2:
                    b.copy(A(sh), A(s), [s], [sh])
                b.tt(A(tmp), A(s), A(s), ALU.mult, [s], [tmp])
                b.tt(A(s), A(s), A(c), ALU.mult, [s, c], [s])
                b.ts(A(s), A(s), 2.0, None, ALU.mult, None, [s], [s])
                b.tt(A(c), A(c), A(c), ALU.mult, [c], [c])
                b.tt(A(c), A(c), A(tmp), ALU.subtract, [c, tmp], [c])
            for nm in ("z", "xx", "ang"):
                P.release(o.pop(nm))
            o["tmp"] = tmp; o["sh"] = sh
            return o

        def k_math(o, are, aim, n, tiles_r):
            A = lambda t_: f32(t_, n)
            nr = P.alloc(n * 4, "nr"); li = P.alloc(n * 4, "li"); den = P.alloc(n * 4, "den")
            tmp = o["tmp"]
            b.tt(A(tmp), A(o["sh"]), A(o["sh"]), ALU.mult, [o["sh"]], [tmp])
            b.tt(A(nr), A(o["em1"]), A(o["c"]), ALU.mult, [o["em1"], o["c"]], [nr])
            b.stt(A(nr), A(tmp), -2.0, A(nr), ALU.mult, ALU.add, [tmp, nr], [nr])
            b.tt(A(li), A(o["r"]), A(o["s"]), ALU.mult, [o["r"], o["s"]], [li])
            b.tt(A(den), are, are, ALU.mult, tiles_r, [den])
            b.tt(A(tmp), aim, aim, ALU.mult, tiles_r, [tmp])
            b.tt(A(den), A(den), A(tmp), ALU.add, [den, tmp], [den])
            b.recip(A(den), A(den), [den], [den])
            kre = P.alloc(n * 4, "kre"); kim = P.alloc(n * 4, "kim")
            b.tt(A(kre), A(nr), are, ALU.mult, [nr] + tiles_r, [kre])
            b.tt(A(tmp), A(li), aim, ALU.mult, [li] + tiles_r, [tmp])
            b.tt(A(kre), A(kre), A(tmp), ALU.add, [kre, tmp], [kre])
            b.tt(A(kre), A(kre), A(den), ALU.mult, [kre, den], [kre])
            b.tt(A(kim), A(li), are, ALU.mult, [li] + tiles_r, [kim])
            b.tt(A(tmp), A(nr), aim, ALU.mult, [nr] + tiles_r, [tmp])
            b.tt(A(kim), A(kim), A(tmp), ALU.subtract, [kim, tmp], [kim])
            b.tt(A(kim), A(kim), A(den), ALU.mult, [kim, den], [kim])
            for t_ in (nr, li, den):
                P.release(t_)
            o["kre"] = kre; o["kim"] = kim

        if l == 0:
            pfA = P.alloc(768, "afmA"); P.dma("sp", f32(pfA, 192), afm2, writes=[pfA])
            foA = lam_math(f32(pfA, 64, 0), f32(pfA, 64, 64), f32(pfA, 64, 128), 64, [pfA], "fmA_")
            k_math(foA, f32(pfA, 64, 0), f32(pfA, 64, 64), 64, [pfA])
            lamrA = P.alloc(256, "lamrA"); lamiA = P.alloc(256, "lamiA")
            kt = P.alloc(256, "kt"); ikr = P.alloc(256, "ikr"); iki = P.alloc(256, "iki"); lr0 = P.alloc(256, "lr0"); li0 = P.alloc(256, "li0")
            F = lambda t_: f32(t_, 64)
            KrA, KiA, RtA, CtA, StA = foA["kre"], foA["kim"], foA["r"], foA["c"], foA["s"]
            b.tt(F(kt), F(KrA), F(KrA), ALU.mult, [KrA], [kt])
            b.tt(F(ikr), F(KiA), F(KiA), ALU.mult, [KiA], [ikr])
            b.tt(F(kt), F(kt), F(ikr), ALU.add, [kt, ikr], [kt])
            b.recip(F(kt), F(kt), [kt], [kt])
            b.tt(F(ikr), F(KrA), F(kt), ALU.mult, [KrA, kt], [ikr])
            b.stt(F(iki), F(KiA), -1.0, F(kt), ALU.mult, ALU.mult, [KiA, kt], [iki])
            b.tt(F(lr0), F(RtA), F(CtA), ALU.mult, [RtA, CtA], [lr0])
            b.tt(F(li0), F(RtA), F(StA), ALU.mult, [RtA, StA], [li0])
            b.tt(F(lamrA), F(lr0), F(ikr), ALU.mult, [lr0, ikr], [lamrA])
            b.tt(F(kt), F(li0), F(iki), ALU.mult, [li0, iki], [kt])
            b.tt(F(lamrA), F(lamrA), F(kt), ALU.subtract, [lamrA, kt], [lamrA])
            b.tt(F(lamiA), F(lr0), F(iki), ALU.mult, [lr0, iki], [lamiA])
            b.tt(F(kt), F(li0), F(ikr), ALU.mult, [li0, ikr], [kt])
            b.tt(F(lamiA), F(lamiA), F(kt), ALU.add, [lamiA, kt], [lamiA])
            for t_ in (kt, ikr, iki, lr0, li0, pfA, foA["dt"], foA["em1"], foA["tmp"], foA["sh"]):
                P.release(t_)
            GLB = dict(r=RtA, c=CtA, s=StA, kre=KrA, kim=KiA, lamr=lamrA, lami=lamiA)

        def lview(base):
            v = T(P, base.ap[:, 16 * l:16 * l + 16], base.name + f"_l{l}"); v.last_w = base.last_w
            return v
        Rt, Ct, St, Kr, Ki, lamr, lami = (lview(GLB[x_]) for x_ in ("r", "c", "s", "kre", "kim", "lamr", "lami"))
        CT = P.alloc(12288, "CT")
        CTA = bf(CT).rearrange("p (r b n) -> p r b n", r=3, b=16)
        for hb in range(2):
            cs_ = P.alloc(8192, "cstage"); c1 = P.alloc(4096, "c1"); c2 = P.alloc(4096, "c2")
            for ri in range(2):
                P.dma("sp", f32(cs_, 1024, 1024 * ri), Cq[l, ri, :, 1024 * hb:1024 * hb + 1024], writes=[cs_])
            cre = f32(cs_, 1024, 0).rearrange("p (b n) -> p b n", b=8); cim = f32(cs_, 1024, 1024).rearrange("p (b n) -> p b n", b=8)
            krb = f32(Kr, 16)[:, 8 * hb:8 * hb + 8].unsqueeze(2).to_broadcast([128, 8, 128])
            kib = f32(Ki, 16)[:, 8 * hb:8 * hb + 8].unsqueeze(2).to_broadcast([128, 8, 128])
            v1 = f32(c1, 1024).rearrange("p (b n) -> p b n", b=8); v2 = f32(c2, 1024).rearrange("p (b n) -> p b n", b=8)
            b.tt(v1, cre, krb, ALU.mult, [cs_, Kr], [c1]); b.tt(v2, cim, kib, ALU.mult, [cs_, Ki], [c2])
            b.tt(CTA[:, 0, 8 * hb:8 * hb + 8, :], v1, v2, ALU.subtract, [c1, c2], [CT])
            b.tt(CTA[:, 1, 8 * hb:8 * hb + 8, :], v2, v1, ALU.subtract, [c1, c2], [CT])
            b.tt(v1, cre, kib, ALU.mult, [cs_, Ki], [c1]); b.tt(v2, cim, krb, ALU.mult, [cs_, Kr], [c2])
            b.stt(CTA[:, 2, 8 * hb:8 * hb + 8, :], v1, -1.0, v2, ALU.mult, ALU.subtract, [c1, c2], [CT])
            for t_ in (cs_, c1, c2):
                P.release(t_)
        TC = P.alloc(16384, "TC"); TS = P.alloc(16384, "TS")
        TCA = f32(TC).rearrange("p (b n) -> p b n", b=16); TSA = f32(TS).rearrange("p (b n) -> p b n", b=16)
        b.memset(TCA[:, :, 0:1], 1.0, [TC]); b.memset(TSA[:, :, 0:1], 0.0, [TS])
        ec = P.alloc(64, "ec"); es_ = P.alloc(64, "es"); et = P.alloc(64, "et")
        b.copy(f32(ec, 16), f32(Ct, 16), [Ct], [ec]); b.copy(f32(es_, 16), f32(St, 16), [St], [es_])
        tq = P.alloc(16 * 128 * 4, "tq")
        kk = 1
        while kk < 256:
            ecb = f32(ec, 16).unsqueeze(2).to_broadcast([128, 16, kk]); esb = f32(es_, 16).unsqueeze(2).to_broadcast([128, 16, kk])
            tqa = f32(tq, 16 * kk).rearrange("p (b n) -> p b n", b=16)
            b.tt(TCA[:, :, kk:2 * kk], TCA[:, :, 0:kk], ecb, ALU.mult, [TC, ec], [TC])
            b.tt(tqa, TSA[:, :, 0:kk], esb, ALU.mult, [TS, es_], [tq])
            b.tt(TCA[:, :, kk:2 * kk], TCA[:, :, kk:2 * kk], tqa, ALU.subtract, [TC, tq], [TC])
            b.tt(TSA[:, :, kk:2 * kk], TSA[:, :, 0:kk], ecb, ALU.mult, [TS, ec], [TS])
            b.tt(tqa, TCA[:, :, 0:kk], esb, ALU.mult, [TC, es_], [tq])
            b.tt(TSA[:, :, kk:2 * kk], TSA[:, :, kk:2 * kk], tqa, ALU.add, [TS, tq], [TS])
            b.tt(f32(et, 16), f32(es_, 16), f32(es_, 16), ALU.mult, [es_], [et])
            b.tt(f32(es_, 16), f32(es_, 16), f32(ec, 16), ALU.mult, [es_, ec], [es_])
            b.ts(f32(es_, 16), f32(es_, 16), 2.0, None, ALU.mult, None, [es_], [es_])
            b.tt(f32(ec, 16), f32(ec, 16), f32(ec, 16), ALU.mult, [ec], [ec])
            b.tt(f32(ec, 16), f32(ec, 16), f32(et, 16), ALU.subtract, [ec, et], [ec])
            kk *= 2
        P.release(tq); P.release(et)
        nes = P.alloc(128, "nes")
        b.ts(f32(nes, 16), f32(es_, 16), -1.0, None, ALU.mult, None, [es_], [nes])
        b.ts(f32(nes, 32)[:, 16:32], TSA[:, :, 255], -1.0, None, ALU.mult, None, [TS], [nes])
        hr = P.alloc(1024, "h0r"); hi = P.alloc(1024, "h0i")
        P.dma("sp", f32(hr, 256), h0re[l].rearrange("p b s -> p (b s)"), writes=[hr])
        P.dma("sp", f32(hi, 256), h0im[l].rearrange("p b s -> p (b s)"), writes=[hi])
        injr = P.alloc(1024, "injr"); inji = P.alloc(1024, "inji"); itmp = P.alloc(1024, "itmp")
        h3 = lambda t_: f32(t_, 256).rearrange("p (b s) -> p b s", b=16)
        lrb = f32(lamr, 16).unsqueeze(2).to_broadcast([128, 16, 16]); lib = f32(lami, 16).unsqueeze(2).to_broadcast([128, 16, 16])
        b.tt(h3(injr), h3(hr), lrb, ALU.mult, [hr, lamr], [injr])
        b.tt(h3(itmp), h3(hi), lib, ALU.mult, [hi, lami], [itmp])
        b.tt(h3(injr), h3(injr), h3(itmp), ALU.subtract, [injr, itmp], [injr])
        b.tt(h3(inji), h3(hi), lrb, ALU.mult, [hi, lamr], [inji])
        b.tt(h3(itmp), h3(hr), lib, ALU.mult, [hr, lami], [itmp])
        b.tt(h3(inji), h3(inji), h3(itmp), ALU.add, [inji, itmp], [inji])
        for t_ in (hr, hi, itmp):
            P.release(t_)
        initr = P.alloc(64, "initr"); initi = P.alloc(64, "initi")
        b.memset(f32(initr, 16), 0.0, [initr]); b.memset(f32(initi, 16), 0.0, [initi])
        hfr = P.alloc(64, "hfr"); hfi = P.alloc(64, "hfi")
        hsr = P.alloc(1024, "hsr"); hsi = P.alloc(1024, "hsi")

        UE = [P.alloc(max(15 + TT, NSS * 23) * 4, f"uext{g}") for g in range(4)]
        for g in range(4):
            b.memset(f32(UE[g], 15), 0.0, [UE[g]])

        def front_gen(t, c):
            c0, c1 = tile_cols(t); n = c1 - c0
            sample = (t == NT - 1)
            nseq, L = (NSS, LS) if sample else (1, TT)
            c.update(t=t, n=n, sample=sample, nseq=nseq, L=L)
            ps = P.ps_alloc()
            for k in range(8):
                sq = P.alloc(n * 2, "sq")
                b.act(bf(sq, n), XU[k][t].ap, AF.Square, [XU[k][t]], [sq])
                b.mm(ps, ONES, bf(sq, n), k == 0, k == 7, [cst, sq], n)
                P.release(sq)
            rs = P.alloc(n * 4, "rstd")
            b.act(f32(rs, n), ps.ap[:, 0:n], AF.Sqrt, [ps], [rs], scale=1.0 / D, bias=EPS)
            P.ps_release(ps)
            yield
            xn = [P.alloc(n * 2, f"xn{k}") for k in range(8)]
            b.recip(f32(rs, n), f32(rs, n), [rs], [rs])
            g0_ = (l * 3 + 0) * 8
            for k in range(8):
                b.stt(bf(xn[k], n), XU[k][t].ap, G_[:, g0_ + k:g0_ + k + 1], f32(rs, n), ALU.mult, ALU.mult, [XU[k][t], rs, cst], [xn[k]])
            P.release(rs)
            yield
            if sample:
                for g in range(4):
                    P.dma("sp", f32(UE[g], NSS * 23).rearrange("p (s j) -> p s j", s=NSS)[:, :, 0:15], poolbuf[l, :, g], writes=[UE[g]])
            us = [P.alloc(n * 2, f"us{j}") for j in range(4)]
            for m in range(8):
                ps = P.ps_alloc()
                for k in range(8):
                    b.mm(ps, WinA[:, k, 128 * m:128 * m + 128], bf(xn[k], n), k == 0, k == 7, [Win, xn[k]], n)
                if m < 4:
                    dst = f32(UE[m], nseq * (15 + L)).rearrange("p (s j) -> p s j", s=nseq)[:, :, 15:15 + L]
                    b.act(dst, ps.ap[:, 0:n].rearrange("p (s j) -> p s j", s=nseq), AF.Copy, [ps], [UE[m]])
                else:
                    b.act(bf(us[m - 4], n), ps.ap[:, 0:n], AF.Copy, [ps], [us[m - 4]])
                P.ps_release(ps)
                yield
            if t == 7 or sample:
                ps = P.ps_alloc()
                m0 = n - 15 if t == 7 else 0
                mrows = n - m0
                for k in range(8):
                    o_ap = ps.ap[0:mrows, 0:512]; l_ap = bf(xn[k], n)[:, m0:n]; r_ap = WinA[:, k, 0:512]
                    P.op("pe", lambda e, o_ap=o_ap, l_ap=l_ap, r_ap=r_ap, st=(k == 0), sp_=(k == 7): e.matmul(o_ap, lhsT=l_ap, rhs=r_ap, start=st, stop=sp_), [Win, xn[k]], [ps])
                zt = P.alloc(2048, "ztm")
                b.act(f32(zt, 512)[0:mrows, :], ps.ap[0:mrows, 0:512], AF.Copy, [ps], [zt])
                P.ps_release(ps)
                if t == 7:
                    P.dma("sp", npool_p[l], f32(zt, 512)[0:15, :], reads=[zt], is_output=True)
                else:
                    for s_ in range(NSS):
                        P.dma("sp", npool_s[l, s_, 7:15, :], f32(zt, 512)[8 * s_:8 * s_ + 8, :], reads=[zt], is_output=True)
                    P.dma("sp", npool_s[l, :, 0:7, :], spool_tm[l, :, 8:15, :], is_output=True)
                P.release(zt)
            for k in range(8):
                P.release(xn[k])
            yield
            ycat = [P.alloc(n * 2, f"yc{k}") for k in range(4)] + [None] * 4
            c.update(us=us, ycat=ycat)
            W_ = 15 + L
            for g in range(4):
                E3 = f32(UE[g], nseq * W_).rearrange("p (s j) -> p s j", s=nseq)
                sa = P.alloc(nseq * W_ * 4, "sa"); sb = P.alloc(nseq * W_ * 4, "sb")
                A3 = f32(sa, nseq * W_).rearrange("p (s j) -> p s j", s=nseq); B3 = f32(sb, nseq * W_).rearrange("p (s j) -> p s j", s=nseq)
                b.tt(A3[:, :, 1:W_], E3[:, :, 1:W_], E3[:, :, 0:W_ - 1], ALU.add, [UE[g]], [sa])
                cur, curT, oth, othT, lo, sh_ = A3, sa, B3, sb, 1, 2
                for _ in range(g):
                    b.tt(oth[:, :, lo + sh_:W_], cur[:, :, lo + sh_:W_], cur[:, :, lo:W_ - sh_], ALU.add, [curT], [othT])
                    cur, curT, oth, othT = oth, othT, cur, curT
                    lo += sh_; sh_ *= 2
                df = P.alloc(n * 2, "diff")
                D3 = bf(df, n).rearrange("p (s j) -> p s j", s=nseq)
                b.stt(D3, cur[:, :, 15:W_], 1.0 / POOLW[g], E3[:, :, 15:W_], ALU.mult, ALU.subtract, [curT, UE[g]], [df])
                if t == 0:
                    fx = P.alloc(64, "fx")
                    b.tt(f32(fx, 16), f32(curT, W_)[:, 15:31], INVC[:, 16 * g:16 * g + 16], ALU.mult, [curT, cst], [fx])
                    b.tt(bf(df, n)[:, 0:16], f32(fx, 16), f32(UE[g], W_)[:, 15:31], ALU.subtract, [fx, UE[g]], [df])
                    P.release(fx)
                P.release(sa); P.release(sb)
                ps = P.ps_alloc()
                b.mm(ps, WpoolA[:, g, :], bf(df, n), True, True, [Wpool, df], n)
                b.act(bf(ycat[g], n), ps.ap[:, 0:n], AF.Copy, [ps, cst], [ycat[g]], scale=PSC[:, l * 4 + g:l * 4 + g + 1])
                P.ps_release(ps); P.release(df)
                if not sample and t < 7:
                    hc = P.alloc(64, "hc")
                    b.copy(f32(hc, 15), f32(UE[g], W_)[:, L:L + 15], [UE[g]], [hc])
                    b.copy(f32(UE[g], 15), f32(hc, 15), [hc], [UE[g]])
                    P.release(hc)
                yield
            return

        def ssm(c, gen=None, wgen=None):
            t = c["t"]; n = c["n"]; sample = c["sample"]; nseq = c["nseq"]; L = c["L"]; us = c["us"]
            gel = [P.alloc(n * 2, f"gel{j}") for j in range(4)]
            ystate = {"yps": None}

            G = 2 if sample else 1
            NG = 16 // G
            gn = G * n

            NRING = 2
            ring = [dict(pr=P.alloc(gn * 4, "pr"), pi_=P.alloc(gn * 4, "pi"), qr=P.alloc(gn * 4, "qr"), qi=P.alloc(gn * 4, "qi"),
                         a=tuple(P.alloc(gn * 2, f"a{q_}") for q_ in range(4))) for _ in range(NRING)]

            def gview(ap):
                if sample:
                    return ap.rearrange("p (g s j) -> p g s j", g=G, s=NSS)
                return ap.rearrange("p (g m) -> p g m", g=G)

            def ph0(gi):
                psb = P.ps_alloc()
                for g in range(G):
                    blk = gi * G + g; j, i = blk // 4, blk % 4
                    for ri in (0, 1):
                        o_ap = psb.ap[:, 256 * ri + g * n:256 * ri + (g + 1) * n]; l_ap = BBA[:, j, ri, 128 * i:128 * i + 128]; r_ap = bf(us[j], n)
                        P.op("pe", lambda e, o_ap=o_ap, l_ap=l_ap, r_ap=r_ap: e.matmul(o_ap, lhsT=l_ap, rhs=r_ap, start=True, stop=True), [BB, us[j]], [psb])
                return psb

            def ph1(gi, psb):
                b0 = gi * G
                if sample:
                    Cb = TCA[:, b0:b0 + G, 0:LS].unsqueeze(2).to_broadcast([128, G, NSS, LS])
                    Sb = TSA[:, b0:b0 + G, 0:LS].unsqueeze(2).to_broadcast([128, G, NSS, LS])
                else:
                    Cb = TCA[:, b0:b0 + G, 0:n]; Sb = TSA[:, b0:b0 + G, 0:n]
                V = gview
                rs_ = ring[gi % NRING]
                pr, pi_, qr, qi = rs_["pr"], rs_["pi_"], rs_["qr"], rs_["qi"]
                tw, tw2 = qr, qi
                re_ap = psb.ap[:, 0:gn]; im_ap = psb.ap[:, 256:256 + gn]
                b.tt(V(f32(pr, gn)), V(re_ap), Cb, ALU.mult, [psb, TC], [pr])
                b.tt(V(f32(tw, gn)), V(im_ap), Sb, ALU.mult, [psb, TS], [tw])
                b.tt(V(f32(pi_, gn)), V(im_ap), Cb, ALU.mult, [psb, TC], [pi_])
                b.tt(V(f32(tw2, gn)), V(re_ap), Sb, ALU.mult, [psb, TS], [tw2])
                b.tt(f32(pr, gn), f32(pr, gn), f32(tw, gn), ALU.add, [pr, tw], [pr], eng="pool")
                b.tt(f32(pi_, gn), f32(pi_, gn), f32(tw2, gn), ALU.subtract, [pi_, tw2], [pi_], eng="pool")
                P.ps_release(psb)
                return dict(gi=gi, pr=pr, pi_=pi_, qr=qr, qi=qi, Cb=Cb, Sb=Sb)

            def ph2(c):
                gi, pr, pi_, Cb, Sb = c["gi"], c["pr"], c["pi_"], c["Cb"], c["Sb"]
                V = gview
                b0 = gi * G
                qr, qi = c["qr"], c["qi"]
                if sample:
                    p0 = V(f32(pr, gn))[:, :, :, 0]; p1 = V(f32(pi_, gn))[:, :, :, 0]
                    b.tt(p0, p0, h3(injr)[:, b0:b0 + G, :], ALU.add, [pr, injr], [pr])
                    b.tt(p1, p1, h3(inji)[:, b0:b0 + G, :], ALU.add, [pi_, inji], [pi_])
                    rm = P.alloc(gn * 4, "rm")
                    b.tt(f32(rm, gn).rearrange("p (g m) -> p g m", g=G), MASK.unsqueeze(1).to_broadcast([128, G, 128]),
                         f32(Rt, 16)[:, b0:b0 + G].unsqueeze(2).to_broadcast([128, G, 128]), ALU.mult, [cst, Rt], [rm])
                    b.scan(f32(qr, gn), f32(rm, gn), f32(pr, gn), 0.0, [rm, pr], [qr])
                    b.scan(f32(qi, gn), f32(rm, gn), f32(pi_, gn), 0.0, [rm, pi_], [qi])
                    P.release(rm)
                else:
                    for g in range(G):
                        blk = b0 + g; sl = slice(g * n, (g + 1) * n)
                        rb = f32(Rt, 16)[:, blk:blk + 1].to_broadcast([128, n])
                        b.scan(f32(qr, gn)[:, sl], rb, f32(pr, gn)[:, sl], f32(initr, 16)[:, blk:blk + 1], [Rt, pr, initr], [qr])
                        b.scan(f32(qi, gn)[:, sl], rb, f32(pi_, gn)[:, sl], f32(initi, 16)[:, blk:blk + 1], [Rt, pi_, initi], [qi])
                a1, a2, a3, a4 = ring[gi % NRING]["a"]
                b.tt(V(bf(a4, gn)), V(f32(qi, gn)), Cb, ALU.mult, [qi, TC], [a4], eng="pool")
                b.tt(V(bf(a1, gn)), V(f32(qr, gn)), Cb, ALU.mult, [qr, TC], [a1])
                b.tt(V(bf(a2, gn)), V(f32(qi, gn)), Sb, ALU.mult, [qi, TS], [a2])
                b.tt(V(bf(a3, gn)), V(f32(qr, gn)), Sb, ALU.mult, [qr, TS], [a3], eng="pool")
                if not sample:
                    for g in range(G):
                        blk = b0 + g
                        ql_r = f32(qr, gn)[:, (g + 1) * n - 1:(g + 1) * n]; ql_i = f32(qi, gn)[:, (g + 1) * n - 1:(g + 1) * n]
                        sm = P.alloc(64, "sm")
                        if t < 7:
                            cL = f32(ec, 16)[:, blk:blk + 1]; sL = f32(es_, 16)[:, blk:blk + 1]
                            dr, di = f32(initr, 16)[:, blk:blk + 1], f32(initi, 16)[:, blk:blk + 1]; dT = (initr, initi)
                        else:
                            cL = TCA[:, blk, n - 1:n]; sL = TSA[:, blk, n - 1:n]
                            dr, di = f32(hfr, 16)[:, blk:blk + 1], f32(hfi, 16)[:, blk:blk + 1]; dT = (hfr, hfi)
                        rdT = [ec, es_, nes] if t < 7 else [TC, TS, nes]
                        nsL = f32(nes, 32)[:, blk:blk + 1] if t < 7 else f32(nes, 32)[:, 16 + blk:16 + blk + 1]
                        sm2 = P.alloc(64, "sm2")
                        b.act(f32(sm, 1), ql_i, AF.Identity, [qi] + rdT, [sm], scale=nsL)
                        b.act(dr, ql_r, AF.Identity, [qr, sm] + rdT, [dT[0]], scale=cL, bias=f32(sm, 1))
                        b.act(f32(sm2, 1), ql_r, AF.Identity, [qr] + rdT, [sm2], scale=sL)
                        b.act(di, ql_i, AF.Identity, [qi, sm2] + rdT, [dT[1]], scale=cL, bias=f32(sm2, 1))
                        P.release(sm); P.release(sm2)
                else:
                    q7r = V(f32(qr, gn))[:, :, :, LS - 1]; q7i = V(f32(qi, gn))[:, :, :, LS - 1]
                    c7 = TCA[:, b0:b0 + G, LS - 1:LS].to_broadcast([128, G, NSS]); s7 = TSA[:, b0:b0 + G, LS - 1:LS].to_broadcast([128, G, NSS])
                    sm = P.alloc(G * NSS * 4, "sm"); smv = f32(sm, G * NSS).rearrange("p (g s) -> p g s", g=G)
                    dr_ = h3(hsr)[:, b0:b0 + G, :]; di_ = h3(hsi)[:, b0:b0 + G, :]
                    b.tt(smv, q7i, s7, ALU.mult, [qi, TS], [sm])
                    b.tt(dr_, q7r, c7, ALU.mult, [qr, TC], [hsr])
                    b.tt(dr_, dr_, smv, ALU.subtract, [hsr, sm], [hsr])
                    b.tt(smv, q7r, s7, ALU.mult, [qr, TS], [sm])
                    b.tt(di_, q7i, c7, ALU.mult, [qi, TC], [hsi])
                    b.tt(di_, di_, smv, ALU.add, [hsi, sm], [hsi])
                    P.release(sm)
                return dict(gi=gi, a=(a1, a2, a3, a4))

            def ph3(c):
                gi = c["gi"]
                a1, a2, a3, a4 = c["a"]
                for g in range(G):
                    blk = gi * G + g; j, i = blk // 4, blk % 4
                    sl = slice(g * n, (g + 1) * n)
                    if i == 0:
                        ystate["yps"] = P.ps_alloc()
                    yps = ystate["yps"]
                    b.mm(yps, CTA[:, 0, blk, :], bf(a1, gn)[:, sl], i == 0, False, [CT, a1], n)
                    b.mm(yps, CTA[:, 1, blk, :], bf(a2, gn)[:, sl], False, False, [CT, a2], n)
                    b.mm(yps, CTA[:, 2, blk, :], bf(a3, gn)[:, sl], False, False, [CT, a3], n)
                    b.mm(yps, CTA[:, 2, blk, :], bf(a4, gn)[:, sl], False, False, [CT, a4], n)
                    if i == 3:
                        b.mm(yps, DDA[:, j, :], bf(us[j], n), False, True, [DD, us[j]], n)
                        b.act(bf(gel[j], n), yps.ap[:, 0:n], AF.Gelu, [yps], [gel[j]])
                        P.ps_release(yps)
            c0s, c1s, c2s = {0: ph0(0), 1: ph0(1)}, {}, {}
            for step in range(NG + 2):
                if step + 2 < NG:
                    c0s[step + 2] = ph0(step + 2)
                if step < NG:
                    c1s[step] = ph1(step, c0s.pop(step))
                if 1 <= step <= NG:
                    c2s[step - 1] = ph2(c1s.pop(step - 1))
                if step >= 2:
                    ph3(c2s.pop(step - 2))
                if wgen is not None:
                    next(wgen, None)
                if gen is not None and step >= 2:
                    next(gen, None)
            for g_ in (wgen, gen):
                if g_ is not None:
                    for _ in g_:
                        pass
            for rs_ in ring:
                for t_ in (rs_["pr"], rs_["pi_"], rs_["qr"], rs_["qi"]) + rs_["a"]:
                    P.release(t_)
            for j in range(4):
                P.release(us[j])
            c["gel"] = gel

        def glu(c):
            t = c["t"]; n = c["n"]; gel = c["gel"]; ycat = c["ycat"]
            for m in range(4):
                ps = P.ps_alloc()
                for k in range(4):
                    b.mm(ps, WgluA[:, k, 128 * m:128 * m + 128], bf(gel[k], n), k == 0, k == 3, [Wglu, gel[k]], n)
                sg = P.alloc(n * 2, "sg")
                b.act(bf(sg, n), ps.ap[:, 0:n], AF.Sigmoid, [ps, cst], [sg], bias=BGL[:, l * 4 + m:l * 4 + m + 1])
                P.ps_release(ps)
                ycat[4 + m] = P.alloc(n * 2, f"yc{4 + m}")
                b.tt(bf(ycat[4 + m], n), bf(gel[m], n), bf(sg, n), ALU.mult, [gel[m], sg], [ycat[4 + m]], eng="pool")
                P.release(sg)
            for j in range(4):
                P.release(gel[j])

        def wout_gen(c):
            t = c["t"]; n = c["n"]; ycat = c["ycat"]
            prev = None
            for m in range(9):
                cur = None
                if m < 8:
                    ps = P.ps_alloc()
                    for k in range(8):
                        b.mm(ps, WoutA[:, k, 128 * m:128 * m + 128], bf(ycat[k], n), k == 0, k == 7, [Wout, ycat[k]], n)
                    cur = (m, ps)
                if prev is not None:
                    pm, pps = prev
                    b.tt(XU[pm][t].ap, pps.ap[:, 0:n], XU[pm][t].ap, ALU.add, [pps, XU[pm][t]], [XU[pm][t]])
                    P.ps_release(pps)
                prev = cur
                yield
            for k in range(8):
                P.release(ycat[k])

        def wout(c):
            for _ in wout_gen(c):
                pass

        ctxs = {0: {}}
        for _ in front_gen(0, ctxs[0]):
            pass
        for t in range(NT):
            gen = None
            if t + 1 < NT:
                ctxs[t + 1] = {}
                gen = front_gen(t + 1, ctxs[t + 1])
                next(gen)
            wgen = wout_gen(ctxs.pop(t - 1)) if t >= 1 else None
            ssm(ctxs[t], gen, wgen); glu(ctxs[t])
        wout(ctxs.pop(NT - 1))
        def cmul_k(xr, xi, n, view, kr_ap, ki_ap):
            t1 = P.alloc(n * 4, "ck1"); t2 = P.alloc(n * 4, "ck2")
            b.tt(view(t1), view(xr), kr_ap, ALU.mult, [xr, Kr], [t1])
            b.tt(view(t2), view(xi), ki_ap, ALU.mult, [xi, Ki], [t2])
            b.tt(view(t1), view(t1), view(t2), ALU.subtract, [t1, t2], [t1])
            b.tt(view(t2), view(xr), ki_ap, ALU.mult, [xr, Ki], [t2])
            b.tt(view(xr), view(xi), kr_ap, ALU.mult, [xi, Kr], [xr])
            b.tt(view(xi), view(xr), view(t2), ALU.add, [xr, t2], [xi])
            b.copy(view(xr), view(t1), [t1], [xr])
            P.release(t1); P.release(t2)
        cmul_k(hfr, hfi, 16, lambda t_: f32(t_, 16), f32(Kr, 16), f32(Ki, 16))
        cmul_k(hsr, hsi, 256, h3, f32(Kr, 16).unsqueeze(2).to_broadcast([128, 16, 16]), f32(Ki, 16).unsqueeze(2).to_broadcast([128, 16, 16]))
        P.dma("sp", nre_p[l], f32(hfr, 16), reads=[hfr], is_output=True)
        P.dma("sp", nim_p[l], f32(hfi, 16), reads=[hfi], is_output=True)
        P.dma("sp", nre_s[l].rearrange("p b s -> p (b s)"), f32(hsr, 256), reads=[hsr], is_output=True)
        P.dma("sp", nim_s[l].rearrange("p b s -> p (b s)"), f32(hsi, 256), reads=[hsi], is_output=True)
        for t_ in (Win, Wout, Wglu, Wpool, CT, DD, BB, TC, TS, ec, es_, nes, injr, inji,
                   initr, initi, hfr, hfi, hsr, hsi) + tuple(UE):
            P.release(t_)

        XNblk = P.alloc(8 * NTOK * 2, "XN")
        XNA = bf(XNblk).rearrange("p (k n) -> p k n", k=8)
        FT = [(0, 512, [0, 1]), (512, 1024, [2, 3]), (1024, 1536, [4, 5]), (1536, 2048, [6, 7]), (2048, 2176, [8])]
        XNU = {}
        for (c0_, c1_, uu_) in FT:
            XNU[c0_] = T(P, XNblk.ap, f"xn_{c0_}"); XNU[c0_].inherit = list(XNblk.inherit)

        def norm_all(gi):
            for (c0, c1, uu) in FT:
                n = c1 - c0
                rmsnorm_tile([[XU[k][u] for u in uu] for k in range(8)], n, (G_, (l * 3 + gi) * 8),
                             lambda k, c0=c0, c1=c1: XNA[:, k, c0:c1], XNU[c0])
        groups = [(q * 4, 4) for q in range(5)] + [(20, 2)]
        def load_group(gi):
            ch0, nck = groups[gi]
            Wg = P.alloc(8 * 128 * nck * 2, "Wg"); Wu = P.alloc(8 * 128 * nck * 2, "Wu"); Wd = P.alloc(nck * 1024 * 2, "Wd")
            load_w(Wg, w_gu[l][:, 128 * ch0:128 * (ch0 + nck)], 8, 128 * nck)
            load_w(Wu, w_gu[l][:, DFF + 128 * ch0:DFF + 128 * (ch0 + nck)], 8, 128 * nck)
            load_w(Wd, w_dn[l][128 * ch0:128 * (ch0 + nck), :], nck, 1024)
            return (Wg, Wu, Wd)

        def load_ple():
            Wpg = P.alloc(16384, "Wpg"); load_w(Wpg, w_pg[l], 8, 1024)
            Wpl = P.alloc(4096, "Wpl"); load_w(Wpl, w_ple[l], 2, 1024)
            return (Wpg, Wpl)
        GW0 = load_group(0)
        norm_all(1)
        nxt = GW0
        PW = None
        for gi, (ch0, nck) in enumerate(groups):
            Wg, Wu, Wd = nxt
            if gi + 1 < len(groups):
                nxt = load_group(gi + 1)
            else:
                PW = load_ple()
            WgA = bf(Wg).rearrange("p (k n) -> p k n", k=8); WuA = bf(Wu).rearrange("p (k n) -> p k n", k=8)
            WdA = bf(Wd).rearrange("p (k n) -> p k n", k=nck)
            def gateup(c0, c1, uu):
                n = c1 - c0
                hh = [P.alloc(n * 2, f"h{c}") for c in range(nck)]
                for c in range(nck):
                    pg = P.ps_alloc(); pu = P.ps_alloc()
                    for k in range(8):
                        b.mm(pg, WgA[:, k, 128 * c:128 * c + 128], XNA[:, k, c0:c1], k == 0, k == 7, [Wg, XNU[c0]], n)
                    for k in range(8):
                        b.mm(pu, WuA[:, k, 128 * c:128 * c + 128], XNA[:, k, c0:c1], k == 0, k == 7, [Wu, XNU[c0]], n)
                    sl = P.alloc(n * 4, "silu")
                    b.act(f32(sl, n), pg.ap[:, 0:n], AF.Silu, [pg], [sl])
                    b.tt(bf(hh[c], n), pu.ap[:, 0:n], f32(sl, n), ALU.mult, [pu, sl], [hh[c]])
                    P.ps_release(pg); P.ps_release(pu); P.release(sl)
                return (n, uu, hh)

            def down(ctx):
                n, uu, hh = ctx
                for m in range(8):
                    ps = P.ps_alloc()
                    for c in range(nck):
                        b.mm(ps, WdA[:, c, 128 * m:128 * m + 128], bf(hh[c], n), c == 0, c == nck - 1, [Wd, hh[c]], n)
                    xa = P_ap_join([XU[m][u] for u in uu])
                    b.tt(xa, ps.ap[:, 0:n], xa, ALU.add, [ps] + [XU[m][u] for u in uu], [XU[m][u] for u in uu])
                    P.ps_release(ps)
                for c in range(nck):
                    P.release(hh[c])
            prev = None
            for (c0, c1, uu) in FT:
                cur = gateup(c0, c1, uu)
                if prev is not None:
                    down(prev)
                prev = cur
            down(prev)
            P.release(Wg); P.release(Wu); P.release(Wd)

        Wpg, Wpl = PW
        if l + 1 < depth_run:
            MW = load_mixer_weights(l + 1)
        WpgA = bf(Wpg).rearrange("p (k n) -> p k n", k=8); WplA = bf(Wpl).rearrange("p (k n) -> p k n", k=2)
        def norm_gen(fi, gi):
            c0, c1, uu = FT[fi]; n = c1 - c0
            xin = [P_ap_join([XU[k][u] for u in uu]) for k in range(8)]
            ps = P.ps_alloc()
            for k in range(8):
                sq = P.alloc(n * 2, "sq")
                b.act(bf(sq, n), xin[k], AF.Square, [XU[k][u] for u in uu], [sq])
                b.mm(ps, ONES, bf(sq, n), k == 0, k == 7, [cst, sq], n)
                P.release(sq)
            rs = P.alloc(n * 4, "rstd")
            b.act(f32(rs, n), ps.ap[:, 0:n], AF.Sqrt, [ps], [rs], scale=1.0 / D, bias=EPS)
            P.ps_release(ps)
            yield
            b.recip(f32(rs, n), f32(rs, n), [rs], [rs])
            g0_ = (l * 3 + gi) * 8
            for k in range(8):
                b.stt(XNA[:, k, c0:c1], xin[k], G_[:, g0_ + k:g0_ + k + 1], f32(rs, n), ALU.mult, ALU.mult,
                      [XU[k][u] for u in uu] + [rs, cst], [XNU[c0]])
            P.release(rs)
            yield
        for _ in norm_gen(0, 2):
            pass
        for fi, (c0, c1, uu) in enumerate(FT):
            ngen = norm_gen(fi + 1, 2) if fi + 1 < len(FT) else None
            n = c1 - c0
            pb = P.alloc(2 * n * 2, "pTb")
            pbA = bf(pb, 2 * n).rearrange("p (k n) -> p k n", k=2)
            P.dma("pool", pbA, pT[l, :, :, c0:c1], writes=[pb])
            for m in range(8):
                ps = P.ps_alloc(); ps2 = P.ps_alloc()
                for k in range(8):
                    b.mm(ps, WpgA[:, k, 128 * m:128 * m + 128], XNA[:, k, c0:c1], k == 0, k == 7, [Wpg, XNU[c0]], n)
                for k in range(2):
                    b.mm(ps2, WplA[:, k, 128 * m:128 * m + 128], pbA[:, k, :], k == 0, k == 1, [Wpl, pb], n)
                sg = P.alloc(n * 4, "psg")
                b.act(f32(sg, n), ps.ap[:, 0:n], AF.Sigmoid, [ps], [sg])
                b.tt(f32(sg, n), ps2.ap[:, 0:n], f32(sg, n), ALU.mult, [ps2, sg], [sg])
                xa = P_ap_join([XU[m][u] for u in uu])
                b.tt(xa, xa, f32(sg, n), ALU.add, [sg] + [XU[m][u] for u in uu], [XU[m][u] for u in uu])
                P.ps_release(ps); P.ps_release(ps2); P.release(sg)
                if ngen is not None and m in (2, 6):
                    next(ngen, None)
            if ngen is not None:
                for _ in ngen:
                    pass
            P.release(pb)
        for u_ in XNU.values():
            XNblk.inherit = XNblk.inherit + u_.events()
        P.release(Wpg); P.release(Wpl); P.release(XNblk)

    for (c0, c1, uu) in [(0, 512, [0, 1]), (512, 1024, [2, 3]), (1024, 1536, [4, 5]), (1536, 2048, [6, 7]), (2048, 2176, [8])]:
        n = c1 - c0
        yo = [P.alloc(n * 4, f"yo{k}") for k in range(8)]
        rmsnorm_tile([[XU[k][u] for u in uu] for k in range(8)], n, (GF, 0), lambda k: f32(yo[k], n), yo)
        for k in range(8):
            P.dma("sp", yT[:, k, c0:c1], f32(yo[k], n), reads=[yo[k]], is_output=True)
            P.release(yo[k])
    P.finalize()
    return nc, P


_CACHE = {}


def _prep_shared(inp):
    f = np.float32
    sh = {}
    for nm in ("w_in", "w_out", "w_pool", "w_glu", "w_gate_up", "w_down", "w_ple", "w_ple_gate"):
        sh[nm] = np.ascontiguousarray(inp[nm], dtype=f)
    g3 = np.stack([inp["g_mix"], inp["g_ffn"], inp["g_ple"]], axis=1)
    sh["gvec"] = np.ascontiguousarray(g3.reshape(DEPTH, 3, 8, 128).transpose(3, 0, 1, 2).reshape(128, DEPTH * 3 * 8), dtype=f)
    sh["gfin"] = np.ascontiguousarray(np.asarray(inp["g_final"]).reshape(8, 128).T, dtype=f)
    sh["pscale"] = np.ascontiguousarray(np.asarray(inp["pool_scale"]).reshape(DEPTH, 4, 128).transpose(2, 0, 1).reshape(128, DEPTH * 4), dtype=f)
    sh["bglu"] = np.ascontiguousarray(np.asarray(inp["b_glu"]).reshape(DEPTH, 4, 128).transpose(2, 0, 1).reshape(128, DEPTH * 4), dtype=f)
    ic = np.zeros((128, 4, 16), f)
    for g, w in enumerate(POOLW):
        ic[:, g, :] = 1.0 / np.minimum(np.arange(16) + 1, w)
    sh["invcnt"] = ic.reshape(128, 64)
    mk = np.ones((128, 128), f); mk[:, 0::8] = 0.0
    sh["mask8"] = mk
    are = np.asarray(inp["ssm_a_re"], f).reshape(DEPTH, 2048); aim = np.asarray(inp["ssm_a_im"], f).reshape(DEPTH, 2048)
    ldt = np.repeat(np.asarray(inp["ssm_log_dt"], f), 64, axis=1)
    sh["abc"] = np.ascontiguousarray(np.stack([are, aim, ldt], axis=1))
    fm = lambda a: a.reshape(DEPTH, 16, 128).transpose(0, 2, 1)
    cat = lambda a: np.concatenate([fm(a)[l_] for l_ in range(DEPTH)], axis=1)
    sh["afm2"] = np.ascontiguousarray(np.concatenate([cat(are), cat(aim), cat(ldt)], axis=1))
    Bq = np.zeros((DEPTH, 2, 128, 2048), f)
    for ri, nm in enumerate(("ssm_b_re", "ssm_b_im")):
        Bm = np.asarray(inp[nm], f)
        for g in range(32):
            j, gg = g // 8, g % 8
            Bq[:, ri, 16 * gg:16 * gg + 16, 512 * j + 64 * gg:512 * j + 64 * gg + 64] = Bm[:, g].transpose(0, 2, 1)
    sh["Bq"] = Bq
    Cq = np.zeros((DEPTH, 2, 128, 16, 128), f)
    for ri, nm in enumerate(("ssm_c_re", "ssm_c_im")):
        Cm = np.asarray(inp[nm], f)
        for g in range(32):
            blk, gg = g // 2, g % 2
            c0 = 32 * (blk % 4) + 16 * gg
            Cq[:, ri, 64 * gg:64 * gg + 64, blk, c0:c0 + 16] = Cm[:, g].transpose(0, 2, 1)
    sh["Cq"] = Cq.reshape(DEPTH, 2, 128, 2048)
    Dq = np.zeros((DEPTH, 128, 4, 128), f)
    dd = np.asarray(inp["ssm_d"], f).reshape(DEPTH, 4, 128)
    for j in range(4):
        Dq[:, np.arange(128), j, np.arange(128)] = dd[:, j, :]
    sh["Dq"] = Dq.reshape(DEPTH, 128, 512)
    return sh


def _prep_core(inp, c):
    f = np.float32
    m = {}
    xs = np.asarray(inp["x_sample"], f)[NSS * c:NSS * c + NSS].reshape(NSS * LS, D)
    xa = np.concatenate([np.asarray(inp["x_prompt"], f)[c], xs], axis=0)
    m["xT"] = np.ascontiguousarray(xa.T.reshape(8, 128, NTOK).transpose(1, 0, 2))
    ps = np.asarray(inp["p_sample"], f)[:, NSS * c:NSS * c + NSS].reshape(DEPTH, NSS * LS, PLE)
    pa = np.concatenate([np.asarray(inp["p_prompt"], f)[:, c], ps], axis=1)
    m["pT"] = np.ascontiguousarray(pa.transpose(0, 2, 1).reshape(DEPTH, 2, 128, NTOK).transpose(0, 2, 1, 3))
    sp = np.asarray(inp["state_pool"], f)[:, NSS * c:NSS * c + NSS]
    m["spool_tm"] = np.ascontiguousarray(sp)
    m["poolbuf"] = np.ascontiguousarray(sp.reshape(DEPTH, NSS, 15, 4, 128).transpose(0, 4, 3, 1, 2))
    for nm, key in (("h0re", "state_ssm_re"), ("h0im", "state_ssm_im")):
        h = np.asarray(inp[key], f)[:, NSS * c:NSS * c + NSS].reshape(DEPTH, NSS, 16, 128)
        m[nm] = np.ascontiguousarray(h.transpose(0, 3, 2, 1))
    return m


def kernel(**inputs):
    from concourse.bass_utils import run_bass_kernel_spmd
    import os
    dr = int(os.environ.get("KDEPTH", DEPTH))
    if ("prog", dr) not in _CACHE:
        _CACHE[("prog", dr)] = build_program(dr)
    nc, _ = _CACHE[("prog", dr)]
    sh = _prep_shared(inputs)
    in_maps = []
    for c in range(NCORES):
        m = dict(sh); m.update(_prep_core(inputs, c)); in_maps.append(m)
    res = run_bass_kernel_spmd(nc, in_maps, core_ids=list(range(NCORES)))
    R = res.results
    f = np.float32
    y_p = np.zeros((NCORES, SEQ, D), f); y_s = np.zeros((NCORES * NSS, LS, D), f)
    pool_p = np.zeros((DEPTH, NCORES, 15, 512), f); pool_s = np.zeros((DEPTH, NCORES * NSS, 15, 512), f)
    re_p = np.zeros((DEPTH, NCORES, 32, 64), f); im_p = np.zeros_like(re_p)
    re_s = np.zeros((DEPTH, NCORES * NSS, 32, 64), f); im_s = np.zeros_like(re_s)
    for c in range(NCORES):
        r = R[c]
        ya = np.asarray(r["yT"]).transpose(1, 0, 2).reshape(D, NTOK).T
        y_p[c] = ya[:SEQ]; y_s[NSS * c:NSS * c + NSS] = ya[SEQ:].reshape(NSS, LS, D)
        pool_p[:, c] = np.asarray(r["npool_p"]); pool_s[:, NSS * c:NSS * c + NSS] = np.asarray(r["npool_s"])
        re_p[:, c] = np.asarray(r["nre_p"]).transpose(0, 2, 1).reshape(DEPTH, 32, 64)
        im_p[:, c] = np.asarray(r["nim_p"]).transpose(0, 2, 1).reshape(DEPTH, 32, 64)
        re_s[:, NSS * c:NSS * c + NSS] = np.asarray(r["nre_s"]).transpose(0, 3, 2, 1).reshape(DEPTH, NSS, 32, 64)
        im_s[:, NSS * c:NSS * c + NSS] = np.asarray(r["nim_s"]).transpose(0, 3, 2, 1).reshape(DEPTH, NSS, 32, 64)
    return (y_p, y_s, pool_p, re_p, im_p, pool_s, re_s, im_s)
```

```python
import numpy as np
import concourse.bass as bass
import concourse.mybir as mybir
from contextlib import ExitStack

F32 = mybir.dt.float32
BF16 = mybir.dt.bfloat16
AF = mybir.ActivationFunctionType
ALU = mybir.AluOpType

ENGS = ("pe", "act", "dve", "pool", "sp")
EPOCH = 8000
NDMASEM = 20


class Instr:
    __slots__ = ("eng", "fn", "deps", "flag", "sem", "val", "is_dma", "dsem", "dval", "idx_")

    def __init__(self, eng, fn, deps):
        self.eng = eng; self.fn = fn; self.deps = deps
        self.flag = False; self.sem = None; self.val = 0
        self.is_dma = False; self.dsem = None; self.dval = 0


class DmaEv:
    __slots__ = ("sem", "val")

    def __init__(self, sem, val):
        self.sem = sem; self.val = val


class T:
    def __init__(self, prog, ap, name="", off=None, nbytes=None):
        self.prog = prog; self.ap = ap; self.name = name
        self.last_w = None
        self.readers = {}
        self.dreaders = []
        self.inherit = []
        self.off = off; self.nbytes = nbytes

    def events(self):
        ev = list(self.readers.values()) + list(self.dreaders) + list(self.inherit)
        if self.last_w is not None:
            ev.append(self.last_w)
        return ev


class Prog:
    def __init__(self, nc, arena_words):
        self.nc = nc
        self.es = ExitStack()
        self.streams = {e: [] for e in ENGS}
        self.arena = self.es.enter_context(nc.sbuf_tensor("arena", [128, arena_words], F32))
        self.arena_bytes = arena_words * 4
        self.free = [[0, self.arena_bytes, []]]
        self.psum = []
        for b in range(8):
            t = self.es.enter_context(nc.psum_tensor(f"psb{b}", [128, 512], F32))
            self.psum.append(T(self, t[:, :], f"ps{b}"))
        self.ps_free = list(range(8))
        self.dma_sems = {}
        self.dma_rr = {}
        self.dma_last = {}
        self.dma_cnt = {}
        self.out_events = []
        self.n_instr = 0

    def alloc(self, nbytes, name=""):
        nbytes = (nbytes + 63) // 64 * 64
        best = None
        for i, seg in enumerate(self.free):
            if seg[1] - seg[0] >= nbytes:
                st = seg[3] if len(seg) > 3 else -1
                if best is None or st < best[0]:
                    best = (st, i)
        if best is not None:
            i = best[1]; seg = self.free[i]; lo, hi, ev = seg[0], seg[1], seg[2]
            st = seg[3] if len(seg) > 3 else -1
            if hi - lo == nbytes:
                self.free.pop(i)
            else:
                self.free[i] = [lo + nbytes, hi, ev, st]
            t = T(self, self.arena[:, lo // 4:(lo + nbytes) // 4], name, lo, nbytes)
            t.inherit = list(ev)
            return t
        raise MemoryError(f"arena full allocating {nbytes} for {name}; free={[(q[0], q[1]) for q in self.free]}")

    def release(self, t):
        ev = self._dedupe(t.events())
        self.free.append([t.off, t.off + t.nbytes, ev, self.n_instr])
        self.free.sort(key=lambda s: s[0])
        merged = []
        for seg in self.free:
            if len(seg) < 4:
                seg.append(-1)
            if merged and merged[-1][1] == seg[0]:
                merged[-1][1] = seg[1]
                merged[-1][2] = self._dedupe(merged[-1][2] + seg[2])
                merged[-1][3] = max(merged[-1][3], seg[3])
            else:
                merged.append(seg)
        self.free = merged

    @staticmethod
    def _dedupe(evs):
        best = {}; out = []
        for e in evs:
            if isinstance(e, Instr) and not e.is_dma:
                k = e.eng
                if k not in best or best[k].idx_ < e.idx_:
                    best[k] = e
            else:
                out.append(e)
        seen = set(); res = []
        for e in out:
            if id(e) not in seen:
                seen.add(id(e)); res.append(e)
        return list(best.values()) + res

    def ps_alloc(self):
        assert self.ps_free, "out of PSUM banks"
        return self.psum[self.ps_free.pop(0)]

    def ps_release(self, t):
        self.ps_free.append(self.psum.index(t))

    def _deps(self, reads, writes):
        deps = []
        for t in reads:
            if t.last_w is not None:
                deps.append(t.last_w)
            deps.extend(t.inherit)
        for t in writes:
            deps.extend(t.events())
        return deps

    def op(self, eng, fn, reads=(), writes=()):
        ins = Instr(eng, fn, self._deps(reads, writes))
        ins.idx_ = self.n_instr; self.n_instr += 1
        self.streams[eng].append(ins)
        for t in reads:
            t.readers[eng] = ins
        for t in writes:
            t.last_w = ins; t.readers = {}; t.dreaders = []; t.inherit = []
        return ins

    def dma(self, queue, out_ap, in_ap, reads=(), writes=(), is_output=False, **kw):
        k = self.dma_rr.get(queue, 0)
        self.dma_rr[queue] = (k + 1) % NDMASEM
        key = (queue, k)
        if key not in self.dma_sems:
            self.dma_sems[key] = self.es.enter_context(self.nc.semaphore(f"d_{queue}{k}"))
            self.dma_cnt[key] = 0
        sem = self.dma_sems[key]
        deps = self._deps(reads, writes)
        if key in self.dma_last:
            deps.append(self.dma_last[key])
        self.dma_cnt[key] += 16
        ev = DmaEv(sem, self.dma_cnt[key])
        self.dma_last[key] = ev

        def fn(e, out_ap=out_ap, in_ap=in_ap, kw=kw):
            return e.dma_start(out=out_ap, in_=in_ap, **kw)
        ins = Instr(queue, fn, deps)
        ins.idx_ = self.n_instr; self.n_instr += 1
        ins.is_dma = True; ins.dsem = sem
        self.streams[queue].append(ins)
        for t in reads:
            t.dreaders.append(ev)
        for t in writes:
            t.last_w = ev; t.readers = {}; t.dreaders = []; t.inherit = []
        if is_output or not writes:
            self.out_events.append(ev)
        return ev

    def finalize(self):
        nc = self.nc
        for eng in ENGS:
            for ins in self.streams[eng]:
                for d in ins.deps:
                    if isinstance(d, Instr):
                        if d.eng == "pe" and ins.eng == "pe" and not ins.is_dma:
                            continue
                        d.flag = True
        sems = {}
        for eng in ENGS:
            c = 0
            for ins in self.streams[eng]:
                if ins.flag:
                    ep = c // EPOCH
                    if (eng, ep) not in sems:
                        sems[(eng, ep)] = self.es.enter_context(nc.semaphore(f"c_{eng}{ep}"))
                    ins.sem = sems[(eng, ep)]; ins.val = c % EPOCH + 1
                    c += 1
        out_events = self.out_events
        streams = self.streams
        engmap = {"pe": "tensor", "act": "scalar", "dve": "vector", "pool": "gpsimd", "sp": "sync"}
        blk = self.es.enter_context(nc.Block())
        stats = {}

        def make(eng):
            def body(e):
                waited = {}
                nw = 0
                for ins in streams[eng]:
                    need = {}
                    for d in ins.deps:
                        if isinstance(d, Instr):
                            if d.sem is None:
                                continue
                            s, v = d.sem, d.val
                        else:
                            s, v = d.sem, d.val
                        if need.get(id(s), (None, 0))[1] < v:
                            need[id(s)] = (s, v)
                    for sid, (s, v) in need.items():
                        if waited.get(sid, 0) < v:
                            e.wait_ge(s, v); waited[sid] = v; nw += 1
                    bi = ins.fn(e)
                    if ins.is_dma:
                        bi.then_inc(ins.dsem, 16)
                    elif ins.flag:
                        bi.then_inc(ins.sem, 1)
                if eng == "sp":
                    fin = {}
                    for ev in out_events:
                        if fin.get(id(ev.sem), (None, 0))[1] < ev.val:
                            fin[id(ev.sem)] = (ev.sem, ev.val)
                    for sid, (s, v) in fin.items():
                        if waited.get(sid, 0) < v:
                            e.wait_ge(s, v)
                stats[eng] = (len(streams[eng]), nw)
            return body
        for eng in ENGS:
            if streams[eng] or eng == "sp":
                getattr(blk, engmap[eng])(make(eng))
        self.stats = stats
        self.es.close()


NCORES = 8
D = 1024; DEPTH = 4; SEQ = 2048; NSS = 16; LS = 8
NTOK = SEQ + NSS * LS
DFF = 2816; PLE = 256
TT = 256
NT = 9
EPS = 1e-6
POOLW = (2, 4, 8, 16)


def tile_cols(t):
    return (256 * t, 256 * t + 256) if t < 8 else (2048, 2176)


class B:
    def __init__(self, P):
        self.P = P

    def mm(self, ps, lhsT, rhs, start, stop, reads, n):
        self.P.op("pe", lambda e: e.matmul(ps.ap[:, 0:n], lhsT=lhsT, rhs=rhs, start=start, stop=stop), reads, [ps])

    def act(self, out, in_, func, reads, writes, **kw):
        self.P.op("act", lambda e: e.activation(out=out, in_=in_, func=func, **kw), reads, writes)

    def tt(self, out, in0, in1, op, reads, writes, eng="dve"):
        self.P.op(eng, lambda e: e.tensor_tensor(out=out, in0=in0, in1=in1, op=op), reads, writes)

    def ts(self, out, in0, s1, s2, op0, op1, reads, writes, eng="dve"):
        if op1 is None:
            self.P.op(eng, lambda e: e.tensor_scalar(out=out, in0=in0, scalar1=s1, scalar2=None, op0=op0), reads, writes)
        else:
            self.P.op(eng, lambda e: e.tensor_scalar(out=out, in0=in0, scalar1=s1, scalar2=s2, op0=op0, op1=op1), reads, writes)

    def stt(self, out, in0, scalar, in1, op0, op1, reads, writes, eng="dve"):
        self.P.op(eng, lambda e: e.scalar_tensor_tensor(out=out, in0=in0, scalar=scalar, in1=in1, op0=op0, op1=op1), reads, writes)

    def scan(self, out, d0, d1, init, reads, writes):
        self.P.op("dve", lambda e: e.tensor_tensor_scan(out=out, data0=d0, data1=d1, initial=init, op0=ALU.mult, op1=ALU.add), reads, writes)

    def copy(self, out, in_, reads, writes, eng="dve"):
        self.P.op(eng, lambda e: e.tensor_copy(out=out, in_=in_), reads, writes)

    def memset(self, out, val, writes, eng="dve"):
        self.P.op(eng, lambda e: e.memset(out, val), [], writes)

    def recip(self, out, in_, reads, writes):
        self.P.op("dve", lambda e: e.reciprocal(out=out, in_=in_), reads, writes)


def build_program(depth_run=DEPTH):
    nc = bass.Bass("TRN2", target_bir_lowering=False)

    def din(name, shape):
        return nc.dram_tensor(name, list(shape), F32, kind="ExternalInput").ap()

    def dout(name, shape):
        return nc.dram_tensor(name, list(shape), F32, kind="ExternalOutput").ap()

    xT = din("xT", [128, 8, NTOK])
    pT = din("pT", [DEPTH, 128, 2, NTOK])
    poolbuf = din("poolbuf", [DEPTH, 128, 4, NSS, 15])
    spool_tm = din("spool_tm", [DEPTH, NSS, 15, 512])
    h0re = din("h0re", [DEPTH, 128, 16, NSS]); h0im = din("h0im", [DEPTH, 128, 16, NSS])
    gvec = din("gvec", [128, DEPTH * 3 * 8]); gfin = din("gfin", [128, 8])
    pscale = din("pscale", [128, DEPTH * 4]); bglu = din("bglu", [128, DEPTH * 4])
    invcnt = din("invcnt", [128, 64]); mask8 = din("mask8", [128, 128])
    afm2 = din("afm2", [128, 192])
    abc = din("abc", [DEPTH, 3, 2048])
    Bq = din("Bq", [DEPTH, 2, 128, 2048])
    Cq = din("Cq", [DEPTH, 2, 128, 2048])
    Dq = din("Dq", [DEPTH, 128, 512])
    w_in = din("w_in", [DEPTH, D, D]); w_out = din("w_out", [DEPTH, D, D])
    w_pool = din("w_pool", [DEPTH, 4, 128, 128]); w_glu = din("w_glu", [DEPTH, 512, 512])
    w_gu = din("w_gate_up", [DEPTH, D, 2 * DFF]); w_dn = din("w_down", [DEPTH, DFF, D])
    w_ple = din("w_ple", [DEPTH, PLE, D]); w_pg = din("w_ple_gate", [DEPTH, D, D])

    yT = dout("yT", [128, 8, NTOK])
    npool_p = dout("npool_p", [DEPTH, 15, 512])
    npool_s = dout("npool_s", [DEPTH, NSS, 15, 512])
    nre_p = dout("nre_p", [DEPTH, 128, 16]); nim_p = dout("nim_p", [DEPTH, 128, 16])
    nre_s = dout("nre_s", [DEPTH, 128, 16, NSS]); nim_s = dout("nim_s", [DEPTH, 128, 16, NSS])

    P = Prog(nc, 212000 // 4)
    b = B(P)

    def f32(t, n=None, off=0):
        a = t.ap
        return a[:, off:off + n] if n is not None else a

    def bf(t, n=None, off=0):
        a = t.ap.bitcast(BF16)
        return a[:, off:off + n] if n is not None else a

    Xblk = P.alloc(8 * NTOK * 4, "X")
    XU = [[T(P, Xblk.ap[:, k * NTOK + tile_cols(t)[0]: k * NTOK + tile_cols(t)[1]], f"x{k}_{t}") for t in range(NT)] for k in range(8)]
    cst = P.alloc(4096, "consts")
    ca = cst.ap
    G_ = ca[:, 0:96]; GF = ca[:, 96:104]; PSC = ca[:, 104:120]; BGL = ca[:, 120:136]
    INVC = ca[:, 136:200]; MASK = ca[:, 200:328]
    ONES = ca[:, 328:392].bitcast(BF16)
    onesf = P.alloc(512, "onesf")
    P.dma("sp", G_, gvec, writes=[cst]); P.dma("sp", GF, gfin, writes=[cst])
    P.dma("sp", PSC, pscale, writes=[cst]); P.dma("sp", BGL, bglu, writes=[cst])
    P.dma("sp", INVC, invcnt, writes=[cst]); P.dma("sp", MASK, mask8, writes=[cst])
    b.memset(onesf.ap[:, 0:128], 1.0, [onesf])
    b.copy(ONES, onesf.ap[:, 0:128], [onesf], [cst])
    for t in range(NT):
        c0, c1 = tile_cols(t)
        for k in range(8):
            P.dma("sp", XU[k][t].ap, xT[:, k, c0:c1], writes=[XU[k][t]])

    def load_w(dst_t, src_ap, nk, ncols, eltoff=0):
        dst = bf(dst_t)[:, eltoff:eltoff + nk * ncols].rearrange("p (k n) -> p k n", k=nk)
        src = src_ap.rearrange("(k p) n -> p k n", p=128)
        half = max(1, nk // 2)
        for k0 in range(0, nk, half):
            k1 = min(nk, k0 + half)
            P.dma("pool", dst[:, k0:k1, :], src[:, k0:k1, :], writes=[dst_t])

    def rmsnorm_tile(units, n, gcol0, out_fn, out_tiles, f32out=False):
        gsrc, g0 = gcol0
        xin = [P_ap_join(units[k]) for k in range(8)]
        ps = P.ps_alloc()
        for k in range(8):
            sq = P.alloc(n * 2, "sq")
            b.act(bf(sq, n), xin[k], AF.Square, units[k], [sq])
            b.mm(ps, ONES, bf(sq, n), k == 0, k == 7, [cst, sq], n)
            P.release(sq)
        rs = P.alloc(n * 4, "rstd")
        b.act(f32(rs, n), ps.ap[:, 0:n], AF.Sqrt, [ps], [rs], scale=1.0 / D, bias=EPS)
        P.ps_release(ps)
        b.recip(f32(rs, n), f32(rs, n), [rs], [rs])
        for k in range(8):
            b.stt(out_fn(k), xin[k], gsrc[:, g0 + k:g0 + k + 1], f32(rs, n), ALU.mult, ALU.mult,
                  units[k] + [rs, cst], [out_tiles[k]] if isinstance(out_tiles, list) else [out_tiles])
        P.release(rs)

    def P_ap_join(us):
        if len(us) == 1:
            return us[0].ap
        a0 = us[0].ap; n = sum(u.ap.shape[1] for u in us)
        return us[0].wide(n)

    def wide(self, n):
        return self.base[:, self.c0:self.c0 + n]
    T.wide = wide
    for k in range(8):
        for t in range(NT):
            XU[k][t].base = Xblk.ap; XU[k][t].c0 = k * NTOK + tile_cols(t)[0]

    def load_mixer_weights(l):
        Win = P.alloc(16384, "Win"); load_w(Win, w_in[l], 8, 1024)
        Wout = P.alloc(16384, "Wout"); load_w(Wout, w_out[l], 8, 1024)
        Wglu = P.alloc(4096, "Wglu"); load_w(Wglu, w_glu[l], 4, 512)
        Wpool = P.alloc(1024, "Wpool")
        P.dma("pool", bf(Wpool, 512).rearrange("p (g d) -> p g d", g=4), w_pool[l].rearrange("g c d -> c g d"), writes=[Wpool])
        DD = P.alloc(1024, "DD")
        P.dma("pool", bf(DD, 512), Dq[l], writes=[DD])
        BB = P.alloc(8192, "Braw")
        for ri in range(2):
            P.dma("pool", bf(BB)[:, ri * 2048:(ri + 1) * 2048], Bq[l, ri], writes=[BB])
        return (Win, Wout, Wglu, Wpool, None, DD, BB)

    for l in range(depth_run):
        if l == 0:
            MW = load_mixer_weights(0)
        Win, Wout, Wglu, Wpool, CTs, DD, BB = MW
        WinA = bf(Win).rearrange("p (k n) -> p k n", k=8); WoutA = bf(Wout).rearrange("p (k n) -> p k n", k=8)
        WgluA = bf(Wglu).rearrange("p (k n) -> p k n", k=4); WpoolA = bf(Wpool, 512).rearrange("p (g d) -> p g d", g=4)
        DDA = bf(DD, 512).rearrange("p (j n) -> p j n", j=4)
        BBA = bf(BB).rearrange("p (r j n) -> p j r n", r=2, j=4)

        def lam_math(are, aim, ldt, n, tiles_r, pfx):
            o = {}
            def new(nm):
                o[nm] = P.alloc(n * 4, pfx + nm); return o[nm]
            A = lambda t_: f32(t_, n)
            z = new("z"); b.ts(A(z), ldt, 0.125, None, ALU.mult, None, tiles_r, [z])
            dt = new("dt")
            b.ts(A(dt), A(z), 1.0 / 11, 1.0, ALU.mult, ALU.add, [z], [dt])
            for kk in range(10, 0, -1):
                b.tt(A(dt), A(dt), A(z), ALU.mult, [dt, z], [dt])
                b.ts(A(dt), A(dt), 1.0 / kk, 1.0, ALU.mult, ALU.add, [dt], [dt])
            for _ in range(3):
                b.tt(A(dt), A(dt), A(dt), ALU.mult, [dt], [dt])
            xx = new("xx"); b.tt(A(xx), A(dt), are, ALU.mult, [dt] + tiles_r, [xx])
            em1 = new("em1")
            b.ts(A(em1), A(xx), 1.0 / 6, 1.0, ALU.mult, ALU.add, [xx], [em1])
            for kk in (5, 4, 3, 2):
                b.tt(A(em1), A(em1), A(xx), ALU.mult, [em1, xx], [em1])
                b.ts(A(em1), A(em1), 1.0 / kk, 1.0, ALU.mult, ALU.add, [em1], [em1])
            b.tt(A(em1), A(em1), A(xx), ALU.mult, [em1, xx], [em1])
            r = new("r"); b.ts(A(r), A(em1), 1.0, None, ALU.add, None, [em1], [r])
            ang = new("ang"); b.tt(A(ang), A(dt), aim, ALU.mult, [dt] + tiles_r, [ang])
            s = new("s"); c = new("c"); tmp = new("tmp"); sh = new("sh")
            b.act(A(s), A(ang), AF.Sin, [ang], [s], scale=0.125)
            b.act(A(tmp), A(ang), AF.Sin, [ang], [tmp], scale=0.0625)
            b.tt(A(tmp), A(tmp), A(tmp), ALU.mult, [tmp], [tmp])
            b.ts(A(c), A(tmp), -2.0, 1.0, ALU.mult, ALU.add, [tmp], [c])
            for it in range(3):
                if it == 2:
                    b.copy(A(sh), A(s), [s], [sh])
                b.tt(A(tmp), A(s), A(s), ALU.mult, [s], [tmp])
                b.tt(A(s), A(s), A(c), ALU.mult, [s, c], [s])
                b.ts(A(s), A(s), 2.0, None, ALU.mult, None, [s], [s])
                b.tt(A(c), A(c), A(c), ALU.mult, [c], [c])
                b.tt(A(c), A(c), A(tmp), ALU.subtract, [c, tmp], [c])
            for nm in ("z", "xx", "ang"):
                P.release(o.pop(nm))
            o["tmp"] = tmp; o["sh"] = sh
            return o

        def k_math(o, are, aim, n, tiles_r):
            A = lambda t_: f32(t_, n)
            nr = P.alloc(n * 4, "nr"); li = P.alloc(n * 4, "li"); den = P.alloc(n * 4, "den")
            tmp = o["tmp"]
            b.tt(A(tmp), A(o["sh"]), A(o["sh"]), ALU.mult, [o["sh"]], [tmp])
            b.tt(A(nr), A(o["em1"]), A(o["c"]), ALU.mult, [o["em1"], o["c"]], [nr])
            b.stt(A(nr), A(tmp), -2.0, A(nr), ALU.mult, ALU.add, [tmp, nr], [nr])
            b.tt(A(li), A(o["r"]), A(o["s"]), ALU.mult, [o["r"], o["s"]], [li])
            b.tt(A(den), are, are, ALU.mult, tiles_r, [den])
            b.tt(A(tmp), aim, aim, ALU.mult, tiles_r, [tmp])
            b.tt(A(den), A(den), A(tmp), ALU.add, [den, tmp], [den])
            b.recip(A(den), A(den), [den], [den])
            kre = P.alloc(n * 4, "kre"); kim = P.alloc(n * 4, "kim")
            b.tt(A(kre), A(nr), are, ALU.mult, [nr] + tiles_r, [kre])
            b.tt(A(tmp), A(li), aim, ALU.mult, [li] + tiles_r, [tmp])
            b.tt(A(kre), A(kre), A(tmp), ALU.add, [kre, tmp], [kre])
            b.tt(A(kre), A(kre), A(den), ALU.mult, [kre, den], [kre])
            b.tt(A(kim), A(li), are, ALU.mult, [li] + tiles_r, [kim])
            b.tt(A(tmp), A(nr), aim, ALU.mult, [nr] + tiles_r, [tmp])
            b.tt(A(kim), A(kim), A(tmp), ALU.subtract, [kim, tmp], [kim])
            b.tt(A(kim), A(kim), A(den), ALU.mult, [kim, den], [kim])
            for t_ in (nr, li, den):
                P.release(t_)
            o["kre"] = kre; o["kim"] = kim

        if l == 0:
            pfA = P.alloc(768, "afmA"); P.dma("sp", f32(pfA, 192), afm2, writes=[pfA])
            foA = lam_math(f32(pfA, 64, 0), f32(pfA, 64, 64), f32(pfA, 64, 128), 64, [pfA], "fmA_")
            k_math(foA, f32(pfA, 64, 0), f32(pfA, 64, 64), 64, [pfA])
            lamrA = P.alloc(256, "lamrA"); lamiA = P.alloc(256, "lamiA")
            kt = P.alloc(256, "kt"); ikr = P.alloc(256, "ikr"); iki = P.alloc(256, "iki"); lr0 = P.alloc(256, "lr0"); li0 = P.alloc(256, "li0")
            F = lambda t_: f32(t_, 64)
            KrA, KiA, RtA, CtA, StA = foA["kre"], foA["kim"], foA["r"], foA["c"], foA["s"]
            b.tt(F(kt), F(KrA), F(KrA), ALU.mult, [KrA], [kt])
            b.tt(F(ikr), F(KiA), F(KiA), ALU.mult, [KiA], [ikr])
            b.tt(F(kt), F(kt), F(ikr), ALU.add, [kt, ikr], [kt])
            b.recip(F(kt), F(kt), [kt], [kt])
            b.tt(F(ikr), F(KrA), F(kt), ALU.mult, [KrA, kt], [ikr])
            b.stt(F(iki), F(KiA), -1.0, F(kt), ALU.mult, ALU.mult, [KiA, kt], [iki])
            b.tt(F(lr0), F(RtA), F(CtA), ALU.mult, [RtA, CtA], [lr0])
            b.tt(F(li0), F(RtA), F(StA), ALU.mult, [RtA, StA], [li0])
            b.tt(F(lamrA), F(lr0), F(ikr), ALU.mult, [lr0, ikr], [lamrA])
            b.tt(F(kt), F(li0), F(iki), ALU.mult, [li0, iki], [kt])
            b.tt(F(lamrA), F(lamrA), F(kt), ALU.subtract, [lamrA, kt], [lamrA])
            b.tt(F(lamiA), F(lr0), F(iki), ALU.mult, [lr0, iki], [lamiA])
            b.tt(F(kt), F(li0), F(ikr), ALU.mult, [li0, ikr], [kt])
            b.tt(F(lamiA), F(lamiA), F(kt), ALU.add, [lamiA, kt], [lamiA])
            for t_ in (kt, ikr, iki, lr0, li0, pfA, foA["dt"], foA["em1"], foA["tmp"], foA["sh"]):
                P.release(t_)
            GLB = dict(r=RtA, c=CtA, s=StA, kre=KrA, kim=KiA, lamr=lamrA, lami=lamiA)

        def lview(base):
            v = T(P, base.ap[:, 16 * l:16 * l + 16], base.name + f"_l{l}"); v.last_w = base.last_w
            return v
        Rt, Ct, St, Kr, Ki, lamr, lami = (lview(GLB[x_]) for x_ in ("r", "c", "s", "kre", "kim", "lamr", "lami"))
        CT = P.alloc(12288, "CT")
        CTA = bf(CT).rearrange("p (r b n) -> p r b n", r=3, b=16)
        for hb in range(2):
            cs_ = P.alloc(8192, "cstage"); c1 = P.alloc(4096, "c1"); c2 = P.alloc(4096, "c2")
            for ri in range(2):
                P.dma("sp", f32(cs_, 1024, 1024 * ri), Cq[l, ri, :, 1024 * hb:1024 * hb + 1024], writes=[cs_])
            cre = f32(cs_, 1024, 0).rearrange("p (b n) -> p b n", b=8); cim = f32(cs_, 1024, 1024).rearrange("p (b n) -> p b n", b=8)
            krb = f32(Kr, 16)[:, 8 * hb:8 * hb + 8].unsqueeze(2).to_broadcast([128, 8, 128])
            kib = f32(Ki, 16)[:, 8 * hb:8 * hb + 8].unsqueeze(2).to_broadcast([128, 8, 128])
            v1 = f32(c1, 1024).rearrange("p (b n) -> p b n", b=8); v2 = f32(c2, 1024).rearrange("p (b n) -> p b n", b=8)
            b.tt(v1, cre, krb, ALU.mult, [cs_, Kr], [c1]); b.tt(v2, cim, kib, ALU.mult, [cs_, Ki], [c2])
            b.tt(CTA[:, 0, 8 * hb:8 * hb + 8, :], v1, v2, ALU.subtract, [c1, c2], [CT])
            b.tt(CTA[:, 1, 8 * hb:8 * hb + 8, :], v2, v1, ALU.subtract, [c1, c2], [CT])
            b.tt(v1, cre, kib, ALU.mult, [cs_, Ki], [c1]); b.tt(v2, cim, krb, ALU.mult, [cs_, Kr], [c2])
            b.stt(CTA[:, 2, 8 * hb:8 * hb + 8, :], v1, -1.0, v2, ALU.mult, ALU.subtract, [c1, c2], [CT])
            for t_ in (cs_, c1, c2):
                P.release(t_)
        TC = P.alloc(16384, "TC"); TS = P.alloc(16384, "TS")
        TCA = f32(TC).rearrange("p (b n) -> p b n", b=16); TSA = f32(TS).rearrange("p (b n) -> p b n", b=16)
        b.memset(TCA[:, :, 0:1], 1.0, [TC]); b.memset(TSA[:, :, 0:1], 0.0, [TS])
        ec = P.alloc(64, "ec"); es_ = P.alloc(64, "es"); et = P.alloc(64, "et")
        b.copy(f32(ec, 16), f32(Ct, 16), [Ct], [ec]); b.copy(f32(es_, 16), f32(St, 16), [St], [es_])
        tq = P.alloc(16 * 128 * 4, "tq")
        kk = 1
        while kk < 256:
            ecb = f32(ec, 16).unsqueeze(2).to_broadcast([128, 16, kk]); esb = f32(es_, 16).unsqueeze(2).to_broadcast([128, 16, kk])
            tqa = f32(tq, 16 * kk).rearrange("p (b n) -> p b n", b=16)
            b.tt(TCA[:, :, kk:2 * kk], TCA[:, :, 0:kk], ecb, ALU.mult, [TC, ec], [TC])
            b.tt(tqa, TSA[:, :, 0:kk], esb, ALU.mult, [TS, es_], [tq])
            b.tt(TCA[:, :, kk:2 * kk], TCA[:, :, kk:2 * kk], tqa, ALU.subtract, [TC, tq], [TC])
            b.tt(TSA[:, :, kk:2 * kk], TSA[:, :, 0:kk], ecb, ALU.mult, [TS, ec], [TS])
            b.tt(tqa, TCA[:, :, 0:kk], esb, ALU.mult, [TC, es_], [tq])
            b.tt(TSA[:, :, kk:2 * kk], TSA[:, :, kk:2 * kk], tqa, ALU.add, [TS, tq], [TS])
            b.tt(f32(et, 16), f32(es_, 16), f32(es_, 16), ALU.mult, [es_], [et])
            b.tt(f32(es_, 16), f32(es_, 16), f32(ec, 16), ALU.mult, [es_, ec], [es_])
            b.ts(f32(es_, 16), f32(es_, 16), 2.0, None, ALU.mult, None, [es_], [es_])
            b.tt(f32(ec, 16), f32(ec, 16), f32(ec, 16), ALU.mult, [ec], [ec])
            b.tt(f32(ec, 16), f32(ec, 16), f32(et, 16), ALU.subtract, [ec, et], [ec])
            kk *= 2
        P.release(tq); P.release(et)
        nes = P.alloc(128, "nes")
        b.ts(f32(nes, 16), f32(es_, 16), -1.0, None, ALU.mult, None, [es_], [nes])
        b.ts(f32(nes, 32)[:, 16:32], TSA[:, :, 255], -1.0, None, ALU.mult, None, [TS], [nes])
        hr = P.alloc(1024, "h0r"); hi = P.alloc(1024, "h0i")
        P.dma("sp", f32(hr, 256), h0re[l].rearrange("p b s -> p (b s)"), writes=[hr])
        P.dma("sp", f32(hi, 256), h0im[l].rearrange("p b s -> p (b s)"), writes=[hi])
        injr = P.alloc(1024, "injr"); inji = P.alloc(1024, "inji"); itmp = P.alloc(1024, "itmp")
        h3 = lambda t_: f32(t_, 256).rearrange("p (b s) -> p b s", b=16)
        lrb = f32(lamr, 16).unsqueeze(2).to_broadcast([128, 16, 16]); lib = f32(lami, 16).unsqueeze(2).to_broadcast([128, 16, 16])
        b.tt(h3(injr), h3(hr), lrb, ALU.mult, [hr, lamr], [injr])
        b.tt(h3(itmp), h3(hi), lib, ALU.mult, [hi, lami], [itmp])
        b.tt(h3(injr), h3(injr), h3(itmp), ALU.subtract, [injr, itmp], [injr])
        b.tt(h3(inji), h3(hi), lrb, ALU.mult, [hi, lamr], [inji])
        b.tt(h3(itmp), h3(hr), lib, ALU.mult, [hr, lami], [itmp])
        b.tt(h3(inji), h3(inji), h3(itmp), ALU.add, [inji, itmp], [inji])
        for t_ in (hr, hi, itmp):
            P.release(t_)
        initr = P.alloc(64, "initr"); initi = P.alloc(64, "initi")
        b.memset(f32(initr, 16), 0.0, [initr]); b.memset(f32(initi, 16), 0.0, [initi])
        hfr = P.alloc(64, "hfr"); hfi = P.alloc(64, "hfi")
        hsr = P.alloc(1024, "hsr"); hsi = P.alloc(1024, "hsi")

        UE = [P.alloc(max(15 + TT, NSS * 23) * 4, f"uext{g}") for g in range(4)]
        for g in range(4):
            b.memset(f32(UE[g], 15), 0.0, [UE[g]])

        def front_gen(t, c):
            c0, c1 = tile_cols(t); n = c1 - c0
            sample = (t == NT - 1)
            nseq, L = (NSS, LS) if sample else (1, TT)
            c.update(t=t, n=n, sample=sample, nseq=nseq, L=L)
            ps = P.ps_alloc()
            for k in range(8):
                sq = P.alloc(n * 2, "sq")
                b.act(bf(sq, n), XU[k][t].ap, AF.Square, [XU[k][t]], [sq])
                b.mm(ps, ONES, bf(sq, n), k == 0, k == 7, [cst, sq], n)
                P.release(sq)
            rs = P.alloc(n * 4, "rstd")
            b.act(f32(rs, n), ps.ap[:, 0:n], AF.Sqrt, [ps], [rs], scale=1.0 / D, bias=EPS)
            P.ps_release(ps)
            yield
            xn = [P.alloc(n * 2, f"xn{k}") for k in range(8)]
            b.recip(f32(rs, n), f32(rs, n), [rs], [rs])
            g0_ = (l * 3 + 0) * 8
            for k in range(8):
                b.stt(bf(xn[k], n), XU[k][t].ap, G_[:, g0_ + k:g0_ + k + 1], f32(rs, n), ALU.mult, ALU.mult, [XU[k][t], rs, cst], [xn[k]])
            P.release(rs)
            yield
            if sample:
                for g in range(4):
                    P.dma("sp", f32(UE[g], NSS * 23).rearrange("p (s j) -> p s j", s=NSS)[:, :, 0:15], poolbuf[l, :, g], writes=[UE[g]])
            us = [P.alloc(n * 2, f"us{j}") for j in range(4)]
            for m in range(8):
                ps = P.ps_alloc()
                for k in range(8):
                    b.mm(ps, WinA[:, k, 128 * m:128 * m + 128], bf(xn[k], n), k == 0, k == 7, [Win, xn[k]], n)
                if m < 4:
                    dst = f32(UE[m], nseq * (15 + L)).rearrange("p (s j) -> p s j", s=nseq)[:, :, 15:15 + L]
                    b.act(dst, ps.ap[:, 0:n].rearrange("p (s j) -> p s j", s=nseq), AF.Copy, [ps], [UE[m]])
                else:
                    b.act(bf(us[m - 4], n), ps.ap[:, 0:n], AF.Copy, [ps], [us[m - 4]])
                P.ps_release(ps)
                yield
            if t == 7 or sample:
                ps = P.ps_alloc()
                m0 = n - 15 if t == 7 else 0
                mrows = n - m0
                for k in range(8):
                    o_ap = ps.ap[0:mrows, 0:512]; l_ap = bf(xn[k], n)[:, m0:n]; r_ap = WinA[:, k, 0:512]
                    P.op("pe", lambda e, o_ap=o_ap, l_ap=l_ap, r_ap=r_ap, st=(k == 0), sp_=(k == 7): e.matmul(o_ap, lhsT=l_ap, rhs=r_ap, start=st, stop=sp_), [Win, xn[k]], [ps])
                zt = P.alloc(2048, "ztm")
                b.act(f32(zt, 512)[0:mrows, :], ps.ap[0:mrows, 0:512], AF.Copy, [ps], [zt])
                P.ps_release(ps)
                if t == 7:
                    P.dma("sp", npool_p[l], f32(zt, 512)[0:15, :], reads=[zt], is_output=True)
                else:
                    for s_ in range(NSS):
                        P.dma("sp", npool_s[l, s_, 7:15, :], f32(zt, 512)[8 * s_:8 * s_ + 8, :], reads=[zt], is_output=True)
                    P.dma("sp", npool_s[l, :, 0:7, :], spool_tm[l, :, 8:15, :], is_output=True)
                P.release(zt)
            for k in range(8):
                P.release(xn[k])
            yield
            ycat = [P.alloc(n * 2, f"yc{k}") for k in range(4)] + [None] * 4
            c.update(us=us, ycat=ycat)
            W_ = 15 + L
            for g in range(4):
                E3 = f32(UE[g], nseq * W_).rearrange("p (s j) -> p s j", s=nseq)
                sa = P.alloc(nseq * W_ * 4, "sa"); sb = P.alloc(nseq * W_ * 4, "sb")
                A3 = f32(sa, nseq * W_).rearrange("p (s j) -> p s j", s=nseq); B3 = f32(sb, nseq * W_).rearrange("p (s j) -> p s j", s=nseq)
                b.tt(A3[:, :, 1:W_], E3[:, :, 1:W_], E3[:, :, 0:W_ - 1], ALU.add, [UE[g]], [sa])
                cur, curT, oth, othT, lo, sh_ = A3, sa, B3, sb, 1, 2
                for _ in range(g):
                    b.tt(oth[:, :, lo + sh_:W_], cur[:, :, lo + sh_:W_], cur[:, :, lo:W_ - sh_], ALU.add, [curT], [othT])
                    cur, curT, oth, othT = oth, othT, cur, curT
                    lo += sh_; sh_ *= 2
                df = P.alloc(n * 2, "diff")
                D3 = bf(df, n).rearrange("p (s j) -> p s j", s=nseq)
                b.stt(D3, cur[:, :, 15:W_], 1.0 / POOLW[g], E3[:, :, 15:W_], ALU.mult, ALU.subtract, [curT, UE[g]], [df])
                if t == 0:
                    fx = P.alloc(64, "fx")
                    b.tt(f32(fx, 16), f32(curT, W_)[:, 15:31], INVC[:, 16 * g:16 * g + 16], ALU.mult, [curT, cst], [fx])
                    b.tt(bf(df, n)[:, 0:16], f32(fx, 16), f32(UE[g], W_)[:, 15:31], ALU.subtract, [fx, UE[g]], [df])
                    P.release(fx)
                P.release(sa); P.release(sb)
                ps = P.ps_alloc()
                b.mm(ps, WpoolA[:, g, :], bf(df, n), True, True, [Wpool, df], n)
                b.act(bf(ycat[g], n), ps.ap[:, 0:n], AF.Copy, [ps, cst], [ycat[g]], scale=PSC[:, l * 4 + g:l * 4 + g + 1])
                P.ps_release(ps); P.release(df)
                if not sample and t < 7:
                    hc = P.alloc(64, "hc")
                    b.copy(f32(hc, 15), f32(UE[g], W_)[:, L:L + 15], [UE[g]], [hc])
                    b.copy(f32(UE[g], 15), f32(hc, 15), [hc], [UE[g]])
                    P.release(hc)
                yield
            return

        def ssm(c, gen=None, wgen=None):
            t = c["t"]; n = c["n"]; sample = c["sample"]; nseq = c["nseq"]; L = c["L"]; us = c["us"]
            gel = [P.alloc(n * 2, f"gel{j}") for j in range(4)]
            ystate = {"yps": None}

            G = 4 if sample else 1
            NG = 16 // G
            gn = G * n

            def gview(ap):
                if sample:
                    return ap.rearrange("p (g s j) -> p g s j", g=G, s=NSS)
                return ap.rearrange("p (g m) -> p g m", g=G)

            def ph0(gi):
                psr = P.ps_alloc(); psi = P.ps_alloc()
                for g in range(G):
                    blk = gi * G + g; j, i = blk // 4, blk % 4
                    for ri, pst in ((0, psr), (1, psi)):
                        o_ap = pst.ap[:, g * n:(g + 1) * n]; l_ap = BBA[:, j, ri, 128 * i:128 * i + 128]; r_ap = bf(us[j], n)
                        P.op("pe", lambda e, o_ap=o_ap, l_ap=l_ap, r_ap=r_ap: e.matmul(o_ap, lhsT=l_ap, rhs=r_ap, start=True, stop=True), [BB, us[j]], [pst])
                return (psr, psi)

            def ph1(gi, pss):
                psr, psi = pss
                b0 = gi * G
                if sample:
                    Cb = TCA[:, b0:b0 + G, 0:LS].unsqueeze(2).to_broadcast([128, G, NSS, LS])
                    Sb = TSA[:, b0:b0 + G, 0:LS].unsqueeze(2).to_broadcast([128, G, NSS, LS])
                else:
                    Cb = TCA[:, b0:b0 + G, 0:n]; Sb = TSA[:, b0:b0 + G, 0:n]
                V = gview
                pr = P.alloc(gn * 4, "pr"); pi_ = P.alloc(gn * 4, "pi")
                qr = P.alloc(gn * 4, "qr"); qi = P.alloc(gn * 4, "qi")
                tw, tw2 = qr, qi
                b.tt(V(f32(pr, gn)), V(psr.ap[:, 0:gn]), Cb, ALU.mult, [psr, TC], [pr])
                b.tt(V(f32(tw, gn)), V(psi.ap[:, 0:gn]), Sb, ALU.mult, [psi, TS], [tw])
                b.tt(V(f32(pi_, gn)), V(psi.ap[:, 0:gn]), Cb, ALU.mult, [psi, TC], [pi_])
                b.tt(V(f32(tw2, gn)), V(psr.ap[:, 0:gn]), Sb, ALU.mult, [psr, TS], [tw2])
                b.tt(f32(pr, gn), f32(pr, gn), f32(tw, gn), ALU.add, [pr, tw], [pr], eng="pool")
                b.tt(f32(pi_, gn), f32(pi_, gn), f32(tw2, gn), ALU.subtract, [pi_, tw2], [pi_], eng="pool")
                P.ps_release(psr); P.ps_release(psi)
                return dict(gi=gi, pr=pr, pi_=pi_, qr=qr, qi=qi, Cb=Cb, Sb=Sb)

            def ph2(c):
                gi, pr, pi_, Cb, Sb = c["gi"], c["pr"], c["pi_"], c["Cb"], c["Sb"]
                V = gview
                b0 = gi * G
                qr, qi = c["qr"], c["qi"]
                if sample:
                    p0 = V(f32(pr, gn))[:, :, :, 0]; p1 = V(f32(pi_, gn))[:, :, :, 0]
                    b.tt(p0, p0, h3(injr)[:, b0:b0 + G, :], ALU.add, [pr, injr], [pr])
                    b.tt(p1, p1, h3(inji)[:, b0:b0 + G, :], ALU.add, [pi_, inji], [pi_])
                    rm = qi
                    b.tt(f32(rm, gn).rearrange("p (g m) -> p g m", g=G), MASK.unsqueeze(1).to_broadcast([128, G, 128]),
                         f32(Rt, 16)[:, b0:b0 + G].unsqueeze(2).to_broadcast([128, G, 128]), ALU.mult, [cst, Rt], [rm])
                    b.scan(f32(qr, gn), f32(rm, gn), f32(pr, gn), 0.0, [rm, pr], [qr])
                    b.scan(f32(pr, gn), f32(rm, gn), f32(pi_, gn), 0.0, [rm, pi_], [pr])
                    qi, pr = pr, qi
                else:
                    for g in range(G):
                        blk = b0 + g; sl = slice(g * n, (g + 1) * n)
                        rb = f32(Rt, 16)[:, blk:blk + 1].to_broadcast([128, n])
                        b.scan(f32(qr, gn)[:, sl], rb, f32(pr, gn)[:, sl], f32(initr, 16)[:, blk:blk + 1], [Rt, pr, initr], [qr])
                        b.scan(f32(qi, gn)[:, sl], rb, f32(pi_, gn)[:, sl], f32(initi, 16)[:, blk:blk + 1], [Rt, pi_, initi], [qi])
                P.release(pr); P.release(pi_)
                a1 = P.alloc(gn * 2, "a1"); a2 = P.alloc(gn * 2, "a2"); a3 = P.alloc(gn * 2, "a3"); a4 = P.alloc(gn * 2, "a4")
                b.tt(V(bf(a4, gn)), V(f32(qi, gn)), Cb, ALU.mult, [qi, TC], [a4], eng="pool")
                b.tt(V(bf(a1, gn)), V(f32(qr, gn)), Cb, ALU.mult, [qr, TC], [a1])
                b.tt(V(bf(a2, gn)), V(f32(qi, gn)), Sb, ALU.mult, [qi, TS], [a2], eng="pool")
                b.tt(V(bf(a3, gn)), V(f32(qr, gn)), Sb, ALU.mult, [qr, TS], [a3], eng="pool")
                if not sample:
                    for g in range(G):
                        blk = b0 + g
                        ql_r = f32(qr, gn)[:, (g + 1) * n - 1:(g + 1) * n]; ql_i = f32(qi, gn)[:, (g + 1) * n - 1:(g + 1) * n]
                        sm = P.alloc(64, "sm")
                        if t < 7:
                            cL = f32(ec, 16)[:, blk:blk + 1]; sL = f32(es_, 16)[:, blk:blk + 1]
                            dr, di = f32(initr, 16)[:, blk:blk + 1], f32(initi, 16)[:, blk:blk + 1]; dT = (initr, initi)
                        else:
                            cL = TCA[:, blk, n - 1:n]; sL = TSA[:, blk, n - 1:n]
                            dr, di = f32(hfr, 16)[:, blk:blk + 1], f32(hfi, 16)[:, blk:blk + 1]; dT = (hfr, hfi)
                        rdT = [ec, es_, nes] if t < 7 else [TC, TS, nes]
                        nsL = f32(nes, 32)[:, blk:blk + 1] if t < 7 else f32(nes, 32)[:, 16 + blk:16 + blk + 1]
                        sm2 = P.alloc(64, "sm2")
                        b.act(f32(sm, 1), ql_i, AF.Identity, [qi] + rdT, [sm], scale=nsL)
                        b.act(dr, ql_r, AF.Identity, [qr, sm] + rdT, [dT[0]], scale=cL, bias=f32(sm, 1))
                        b.act(f32(sm2, 1), ql_r, AF.Identity, [qr] + rdT, [sm2], scale=sL)
                        b.act(di, ql_i, AF.Identity, [qi, sm2] + rdT, [dT[1]], scale=cL, bias=f32(sm2, 1))
                        P.release(sm); P.release(sm2)
                else:
                    q7r = V(f32(qr, gn))[:, :, :, LS - 1]; q7i = V(f32(qi, gn))[:, :, :, LS - 1]
                    c7 = TCA[:, b0:b0 + G, LS - 1:LS].to_broadcast([128, G, NSS]); s7 = TSA[:, b0:b0 + G, LS - 1:LS].to_broadcast([128, G, NSS])
                    sm = P.alloc(G * NSS * 4, "sm"); smv = f32(sm, G * NSS).rearrange("p (g s) -> p g s", g=G)
                    dr_ = h3(hsr)[:, b0:b0 + G, :]; di_ = h3(hsi)[:, b0:b0 + G, :]
                    b.tt(smv, q7i, s7, ALU.mult, [qi, TS], [sm])
                    b.tt(dr_, q7r, c7, ALU.mult, [qr, TC], [hsr])
                    b.tt(dr_, dr_, smv, ALU.subtract, [hsr, sm], [hsr])
                    b.tt(smv, q7r, s7, ALU.mult, [qr, TS], [sm])
                    b.tt(di_, q7i, c7, ALU.mult, [qi, TC], [hsi])
                    b.tt(di_, di_, smv, ALU.add, [hsi, sm], [hsi])
                    P.release(sm)
                P.release(qr); P.release(qi)
                return dict(gi=gi, a=(a1, a2, a3, a4))

            def ph3(c):
                gi = c["gi"]
                a1, a2, a3, a4 = c["a"]
                for g in range(G):
                    blk = gi * G + g; j, i = blk // 4, blk % 4
                    sl = slice(g * n, (g + 1) * n)
                    if i == 0:
                        ystate["yps"] = P.ps_alloc()
                    yps = ystate["yps"]
                    b.mm(yps, CTA[:, 0, blk, :], bf(a1, gn)[:, sl], i == 0, False, [CT, a1], n)
                    b.mm(yps, CTA[:, 1, blk, :], bf(a2, gn)[:, sl], False, False, [CT, a2], n)
                    b.mm(yps, CTA[:, 2, blk, :], bf(a3, gn)[:, sl], False, False, [CT, a3], n)
                    b.mm(yps, CTA[:, 2, blk, :], bf(a4, gn)[:, sl], False, False, [CT, a4], n)
                    if i == 3:
                        b.mm(yps, DDA[:, j, :], bf(us[j], n), False, True, [DD, us[j]], n)
                        b.act(bf(gel[j], n), yps.ap[:, 0:n], AF.Gelu, [yps], [gel[j]])
                        P.ps_release(yps)
                for t_ in (a1, a2, a3, a4):
                    P.release(t_)
            c0s, c1s, c2s = {0: ph0(0)}, {}, {}
            for step in range(NG + 2):
                if step + 1 < NG:
                    c0s[step + 1] = ph0(step + 1)
                if step < NG:
                    c1s[step] = ph1(step, c0s.pop(step))
                if 1 <= step <= NG:
                    c2s[step - 1] = ph2(c1s.pop(step - 1))
                if step >= 2:
                    ph3(c2s.pop(step - 2))
                if wgen is not None:
                    next(wgen, None)
                if gen is not None and step >= 2:
                    next(gen, None)
            for g_ in (wgen, gen):
                if g_ is not None:
                    for _ in g_:
                        pass
            for j in range(4):
                P.release(us[j])
            c["gel"] = gel

        def glu(c):
            t = c["t"]; n = c["n"]; gel = c["gel"]; ycat = c["ycat"]
            for m in range(4):
                ps = P.ps_alloc()
                for k in range(4):
                    b.mm(ps, WgluA[:, k, 128 * m:128 * m + 128], bf(gel[k], n), k == 0, k == 3, [Wglu, gel[k]], n)
                sg = P.alloc(n * 2, "sg")
                b.act(bf(sg, n), ps.ap[:, 0:n], AF.Sigmoid, [ps, cst], [sg], bias=BGL[:, l * 4 + m:l * 4 + m + 1])
                P.ps_release(ps)
                ycat[4 + m] = P.alloc(n * 2, f"yc{4 + m}")
                b.tt(bf(ycat[4 + m], n), bf(gel[m], n), bf(sg, n), ALU.mult, [gel[m], sg], [ycat[4 + m]], eng="pool")
                P.release(sg)
            for j in range(4):
                P.release(gel[j])

        def wout_gen(c):
            t = c["t"]; n = c["n"]; ycat = c["ycat"]
            prev = None
            for m in range(9):
                cur = None
                if m < 8:
                    ps = P.ps_alloc()
                    for k in range(8):
                        b.mm(ps, WoutA[:, k, 128 * m:128 * m + 128], bf(ycat[k], n), k == 0, k == 7, [Wout, ycat[k]], n)
                    cur = (m, ps)
                if prev is not None:
                    pm, pps = prev
                    b.tt(XU[pm][t].ap, pps.ap[:, 0:n], XU[pm][t].ap, ALU.add, [pps, XU[pm][t]], [XU[pm][t]])
                    P.ps_release(pps)
                prev = cur
                yield
            for k in range(8):
                P.release(ycat[k])

        def wout(c):
            for _ in wout_gen(c):
                pass

        ctxs = {0: {}}
        for _ in front_gen(0, ctxs[0]):
            pass
        for t in range(NT):
            gen = None
            if t + 1 < NT:
                ctxs[t + 1] = {}
                gen = front_gen(t + 1, ctxs[t + 1])
                next(gen)
            wgen = wout_gen(ctxs.pop(t - 1)) if t >= 1 else None
            ssm(ctxs[t], gen, wgen); glu(ctxs[t])
        wout(ctxs.pop(NT - 1))
        def cmul_k(xr, xi, n, view, kr_ap, ki_ap):
            t1 = P.alloc(n * 4, "ck1"); t2 = P.alloc(n * 4, "ck2")
            b.tt(view(t1), view(xr), kr_ap, ALU.mult, [xr, Kr], [t1])
            b.tt(view(t2), view(xi), ki_ap, ALU.mult, [xi, Ki], [t2])
            b.tt(view(t1), view(t1), view(t2), ALU.subtract, [t1, t2], [t1])
            b.tt(view(t2), view(xr), ki_ap, ALU.mult, [xr, Ki], [t2])
            b.tt(view(xr), view(xi), kr_ap, ALU.mult, [xi, Kr], [xr])
            b.tt(view(xi), view(xr), view(t2), ALU.add, [xr, t2], [xi])
            b.copy(view(xr), view(t1), [t1], [xr])
            P.release(t1); P.release(t2)
        cmul_k(hfr, hfi, 16, lambda t_: f32(t_, 16), f32(Kr, 16), f32(Ki, 16))
        cmul_k(hsr, hsi, 256, h3, f32(Kr, 16).unsqueeze(2).to_broadcast([128, 16, 16]), f32(Ki, 16).unsqueeze(2).to_broadcast([128, 16, 16]))
        P.dma("sp", nre_p[l], f32(hfr, 16), reads=[hfr], is_output=True)
        P.dma("sp", nim_p[l], f32(hfi, 16), reads=[hfi], is_output=True)
        P.dma("sp", nre_s[l].rearrange("p b s -> p (b s)"), f32(hsr, 256), reads=[hsr], is_output=True)
        P.dma("sp", nim_s[l].rearrange("p b s -> p (b s)"), f32(hsi, 256), reads=[hsi], is_output=True)
        for t_ in (Win, Wout, Wglu, Wpool, CT, DD, BB, TC, TS, ec, es_, nes, injr, inji,
                   initr, initi, hfr, hfi, hsr, hsi) + tuple(UE):
            P.release(t_)

        XNblk = P.alloc(8 * NTOK * 2, "XN")
        XNA = bf(XNblk).rearrange("p (k n) -> p k n", k=8)
        FT = [(0, 512, [0, 1]), (512, 1024, [2, 3]), (1024, 1536, [4, 5]), (1536, 2048, [6, 7]), (2048, 2176, [8])]
        XNU = {}
        for (c0_, c1_, uu_) in FT:
            XNU[c0_] = T(P, XNblk.ap, f"xn_{c0_}"); XNU[c0_].inherit = list(XNblk.inherit)

        def norm_all(gi):
            for (c0, c1, uu) in FT:
                n = c1 - c0
                rmsnorm_tile([[XU[k][u] for u in uu] for k in range(8)], n, (G_, (l * 3 + gi) * 8),
                             lambda k, c0=c0, c1=c1: XNA[:, k, c0:c1], XNU[c0])
        groups = [(q * 4, 4) for q in range(5)] + [(20, 2)]
        def load_group(gi):
            ch0, nck = groups[gi]
            Wg = P.alloc(8 * 128 * nck * 2, "Wg"); Wu = P.alloc(8 * 128 * nck * 2, "Wu"); Wd = P.alloc(nck * 1024 * 2, "Wd")
            load_w(Wg, w_gu[l][:, 128 * ch0:128 * (ch0 + nck)], 8, 128 * nck)
            load_w(Wu, w_gu[l][:, DFF + 128 * ch0:DFF + 128 * (ch0 + nck)], 8, 128 * nck)
            load_w(Wd, w_dn[l][128 * ch0:128 * (ch0 + nck), :], nck, 1024)
            return (Wg, Wu, Wd)

        def load_ple():
            Wpg = P.alloc(16384, "Wpg"); load_w(Wpg, w_pg[l], 8, 1024)
            Wpl = P.alloc(4096, "Wpl"); load_w(Wpl, w_ple[l], 2, 1024)
            return (Wpg, Wpl)
        GW0 = load_group(0)
        norm_all(1)
        nxt = GW0
        PW = None
        for gi, (ch0, nck) in enumerate(groups):
            Wg, Wu, Wd = nxt
            if gi + 1 < len(groups):
                nxt = load_group(gi + 1)
            else:
                PW = load_ple()
            WgA = bf(Wg).rearrange("p (k n) -> p k n", k=8); WuA = bf(Wu).rearrange("p (k n) -> p k n", k=8)
            WdA = bf(Wd).rearrange("p (k n) -> p k n", k=nck)
            def gateup(c0, c1, uu):
                n = c1 - c0
                hh = [P.alloc(n * 2, f"h{c}") for c in range(nck)]
                for c in range(nck):
                    pg = P.ps_alloc(); pu = P.ps_alloc()
                    for k in range(8):
                        b.mm(pg, WgA[:, k, 128 * c:128 * c + 128], XNA[:, k, c0:c1], k == 0, k == 7, [Wg, XNU[c0]], n)
                    for k in range(8):
                        b.mm(pu, WuA[:, k, 128 * c:128 * c + 128], XNA[:, k, c0:c1], k == 0, k == 7, [Wu, XNU[c0]], n)
                    sl = P.alloc(n * 4, "silu")
                    b.act(f32(sl, n), pg.ap[:, 0:n], AF.Silu, [pg], [sl])
                    b.tt(bf(hh[c], n), pu.ap[:, 0:n], f32(sl, n), ALU.mult, [pu, sl], [hh[c]])
                    P.ps_release(pg); P.ps_release(pu); P.release(sl)
                return (n, uu, hh)

            def down(ctx):
                n, uu, hh = ctx
                for m in range(8):
                    ps = P.ps_alloc()
                    for c in range(nck):
                        b.mm(ps, WdA[:, c, 128 * m:128 * m + 128], bf(hh[c], n), c == 0, c == nck - 1, [Wd, hh[c]], n)
                    xa = P_ap_join([XU[m][u] for u in uu])
                    b.tt(xa, ps.ap[:, 0:n], xa, ALU.add, [ps] + [XU[m][u] for u in uu], [XU[m][u] for u in uu])
                    P.ps_release(ps)
                for c in range(nck):
                    P.release(hh[c])
            prev = None
            for (c0, c1, uu) in FT:
                cur = gateup(c0, c1, uu)
                if prev is not None:
                    down(prev)
                prev = cur
            down(prev)
            P.release(Wg); P.release(Wu); P.release(Wd)

        Wpg, Wpl = PW
        if l + 1 < depth_run:
            MW = load_mixer_weights(l + 1)
        WpgA = bf(Wpg).rearrange("p (k n) -> p k n", k=8); WplA = bf(Wpl).rearrange("p (k n) -> p k n", k=2)
        def norm_gen(fi, gi):
            c0, c1, uu = FT[fi]; n = c1 - c0
            xin = [P_ap_join([XU[k][u] for u in uu]) for k in range(8)]
            ps = P.ps_alloc()
            for k in range(8):
                sq = P.alloc(n * 2, "sq")
                b.act(bf(sq, n), xin[k], AF.Square, [XU[k][u] for u in uu], [sq])
                b.mm(ps, ONES, bf(sq, n), k == 0, k == 7, [cst, sq], n)
                P.release(sq)
            rs = P.alloc(n * 4, "rstd")
            b.act(f32(rs, n), ps.ap[:, 0:n], AF.Sqrt, [ps], [rs], scale=1.0 / D, bias=EPS)
            P.ps_release(ps)
            yield
            b.recip(f32(rs, n), f32(rs, n), [rs], [rs])
            g0_ = (l * 3 + gi) * 8
            for k in range(8):
                b.stt(XNA[:, k, c0:c1], xin[k], G_[:, g0_ + k:g0_ + k + 1], f32(rs, n), ALU.mult, ALU.mult,
                      [XU[k][u] for u in uu] + [rs, cst], [XNU[c0]])
            P.release(rs)
            yield
        for _ in norm_gen(0, 2):
            pass
        for fi, (c0, c1, uu) in enumerate(FT):
            ngen = norm_gen(fi + 1, 2) if fi + 1 < len(FT) else None
            n = c1 - c0
            pb = P.alloc(2 * n * 2, "pTb")
            pbA = bf(pb, 2 * n).rearrange("p (k n) -> p k n", k=2)
            P.dma("pool", pbA, pT[l, :, :, c0:c1], writes=[pb])
            for m in range(8):
                ps = P.ps_alloc(); ps2 = P.ps_alloc()
                for k in range(8):
                    b.mm(ps, WpgA[:, k, 128 * m:128 * m + 128], XNA[:, k, c0:c1], k == 0, k == 7, [Wpg, XNU[c0]], n)
                for k in range(2):
                    b.mm(ps2, WplA[:, k, 128 * m:128 * m + 128], pbA[:, k, :], k == 0, k == 1, [Wpl, pb], n)
                sg = P.alloc(n * 4, "psg")
                b.act(f32(sg, n), ps.ap[:, 0:n], AF.Sigmoid, [ps], [sg])
                b.tt(f32(sg, n), ps2.ap[:, 0:n], f32(sg, n), ALU.mult, [ps2, sg], [sg])
                xa = P_ap_join([XU[m][u] for u in uu])
                b.tt(xa, xa, f32(sg, n), ALU.add, [sg] + [XU[m][u] for u in uu], [XU[m][u] for u in uu])
                P.ps_release(ps); P.ps_release(ps2); P.release(sg)
                if ngen is not None and m in (2, 6):
                    next(ngen, None)
            if ngen is not None:
                for _ in ngen:
                    pass
            P.release(pb)
        for u_ in XNU.values():
            XNblk.inherit = XNblk.inherit + u_.events()
        P.release(Wpg); P.release(Wpl); P.release(XNblk)

    for (c0, c1, uu) in [(0, 512, [0, 1]), (512, 1024, [2, 3]), (1024, 1536, [4, 5]), (1536, 2048, [6, 7]), (2048, 2176, [8])]:
        n = c1 - c0
        yo = [P.alloc(n * 4, f"yo{k}") for k in range(8)]
        rmsnorm_tile([[XU[k][u] for u in uu] for k in range(8)], n, (GF, 0), lambda k: f32(yo[k], n), yo)
        for k in range(8):
            P.dma("sp", yT[:, k, c0:c1], f32(yo[k], n), reads=[yo[k]], is_output=True)
            P.release(yo[k])
    P.finalize()
    return nc, P


_CACHE = {}


def _prep_shared(inp):
    f = np.float32
    sh = {}
    for nm in ("w_in", "w_out", "w_pool", "w_glu", "w_gate_up", "w_down", "w_ple", "w_ple_gate"):
        sh[nm] = np.ascontiguousarray(inp[nm], dtype=f)
    g3 = np.stack([inp["g_mix"], inp["g_ffn"], inp["g_ple"]], axis=1)
    sh["gvec"] = np.ascontiguousarray(g3.reshape(DEPTH, 3, 8, 128).transpose(3, 0, 1, 2).reshape(128, DEPTH * 3 * 8), dtype=f)
    sh["gfin"] = np.ascontiguousarray(np.asarray(inp["g_final"]).reshape(8, 128).T, dtype=f)
    sh["pscale"] = np.ascontiguousarray(np.asarray(inp["pool_scale"]).reshape(DEPTH, 4, 128).transpose(2, 0, 1).reshape(128, DEPTH * 4), dtype=f)
    sh["bglu"] = np.ascontiguousarray(np.asarray(inp["b_glu"]).reshape(DEPTH, 4, 128).transpose(2, 0, 1).reshape(128, DEPTH * 4), dtype=f)
    ic = np.zeros((128, 4, 16), f)
    for g, w in enumerate(POOLW):
        ic[:, g, :] = 1.0 / np.minimum(np.arange(16) + 1, w)
    sh["invcnt"] = ic.reshape(128, 64)
    mk = np.ones((128, 128), f); mk[:, 0::8] = 0.0
    sh["mask8"] = mk
    are = np.asarray(inp["ssm_a_re"], f).reshape(DEPTH, 2048); aim = np.asarray(inp["ssm_a_im"], f).reshape(DEPTH, 2048)
    ldt = np.repeat(np.asarray(inp["ssm_log_dt"], f), 64, axis=1)
    sh["abc"] = np.ascontiguousarray(np.stack([are, aim, ldt], axis=1))
    fm = lambda a: a.reshape(DEPTH, 16, 128).transpose(0, 2, 1)
    cat = lambda a: np.concatenate([fm(a)[l_] for l_ in range(DEPTH)], axis=1)
    sh["afm2"] = np.ascontiguousarray(np.concatenate([cat(are), cat(aim), cat(ldt)], axis=1))
    Bq = np.zeros((DEPTH, 2, 128, 2048), f)
    for ri, nm in enumerate(("ssm_b_re", "ssm_b_im")):
        Bm = np.asarray(inp[nm], f)
        for g in range(32):
            j, gg = g // 8, g % 8
            Bq[:, ri, 16 * gg:16 * gg + 16, 512 * j + 64 * gg:512 * j + 64 * gg + 64] = Bm[:, g].transpose(0, 2, 1)
    sh["Bq"] = Bq
    Cq = np.zeros((DEPTH, 2, 128, 16, 128), f)
    for ri, nm in enumerate(("ssm_c_re", "ssm_c_im")):
        Cm = np.asarray(inp[nm], f)
        for g in range(32):
            blk, gg = g // 2, g % 2
            c0 = 32 * (blk % 4) + 16 * gg
            Cq[:, ri, 64 * gg:64 * gg + 64, blk, c0:c0 + 16] = Cm[:, g].transpose(0, 2, 1)
    sh["Cq"] = Cq.reshape(DEPTH, 2, 128, 2048)
    Dq = np.zeros((DEPTH, 128, 4, 128), f)
    dd = np.asarray(inp["ssm_d"], f).reshape(DEPTH, 4, 128)
    for j in range(4):
        Dq[:, np.arange(128), j, np.arange(128)] = dd[:, j, :]
    sh["Dq"] = Dq.reshape(DEPTH, 128, 512)
    return sh


def _prep_core(inp, c):
    f = np.float32
    m = {}
    xs = np.asarray(inp["x_sample"], f)[NSS * c:NSS * c + NSS].reshape(NSS * LS, D)
    xa = np.concatenate([np.asarray(inp["x_prompt"], f)[c], xs], axis=0)
    m["xT"] = np.ascontiguousarray(xa.T.reshape(8, 128, NTOK).transpose(1, 0, 2))
    ps = np.asarray(inp["p_sample"], f)[:, NSS * c:NSS * c + NSS].reshape(DEPTH, NSS * LS, PLE)
    pa = np.concatenate([np.asarray(inp["p_prompt"], f)[:, c], ps], axis=1)
    m["pT"] = np.ascontiguousarray(pa.transpose(0, 2, 1).reshape(DEPTH, 2, 128, NTOK).transpose(0, 2, 1, 3))
    sp = np.asarray(inp["state_pool"], f)[:, NSS * c:NSS * c + NSS]
    m["spool_tm"] = np.ascontiguousarray(sp)
    m["poolbuf"] = np.ascontiguousarray(sp.reshape(DEPTH, NSS, 15, 4, 128).transpose(0, 4, 3, 1, 2))
    for nm, key in (("h0re", "state_ssm_re"), ("h0im", "state_ssm_im")):
        h = np.asarray(inp[key], f)[:, NSS * c:NSS * c + NSS].reshape(DEPTH, NSS, 16, 128)
        m[nm] = np.ascontiguousarray(h.transpose(0, 3, 2, 1))
    return m


def kernel(**inputs):
    from concourse.bass_utils import run_bass_kernel_spmd
    import os
    dr = int(os.environ.get("KDEPTH", DEPTH))
    if ("prog", dr) not in _CACHE:
        _CACHE[("prog", dr)] = build_program(dr)
    nc, _ = _CACHE[("prog", dr)]
    sh = _prep_shared(inputs)
    in_maps = []
    for c in range(NCORES):
        m = dict(sh); m.update(_prep_core(inputs, c)); in_maps.append(m)
    res = run_bass_kernel_spmd(nc, in_maps, core_ids=list(range(NCORES)))
    R = res.results
    f = np.float32
    y_p = np.zeros((NCORES, SEQ, D), f); y_s = np.zeros((NCORES * NSS, LS, D), f)
    pool_p = np.zeros((DEPTH, NCORES, 15, 512), f); pool_s = np.zeros((DEPTH, NCORES * NSS, 15, 512), f)
    re_p = np.zeros((DEPTH, NCORES, 32, 64), f); im_p = np.zeros_like(re_p)
    re_s = np.zeros((DEPTH, NCORES * NSS, 32, 64), f); im_s = np.zeros_like(re_s)
    for c in range(NCORES):
        r = R[c]
        ya = np.asarray(r["yT"]).transpose(1, 0, 2).reshape(D, NTOK).T
        y_p[c] = ya[:SEQ]; y_s[NSS * c:NSS * c + NSS] = ya[SEQ:].reshape(NSS, LS, D)
        pool_p[:, c] = np.asarray(r["npool_p"]); pool_s[:, NSS * c:NSS * c + NSS] = np.asarray(r["npool_s"])
        re_p[:, c] = np.asarray(r["nre_p"]).transpose(0, 2, 1).reshape(DEPTH, 32, 64)
        im_p[:, c] = np.asarray(r["nim_p"]).transpose(0, 2, 1).reshape(DEPTH, 32, 64)
        re_s[:, NSS * c:NSS * c + NSS] = np.asarray(r["nre_s"]).transpose(0, 3, 2, 1).reshape(DEPTH, NSS, 32, 64)
        im_s[:, NSS * c:NSS * c + NSS] = np.asarray(r["nim_s"]).transpose(0, 3, 2, 1).reshape(DEPTH, NSS, 32, 64)
    return (y_p, y_s, pool_p, re_p, im_p, pool_s, re_s, im_s)
```
